# Optimizing a Trainium2 kernel written in Bass

```python
import jax, jax.numpy as jnp
from jax import lax
import numpy as np

D_MODEL = 1024
BATCH = 8
SEQ = 2048
DEPTH = 1
DEC_BATCH = 128
DEC_SEQ = 1
PAST_LEN = 16384
PAGE_SIZE = 128

D_MIX = D_MODEL
D_A = D_MIX // 2
D_B = D_MIX - D_A
N_HEADS_A = 8
HEAD_DIM_A = D_A // N_HEADS_A
N_GROUPS_B = 8
CHUNK = 128
CONV_W = 3
D_FF = 4 * D_MODEL
N_MOD = 6
D_IN = 2 * D_A + 3 * D_B
EPS = 1e-6

kernel_name = 'hymba_gmlp_shortconv_step'


def rmsnorm(x, g):
    xf = x.astype(jnp.float32)
    y = xf * lax.rsqrt(jnp.mean(jnp.square(xf), axis=-1, keepdims=True) + EPS)
    return (y * g.astype(jnp.float32)).astype(x.dtype)


def chunk_spatial_mix(v, w_s, b_s):
    bsz, t, h, dh = v.shape
    n_chunks = -(-t // CHUNK)
    pad = n_chunks * CHUNK - t
    vp = jnp.pad(v, ((0, 0), (0, pad), (0, 0), (0, 0)))
    vc = vp.reshape(bsz, n_chunks, CHUNK, h, dh)
    wm = jnp.tril(w_s)
    out = jnp.einsum('hts,bnshd->bnthd', wm, vc) + jnp.transpose(b_s)[None, None, :, :, None]
    return out.reshape(bsz, n_chunks * CHUNK, h, dh)[:, :t]


def short_conv(z, prev, w_conv):
    t = z.shape[1]
    zp = jnp.concatenate([prev, z], axis=1)
    y = w_conv[0] * zp[:, 0:t]
    for k in range(1, CONV_W):
        y = y + w_conv[k] * zp[:, k:k + t]
    return y, zp[:, -(CONV_W - 1):]


def layer(x, c, conv_prev, g_mix, w_ada, b_ada, w_in, g_v, w_s, b_s, w_conv, w_out,
          g_ffn, w_ff1, w_ff2):
    bsz, t, _ = x.shape
    mod = jax.nn.silu(c) @ w_ada + b_ada
    sh1, sc1, gt1, sh2, sc2, gt2 = [m[:, None, :] for m in jnp.split(mod, N_MOD, axis=-1)]

    h = rmsnorm(x, g_mix) * (1.0 + sc1) + sh1
    p = h @ w_in
    u, v, bg, cg, hin = jnp.split(p, [D_A, 2 * D_A, 2 * D_A + D_B, 2 * D_A + 2 * D_B], axis=-1)
    u = jax.nn.gelu(u)
    v = rmsnorm(jax.nn.gelu(v).reshape(bsz, t, N_HEADS_A, HEAD_DIM_A),
                g_v.reshape(N_HEADS_A, HEAD_DIM_A))
    a_out = u * chunk_spatial_mix(v, w_s, b_s).reshape(bsz, t, D_A)
    y_conv, new_conv = short_conv(cg * hin, conv_prev, w_conv)
    b_out = bg * y_conv
    mix = jnp.concatenate([a_out, b_out], axis=-1) @ w_out
    x = x + gt1 * mix

    h2 = rmsnorm(x, g_ffn) * (1.0 + sc2) + sh2
    f = jnp.square(jax.nn.relu(h2 @ w_ff1)) @ w_ff2
    x = x + gt2 * f
    return x, v.reshape(bsz, t, D_A), new_conv


def setup_inputs(seed: int = 0) -> dict:
    key = jax.random.key(seed)
    ks = jax.random.split(key, 20)
    nrm = lambda k, s: jax.random.normal(k, s, jnp.float32)
    return {
        'x_prompt': nrm(ks[0], (BATCH, SEQ, D_MODEL)),
        'x_sample': nrm(ks[1], (DEC_BATCH, DEC_SEQ, D_MODEL)),
        'c_prompt': nrm(ks[2], (BATCH, D_MODEL)),
        'c_sample': nrm(ks[3], (DEC_BATCH, D_MODEL)),
        'state_conv': nrm(ks[4], (DEPTH, DEC_BATCH, CONV_W - 1, D_B)),
        'g_mix': 1.0 + 0.02 * nrm(ks[5], (DEPTH, D_MODEL)),
        'w_ada': nrm(ks[6], (DEPTH, D_MODEL, N_MOD * D_MODEL)) * D_MODEL ** -0.5,
        'b_ada': 0.02 * nrm(ks[7], (DEPTH, N_MOD * D_MODEL)),
        'w_in': nrm(ks[8], (DEPTH, D_MODEL, D_IN)) * D_MODEL ** -0.5,
        'g_v': 1.0 + 0.02 * nrm(ks[9], (DEPTH, D_A)),
        'w_s': nrm(ks[10], (DEPTH, N_HEADS_A, CHUNK, CHUNK)) * CHUNK ** -0.5,
        'b_s': 1.0 + 0.02 * nrm(ks[11], (DEPTH, N_HEADS_A, CHUNK)),
        'w_conv': nrm(ks[12], (DEPTH, CONV_W, D_B)) * CONV_W ** -0.5,
        'w_out': nrm(ks[13], (DEPTH, D_MIX, D_MODEL)) * D_MIX ** -0.5,
        'g_ffn': 1.0 + 0.02 * nrm(ks[14], (DEPTH, D_MODEL)),
        'w_ff1': nrm(ks[15], (DEPTH, D_MODEL, D_FF)) * D_MODEL ** -0.5,
        'w_ff2': nrm(ks[16], (DEPTH, D_FF, D_MODEL)) * D_FF ** -0.5,
        'g_final': 1.0 + 0.02 * nrm(ks[17], (D_MODEL,)),
    }


def reference(x_prompt, x_sample, c_prompt, c_sample, state_conv, g_mix, w_ada, b_ada,
              w_in, g_v, w_s, b_s, w_conv, w_out, g_ffn, w_ff1, w_ff2, g_final):
    xp, xs = x_prompt, x_sample
    conv_p_list, conv_s_list, v_s_list = [], [], []
    for l in range(DEPTH):
        wl = (g_mix[l], w_ada[l], b_ada[l], w_in[l], g_v[l], w_s[l], b_s[l], w_conv[l],
              w_out[l], g_ffn[l], w_ff1[l], w_ff2[l])
        prev_p = jnp.zeros((xp.shape[0], CONV_W - 1, D_B), xp.dtype)
        xp, _, conv_p = layer(xp, c_prompt, prev_p, *wl)
        xs, v_s, conv_s = layer(xs, c_sample, state_conv[l], *wl)
        conv_p_list.append(conv_p)
        conv_s_list.append(conv_s)
        v_s_list.append(v_s)
    y_prompt = rmsnorm(xp, g_final)
    y_sample = rmsnorm(xs, g_final)
    new_conv_prompt = jnp.stack(conv_p_list)
    new_conv_sample = jnp.stack(conv_s_list)
    new_chunk_v_sample = jnp.stack(v_s_list)
    return (y_prompt, y_sample, new_conv_prompt, new_conv_sample, new_chunk_v_sample)
```

```python
import contextlib
import numpy as np
import concourse.bass as bass
import concourse.mybir as mybir
from concourse.bass_utils import run_bass_kernel_spmd

F32 = mybir.dt.float32
BF16 = mybir.dt.bfloat16
U8 = mybir.dt.uint8
AF = mybir.ActivationFunctionType
ALU = mybir.AluOpType
AX = mybir.AxisListType

NCORES = 8
D = 1024
SEQ = 2048
NS_TOK = 16
DA = 512
DFF = 4096
DIN = 2560
EPS = 1e-6
TILE = 256
ESZ = {F32: 4, BF16: 2, U8: 1}


_DBG = [None]


class Buf:
    __slots__ = ("name", "w", "r")

    def __init__(self, name):
        self.name = name
        self.w = None
        self.r = []


class Sched:
    ENG = ("pe", "act", "dve", "pool", "sp")

    def __init__(self):
        self.ops = {e: [] for e in self.ENG}
        self.cnt = {e: 0 for e in self.ENG}
        self.waited = {e: {} for e in self.ENG}
        self.dma_tot = {}
        self.out_sems = set()

    def _waits(self, e, reads, writes):
        need = {}

        def add(h, raw):
            if h is None:
                return
            if h[0] == "E":
                _, pe_, seq = h
                if pe_ == e:
                    if e in ("pe", "sp"):
                        return
                s = "c_" + pe_
                need[s] = max(need.get(s, 0), seq)
            else:
                s = h[1]
                need[s] = max(need.get(s, 0), self.dma_tot[s])

        for b in reads:
            add(b.w, True)
        for b in writes:
            add(b.w, False)
            for h in b.r:
                add(h, False)
        out = []
        wd = self.waited[e]
        for s, v in need.items():
            if wd.get(s, 0) < v:
                wd[s] = v
                out.append((s, v))
        return out

    def _record(self, h, reads, writes):
        for b in writes:
            b.w = h
            b.r = []
        for b in reads:
            b.r.append(h)

    def op(self, e, fn, reads=(), writes=()):
        waits = self._waits(e, reads, writes)
        self.cnt[e] += 1
        h = ("E", e, self.cnt[e])
        self.ops[e].append((waits, fn, ("c_" + e, 1)))
        self._record(h, reads, writes)
        return h

    def dma(self, q, out, in_, reads, writes, sem, is_output=False, noncontig=False, after=()):
        waits = self._waits(q, list(reads) + list(after), writes)
        s = "d_" + sem
        self.dma_tot[s] = self.dma_tot.get(s, 0) + 16
        h = ("D", s, self.dma_tot[s])
        if noncontig:
            fn = lambda eng, o=out, i=in_: eng.dma_start(out=o, in_=i, allow_slow_non_contiguous=True)
        else:
            fn = lambda eng, o=out, i=in_: eng.dma_start(out=o, in_=i)
        self.ops[q].append((waits, fn, (s, 16)))
        self._record(h, reads, writes)
        if is_output:
            self.out_sems.add(s)
        return h


def alias_after(new_bufs, old_bufs):
    hs = []
    for b in old_bufs:
        if b.w is not None:
            hs.append(b.w)
        hs.extend(b.r)
    for nb in new_bufs:
        nb.w = None
        nb.r = list(hs)


class Region:
    def __init__(self, arena, base, size, name):
        self.arena, self.base, self.size, self.off, self.name = arena, base, size, 0, name
        self.bufs = []

    def reset(self):
        old = self.bufs
        self.bufs = []
        self.off = 0
        return old

    def alloc(self, name, shape, dt):
        n = 1
        for d in shape[1:]:
            n *= d
        nbytes = n * ESZ[dt]
        self.off = (self.off + 31) // 32 * 32
        assert self.off + nbytes <= self.size, (self.name, name, self.off, nbytes, self.size)
        o = self.base + self.off
        self.off += nbytes
        v = self.arena[:, o:o + nbytes].bitcast(dt)
        b = Buf(name)
        self.bufs.append(b)
        return v, b


def build_nc():
    nc = bass.Bass("TRN2", target_bir_lowering=False)
    S = Sched()

    def din(name, shape):
        return nc.dram_tensor(name, shape, F32, kind="ExternalInput").ap()

    def dout(name, shape):
        return nc.dram_tensor(name, shape, F32, kind="ExternalOutput").ap()

    x_p = din("x_p", [SEQ, D]); x_s = din("x_s", [NS_TOK, D])
    c_p = din("c_p", [1, D]); c_s = din("c_s", [NS_TOK, D])
    sconv = din("sconv", [NS_TOK, 1024])
    g_mix = din("g_mix", [1, D]); w_ada = din("w_ada", [D, 6 * D]); b_ada = din("b_ada", [1, 6 * D])
    w_in = din("w_in", [D, DIN]); g_v = din("g_v", [1, DA]); w_s = din("w_s", [8, 128, 128])
    b_s = din("b_s", [8, 128]); w_conv = din("w_conv", [3, 512]); w_out = din("w_out", [D, D])
    g_ffn = din("g_ffn", [1, D]); w_ff1 = din("w_ff1", [D, DFF]); w_ff2 = din("w_ff2", [DFF, D])
    g_final = din("g_final", [1, D])
    y_p = dout("y_p", [SEQ, D]); y_s = dout("y_s", [NS_TOK, D])
    ncp = dout("ncp", [2, 512]); ncs = dout("ncs", [NS_TOK, 1024]); nvs = dout("nvs", [NS_TOK, 512])
    x1_scr = nc.dram_tensor("x1_scr", [SEQ + NS_TOK, D], F32).ap()
    mod_scr = nc.dram_tensor("mod_scr", [48, 6 * D], F32).ap()
    mod_bufs = [Buf("modscr%d" % i) for i in range(6)]

    TOTAL = 212700
    arena = nc.alloc_sbuf_tensor("arena", [128, TOTAL], U8).ap()
    R1 = Region(arena, 0, 65536, "R1")
    R2 = Region(arena, 65536, 65664, "R2")
    R3 = Region(arena, 131200, TOTAL - 131200, "R3")

    banks = []
    for b in range(8):
        banks.append((nc.alloc_psum_tensor("ps%d" % b, [128, 512], F32).ap(), Buf("bank%d" % b)))
    bank_i = [0]

    def next_bank():
        b = banks[bank_i[0] % 8]
        bank_i[0] += 1
        return b

    w_in_t, w_in_b = R3.alloc("w_in", [128, 8 * DIN], BF16)
    w_in_v = w_in_t.rearrange("p (k n) -> p k n", k=8)
    w_out_t, w_out_b = R3.alloc("w_out", [128, 8 * D], BF16)
    w_out_v = w_out_t.rearrange("p (k n) -> p k n", k=8)
    win_pb = [Buf("win%d" % i) for i in range(5)]
    wout_pb = [Buf("wout%d" % i) for i in range(2)]
    g1B, g1B_b = R3.alloc("g1B", [128, D], F32)
    sh1B, sh1B_b = R3.alloc("sh1B", [128, D], F32)
    gt1B, gt1B_b = R3.alloc("gt1B", [128, D], F32)
    gvBe, gvBe_b = R3.alloc("gvBe", [128, DA], F32)
    gvBo, gvBo_b = R3.alloc("gvBo", [128, DA], F32)
    wmT, wmT_b = R3.alloc("wmT", [128, 8 * 128], BF16)
    wmT3 = wmT.rearrange("p (h t) -> p h t", h=8)
    wmTs, wmTs_b = R3.alloc("wmTs", [128, 8 * 16], BF16)
    wmTs3 = wmTs.rearrange("p (h t) -> p h t", h=8)
    identb, identb_b = R3.alloc("identb", [128, 128], BF16)
    identf, identf_b = R3.alloc("identf", [128, 128], F32)
    wconvT, wconvT_b = R3.alloc("wconvT", [128, 16], F32)
    wconvT_bs = [wconvT_b]
    wconvT3 = wconvT.rearrange("p (c j) -> p c j", c=4)
    bsK, bsK_b = R3.alloc("bsK", [128, 4 * 128], BF16)
    bsK3 = bsK.rearrange("p (r t) -> p r t", r=4)
    bsKs, bsKs_b = R3.alloc("bsKs", [128, 4 * 16], BF16)
    bsKs3 = bsKs.rearrange("p (r t) -> p r t", r=4)
    indK, indK_b = R3.alloc("indK", [128, 128], BF16)
    mhalf, mhalf_b = R3.alloc("mhalf", [128, 8], F32)
    ones1, ones1_b = R3.alloc("ones1", [128, 128], F32)
    w00, w00_b = R3.alloc("w00", [128, 8], F32)
    zsh, zsh_b = R3.alloc("zsh", [128, 2 * 4 * 16], F32)
    stat, _ = R3.alloc("stat", [128, 64], F32)
    stat_b = [Buf("stat%d" % i) for i in range(64)]
    statc = [0]

    def new_stat(n=1):
        c = statc[0]
        if c + n > 64:
            c = 0
        statc[0] = c + n
        return stat[:, c:c + n], stat_b[c]

    NWS = 4
    wada = []
    for i in range(NWS):
        t, b = R1.alloc("wada%d" % i, [128, 8 * 512], BF16)
        wada.append((t.rearrange("p (k n) -> p k n", k=8), b))
    bada, bada_b = R1.alloc("bada", [128, 6 * D], F32)
    cin, cin_b = R1.alloc("cin", [128, D], F32)
    siluCT, siluCT_b = R1.alloc("siluCT", [128, 8 * 48], BF16)
    siluCT3 = siluCT.rearrange("p (k t) -> p k t", k=8)
    modst = [R1.alloc("modst%d" % i, [128, 512], F32) for i in range(1)]

    wada_src = w_ada.rearrange("(k p) n -> p k n", p=128)
    win_src = w_in.rearrange("(k p) n -> p k n", p=128)
    wout_src = w_out.rearrange("(k p) n -> p k n", p=128)
    WIN_COLS = {"u": (0, 512), "v": (512, 1024), "B": (1024, 1536), "C": (1536, 2048), "h": (2048, 2560)}
    WIN_PB = {"u": win_pb[0], "v": win_pb[1], "B": win_pb[2], "C": win_pb[3], "h": win_pb[4]}

    def ada_dma(q):
        wt, wb = wada[q % NWS]
        S.dma("pool", wt, wada_src[:, :, q * 512:(q + 1) * 512], [], [wb], wb.name)

    def ada_compute(q):
        wt, wb = wada[q % NWS]
        bk, bkb = next_bank()

        def _mm(e, bk=bk, wt=wt, q=q):
            ins = None
            for k in range(8):
                ins = e.matmul(bk[0:48, :], lhsT=siluCT3[:, k, :], rhs=wt[:, k, :], start=(k == 0), stop=(k == 7))
            return ins
        S.op("pe", _mm, reads=[siluCT_b, wb], writes=[bkb])
        mt, mb = modst[q % len(modst)]
        S.op("dve", lambda e, bk=bk, mt=mt, q=q: e.tensor_tensor(out=mt[0:48, :], in0=bk[0:48, :], in1=bada[0:48, q * 512:(q + 1) * 512], op=ALU.add),
             reads=[bkb, bada_b], writes=[mb])
        S.dma("sp", mod_scr[:, q * 512:(q + 1) * 512], mt[0:48, :], [mb], [mod_bufs[q // 2]], "modscr%d" % (q // 2))
        if q < 6:
            bk2, bk2b = next_bank()
            S.op("pe", lambda e, bk2=bk2, mt=mt: e.matmul(bk2[:, :], lhsT=ones1[0:1, 0:128], rhs=mt[0:1, :], start=True, stop=True),
                 reads=[ones1_b, mb], writes=[bk2b])
            cs = slice((q % 2) * 512, (q % 2 + 1) * 512)
            if q < 2:
                S.op("act", lambda e, bk2=bk2, cs=cs: e.activation(out=sh1B[:, cs], in_=bk2[:, :], func=AF.Copy), reads=[bk2b], writes=[sh1B_b])
            elif q < 4:
                S.op("dve", lambda e, bk2=bk2, cs=cs: e.scalar_tensor_tensor(out=g1B[:, cs], in0=bk2[:, :], scalar=1.0, in1=g1B[:, cs],
                                                                            op0=ALU.add, op1=ALU.mult),
                     reads=[bk2b, g1B_b], writes=[g1B_b])
            else:
                S.op("act", lambda e, bk2=bk2, cs=cs: e.activation(out=gt1B[:, cs], in_=bk2[:, :], func=AF.Copy), reads=[bk2b], writes=[gt1B_b])

    def load_mod_consts(kind, gT, gB_, shT, shB_, gtT, gtB_, tmp, tmp_b, gsrc, base, part="all"):
        if kind == "p":
            rows = slice(0, 128)
            def src(i):
                return mod_scr[0:1, (base + i) * D:(base + i + 1) * D].partition_broadcast(128)
            gs = gsrc.partition_broadcast(128)
        elif kind == "p16":
            rows = slice(0, 16)
            def src(i):
                return mod_scr[0:1, (base + i) * D:(base + i + 1) * D].partition_broadcast(16)
            gs = gsrc.partition_broadcast(16)
        else:
            rows = slice(0, 16)
            def src(i):
                return mod_scr[32:48, (base + i) * D:(base + i + 1) * D]
            gs = gsrc.partition_broadcast(16)
        if part in ("sg", "all", "sg_dma"):
            S.dma("sp", shT[rows, :], src(0), [mod_bufs[base + 0]], [shB_], shB_.name)
            S.dma("sp", tmp[rows, :], src(1), [mod_bufs[base + 1]], [tmp_b], tmp_b.name)
            S.dma("sp", gT[rows, :], gs, [], [gB_], gB_.name)
        if part in ("sg", "all", "sg_op"):
            S.op("dve", lambda e: e.scalar_tensor_tensor(out=gT[rows, :], in0=tmp[rows, :], scalar=1.0, in1=gT[rows, :],
                                                         op0=ALU.add, op1=ALU.mult),
                 reads=[tmp_b, gB_], writes=[gB_])
        if part in ("gt", "all"):
            S.dma("sp", gtT[rows, :], src(2), [mod_bufs[base + 2]], [gtB_], gtB_.name)

    xs1 = [R2.alloc("xs1_%d" % i, [128, D], F32) for i in range(2)]
    xs2 = [R2.alloc("xs2_%d" % i, [128, D], F32) for i in range(2)]
    hbs = [R2.alloc("hb%d" % i, [128, D], BF16) for i in range(2)]
    tmpA, tmpA_b = R2.alloc("tmpA", [128, D], F32)
    hT = []
    for i in range(2):
        t, b = R2.alloc("hT%d" % i, [128, 8 * TILE], BF16)
        hT.append((t, b))
    vgs = [R2.alloc("vg%d" % i, [128, DA], F32) for i in range(2)]
    sqvs = [R2.alloc("sqv%d" % i, [128, DA], F32) for i in range(2)]
    vg, vg_b = vgs[0]
    sqv, sqv_b = sqvs[0]
    tmpv, tmpv_b = R2.alloc("tmpv", [128, DA], F32)
    vn = [R2.alloc("vn%d" % i, [128, 2 * DA], BF16) for i in range(2)]
    uT, uT_b = R2.alloc("uT", [128, 4 * TILE], BF16)
    Csb = [R2.alloc("Csb%d" % i, [128, TILE], F32) for i in range(2)]
    zb = [R2.alloc("z%d" % i, [128, 4 * (TILE + 2)], F32) for i in range(1)]
    accb = [R2.alloc("acc%d" % i, [128, TILE], F32) for i in range(2)]
    aT, aT_b = R2.alloc("aT", [128, 4 * TILE], BF16)
    aT_bs = [Buf("aT_sub%d" % j) for j in range(2)]
    bT, bT_b = R2.alloc("bT", [128, 4 * TILE], BF16)
    tmpx = [R2.alloc("tmpx%d" % i, [128, 512], F32) for i in range(2)]
    stg, stg_b = tmpx[0]
    modst.append(tmpx[1])

    ind2 = Csb[0][0][:, 0:128]; ind2_b = Csb[0][1]
    bsrow3 = tmpv.rearrange("p (r t) -> p r t", r=4); bsrow_b = tmpv_b
    tmpf3 = sqv.rearrange("p (r t) -> p r t", r=4); tmpf_b = sqv_b
    bslo3 = hbs[0][0][:, 0:512].rearrange("p (r t) -> p r t", r=4); bslo_b = hbs[0][1]
    wst3 = tmpA.rearrange("p (h s) -> p h s", h=8); wst_b = tmpA_b
    sct = xs2[0][0]; sct_b = xs2[0][1]
    gve4 = gvBe.rearrange("p (a e d) -> p a e d", a=4, e=2)
    gvo4 = gvBo.rearrange("p (a e d) -> p a e d", a=4, e=2)
    zsh4 = zsh.rearrange("p (j c t) -> p j c t", j=2, c=4)
    d2d_b = Buf("d2d")

    def setup_early_a():
        S.op("pool", lambda e: e.memset(identf, 0.0), writes=[identf_b])
        S.op("pool", lambda e: e.affine_select(out=identf, in_=identf, compare_op=ALU.not_equal, fill=1.0,
                                               base=0, pattern=[[-1, 128]], channel_multiplier=1),
             reads=[identf_b], writes=[identf_b])
        S.op("pool", lambda e: e.memset(cin[0:48, :], 0.0), writes=[cin_b])
        S.dma("sp", cin[32:48, :], c_s, [], [cin_b], "cin")
        S.dma("sp", cin[0:1, :], c_p, [], [cin_b], "cin")
        S.dma("sp", bada[0:48, :], b_ada.partition_broadcast(48), [], [bada_b], "bada")
        S.op("pool", lambda e: e.memset(ones1[0:1, :], 1.0), writes=[ones1_b])
        S.dma("sp", g1B, g_mix.partition_broadcast(128), [], [g1B_b], "g1B")

    def setup_early_b():
        S.op("pool", lambda e: e.memset(identb, 0.0), writes=[identb_b])
        S.op("pool", lambda e: e.affine_select(out=identb, in_=identb, compare_op=ALU.not_equal, fill=1.0,
                                               base=0, pattern=[[-1, 128]], channel_multiplier=1),
             reads=[identb_b], writes=[identb_b])
        S.op("pool", lambda e: e.memset(mhalf, -0.5), writes=[mhalf_b])

    def setup_late_loads():
        S.dma("sp", gvBe, g_v.partition_broadcast(128), [], [gvBe_b], "gvBe")
        S.dma("sp", gvBo, g_v.partition_broadcast(128), [], [gvBo_b], "gvBo")
        S.dma("sp", bsrow3[0:2, :, :], b_s.rearrange("(r e) t -> e r t", e=2), [], [bsrow_b], "bsrow")
        S.dma("sp", wst3, w_s.rearrange("h t s -> t h s"), [], [wst_b], "wst")


    def setup_late_prep():
        S.op("pool", lambda e: e.memset(gve4[:, :, 1, :], 0.0), reads=[], writes=[gvBe_b])
        S.op("pool", lambda e: e.memset(gvo4[:, :, 0, :], 0.0), reads=[], writes=[gvBo_b])
        S.op("pool", lambda e: e.affine_select(out=wst3, in_=wst3, compare_op=ALU.is_ge, fill=0.0, base=0,
                                               pattern=[[0, 8], [-1, 128]], channel_multiplier=1),
             reads=[wst_b], writes=[wst_b])
        S.op("pool", lambda e: e.memset(ind2[0:2, :], 1.0), writes=[ind2_b])
        S.op("pool", lambda e: e.affine_select(out=ind2[0:2, :], in_=ind2[0:2, :], compare_op=ALU.is_ge, fill=0.0,
                                               base=0, pattern=[[1, 128]], channel_multiplier=-64),
             reads=[ind2_b], writes=[ind2_b])
        S.op("pool", lambda e: e.affine_select(out=ind2[0:2, :], in_=ind2[0:2, :], compare_op=ALU.is_ge, fill=0.0,
                                               base=63, pattern=[[-1, 128]], channel_multiplier=64),
             reads=[ind2_b], writes=[ind2_b])
        S.op("pool", lambda e: e.memset(indK[0:34, :], 0.0), writes=[indK_b])
        S.op("dve", lambda e: e.tensor_copy(out=indK[0:2, :], in_=ind2[0:2, :]), reads=[ind2_b], writes=[indK_b])
        S.dma("sp", indK[32:34, :], indK[0:2, :], [indK_b], [indK_b], "indK")
        S.op("pool", lambda e: e.memset(bsK[0:34, :], 0.0), writes=[bsK_b])
        S.op("dve", lambda e: e.tensor_copy(out=bsK3[0:2, :, :], in_=bsrow3[0:2, :, :]), reads=[bsrow_b], writes=[bsK_b])
        S.op("dve", lambda e: e.tensor_copy(out=tmpf3[0:2, :, :], in_=bsK3[0:2, :, :]), reads=[bsK_b], writes=[tmpf_b])
        S.op("dve", lambda e: e.tensor_tensor(out=tmpf3[0:2, :, :], in0=bsrow3[0:2, :, :], in1=tmpf3[0:2, :, :], op=ALU.subtract),
             reads=[bsrow_b, tmpf_b], writes=[tmpf_b])
        S.op("dve", lambda e: e.tensor_copy(out=bslo3[0:2, :, :], in_=tmpf3[0:2, :, :]), reads=[tmpf_b], writes=[bslo_b])
        S.dma("sp", bsK3[32:34, :, :], bslo3[0:2, :, :], [bslo_b], [bsK_b], "bsK")

    def setup_late():
        for half in range(2):
            bk, bkb = next_bank()

            def _tr_w(e, bk=bk, half=half):
                ins = None
                for hh in range(4):
                    ins = e.transpose(bk[:, hh * 128:(hh + 1) * 128], wst3[:, half * 4 + hh, :], identf)
                return ins
            S.op("pe", _tr_w, reads=[wst_b, identf_b], writes=[bkb])
            S.op("dve", lambda e, bk=bk, half=half: e.tensor_copy(out=wmT[:, half * 512:(half + 1) * 512], in_=bk[:, 0:512]),
                 reads=[bkb], writes=[wmT_b])

    def setup_sample_hist():
        S.dma("sp", cin[0:16, :], sconv, [], [cin_b], "cin")
        bk, bkb = next_bank()

        def _tr_s(e, bk=bk):
            ins = None
            for jc in range(8):
                ins = e.transpose(bk[:, jc * 16:(jc + 1) * 16], cin[0:16, jc * 128:(jc + 1) * 128], identf[0:16, 0:16])
            return ins
        S.op("pe", _tr_s, reads=[cin_b, identf_b], writes=[bkb])
        S.op("dve", lambda e, bk=bk: e.tensor_copy(out=zsh, in_=bk[:, 0:128]), reads=[bkb], writes=[zsh_b])

    def setup_sample():
        S.dma("sp", w00[0:16, :], w_s[:, 0, 0:1].rearrange("h o -> o h").partition_broadcast(16), [], [w00_b], "w00", noncontig=True)
        S.dma("sp", ncs[:, 0:512], sconv[:, 512:1024], [], [d2d_b], "d2d", is_output=True)
        for h in range(8):
            S.op("dve", lambda e, h=h: e.tensor_scalar(out=wmTs3[0:16, h, :], in0=identf[0:16, 0:16], scalar1=w00[0:16, h:h + 1],
                                                       scalar2=0.0, op0=ALU.mult, op1=ALU.add),
                 reads=[identf_b, w00_b], writes=[wmTs_b])
        S.op("dve", lambda e: e.tensor_copy(out=bsKs3[0:34, :, :], in_=bsK3[0:34, :, 0:1].broadcast_to([34, 4, 16])),
             reads=[bsK_b], writes=[bsKs_b])

    def setup_conv():
        wcv = tmpv
        S.op("pool", lambda e: e.memset(wcv[0:4, :], 0.0), reads=[], writes=[tmpv_b])
        S.dma("sp", wcv[0:3, :], w_conv, [], [tmpv_b], "wcv")
        bk, bkb = next_bank()

        def _tr_cv(e, bk=bk):
            ins = None
            for c in range(4):
                ins = e.transpose(bk[:, c * 4:(c + 1) * 4], wcv[0:4, c * 128:(c + 1) * 128], identf[0:4, 0:4])
            return ins
        S.op("pe", _tr_cv, reads=[tmpv_b, identf_b], writes=[bkb])
        S.op("dve", lambda e, bk=bk: e.tensor_copy(out=wconvT, in_=bk[:, 0:16]), reads=[bkb], writes=[wconvT_b])

    ctr = {"xs": 0, "c": 0, "tx": 0, "hb": 0}

    def norm_stats(src, src_b, P, junk, junk_b):
        ss, ss_b = new_stat(1)
        ms, ms_b = new_stat(1)
        rs, rs_b = new_stat(1)
        S.op("act", lambda e: e.activation(out=junk[0:P, :], in_=src[0:P, :], func=AF.Square, accum_out=ss[0:P, :]),
             reads=[src_b], writes=[junk_b, ss_b])
        S.op("dve", lambda e: e.tensor_scalar(out=ms[0:P, :], in0=ss[0:P, :], scalar1=1.0 / D, scalar2=EPS,
                                              op0=ALU.mult, op1=ALU.add), reads=[ss_b], writes=[ms_b])
        S.op("pool", lambda e: e.tensor_tensor(out=rs[0:P, :], in0=ms[0:P, :], in1=mhalf[0:P, 0:1], op=ALU.pow),
             reads=[ms_b, mhalf_b], writes=[rs_b])
        return rs, rs_b

    def modulate_stats(src, src_b, P, hbt, hbb, nstats):
        return nstats(src, src_b, P, hbt, hbb)

    def modulate_apply(src, src_b, P, rs, rs_b, gT, gB_, shT, shB_, hbt, hbb, tmpT, tmpT_b):
        S.op("dve", lambda e: e.scalar_tensor_tensor(out=tmpT[0:P, :], in0=src[0:P, :], scalar=rs[0:P, 0:1], in1=gT[0:P, :],
                                                     op0=ALU.mult, op1=ALU.mult),
             reads=[src_b, rs_b, gB_], writes=[tmpT_b])
        S.op("dve", lambda e: e.tensor_tensor(out=hbt[0:P, :], in0=tmpT[0:P, :], in1=shT[0:P, :], op=ALU.add),
             reads=[tmpT_b, shB_], writes=[hbb])

    def modulate(src, src_b, P, gT, gB_, shT, shB_, hbt, hbb, tmpT, tmpT_b, nstats):
        rs, rs_b = modulate_stats(src, src_b, P, hbt, hbb, nstats)
        modulate_apply(src, src_b, P, rs, rs_b, gT, gB_, shT, shB_, hbt, hbb, tmpT, tmpT_b)

    def transpose_to(hbt, hbb, P, dstT, dstT_b, j, T, ident, ident_b):
        bk, bkb = next_bank()
        psb = bk.bitcast(BF16)

        def _tr(e):
            ins = None
            for k in range(8):
                ins = e.transpose(psb[:, k * P:(k + 1) * P], hbt[0:P, k * 128:(k + 1) * 128], ident[0:P, 0:P])
            return ins
        S.op("pe", _tr, reads=[hbb, ident_b], writes=[bkb])
        d3 = dstT[:, 0:8 * T].rearrange("p (k t) -> p k t", k=8)[:, :, j * P:(j + 1) * P]
        S.op("act", lambda e: e.activation(out=d3, in_=psb[:, 0:8 * P].rearrange("p (k t) -> p k t", k=8), func=AF.Copy),
             reads=[bkb], writes=[dstT_b])

    def tview(t, T, nk):
        return t[:, 0:nk * T].rearrange("p (k t) -> p k t", k=nk)

    def A_load(tl):
        P, NSUB = tl["P"], tl["NS"]
        tl["xs1"] = []
        for j in range(NSUB):
            xt, xb = xs1[j % 2]
            tl["xs1"].append((xt, xb))
            srcrows = x_p[tl["row0"] + j * P: tl["row0"] + (j + 1) * P, :] if tl["kind"] == "p" else x_s
            S.dma("sp", xt[0:P, :], srcrows, [], [xb], xb.name)

    def A_s1_stats(tl, j):
        P = tl["P"]
        xt, xb = tl["xs1"][j]
        hbt, hbb = hbs[j % 2]
        tl.setdefault("hb", {})[j] = (hbt, hbb)
        tl.setdefault("rs", {})[j] = modulate_stats(xt, xb, P, hbt, hbb, norm_stats)

    def A_s1_mod(tl, j):
        P = tl["P"]
        xt, xb = tl["xs1"][j]
        hbt, hbb = tl["hb"][j]
        rs, rs_b = tl["rs"][j]
        modulate_apply(xt, xb, P, rs, rs_b, g1B, g1B_b, sh1B, sh1B_b, hbt, hbb, tmpA, tmpA_b)

    def A_s1a(tl, j):
        A_s1_stats(tl, j)
        A_s1_mod(tl, j)

    def A_s1b(tl, j):
        P, T = tl["P"], tl["T"]
        hTt, hTb = hT[tl["i"] % 2]
        tl["hT"] = (hTt, hTb)
        hbt, hbb = tl["hb"][j]
        transpose_to(hbt, hbb, P, hTt, hTb, j, T, identb, identb_b)

    def A_xreload(tl):
        P, NSUB = tl["P"], tl["NS"]
        tl["xslots"] = []
        for j in range(NSUB):
            xt, xb = xs2[j % 2]
            tl["xslots"].append((xt, xb))
            srcrows = x_p[tl["row0"] + j * P: tl["row0"] + (j + 1) * P, :] if tl["kind"] == "p" else x_s
            S.dma("sp", xt[0:P, :], srcrows, [], [xb], xb.name)

    def A_v_mm(tl, j):
        P, T = tl["P"], tl["T"]
        hTt, hTb = tl["hT"]
        h3 = tview(hTt, T, 8)
        lo, hi = WIN_COLS["v"]
        bk, bkb = next_bank()
        vgt, vgb = vgs[j % 2]
        sqt, sqb = sqvs[j % 2]

        def _mm(e, bk=bk, j=j):
            ins = None
            for k in range(8):
                ins = e.matmul(bk[0:P, :], lhsT=h3[:, k, j * P:(j + 1) * P], rhs=w_in_v[:, k, lo:hi], start=(k == 0), stop=(k == 7))
            return ins
        S.op("pe", _mm, reads=[hTb, WIN_PB["v"]], writes=[bkb])
        S.op("act", lambda e, bk=bk: e.activation(out=vgt[0:P, :], in_=bk[0:P, :], func=AF.Gelu_apprx_tanh),
             reads=[bkb], writes=[vgb])
        S.op("act", lambda e: e.activation(out=sqt[0:P, :], in_=vgt[0:P, :], func=AF.Square),
             reads=[vgb], writes=[sqb])

    def A_v_chain1(tl, j):
        P = tl["P"]
        sqt, sqb = sqvs[j % 2]
        ss, ss_b = new_stat(8)
        ms, ms_b = new_stat(8)
        rs, rs_b = new_stat(8)
        tl.setdefault("vrs", {})[j] = (rs, rs_b)
        S.op("dve", lambda e: e.tensor_reduce(out=ss[0:P, :], in_=sqt[0:P, :].rearrange("p (h d) -> p h d", h=8),
                                              axis=AX.X, op=ALU.add), reads=[sqb], writes=[ss_b])
        S.op("dve", lambda e: e.tensor_scalar(out=ms[0:P, :], in0=ss[0:P, :], scalar1=1.0 / 64, scalar2=EPS,
                                              op0=ALU.mult, op1=ALU.add), reads=[ss_b], writes=[ms_b])
        S.op("pool", lambda e: e.tensor_tensor(out=rs[0:P, :], in0=ms[0:P, :], in1=mhalf[0:P, :], op=ALU.pow),
             reads=[ms_b, mhalf_b], writes=[rs_b])

    def A_v_chain2(tl, j):
        P = tl["P"]
        vgt, vgb = vgs[j % 2]
        sqt, sqb = sqvs[j % 2]
        rs, rs_b = tl["vrs"][j]
        S.op("dve", lambda e: e.tensor_tensor(out=tmpv[0:P, :].rearrange("p (h d) -> p h d", h=8),
                                              in0=vgt[0:P, :].rearrange("p (h d) -> p h d", h=8),
                                              in1=rs[0:P, :].unsqueeze(2).broadcast_to([P, 8, 64]), op=ALU.mult),
             reads=[vgb, rs_b], writes=[tmpv_b])
        vt, vb = vn[j % 2]
        S.op("dve", lambda e: e.tensor_tensor(out=vt[0:P, 0:DA], in0=tmpv[0:P, :], in1=gvBe[0:P, :], op=ALU.mult),
             reads=[tmpv_b, gvBe_b], writes=[vb])
        S.op("pool", lambda e: e.tensor_tensor(out=vt[0:P, DA:2 * DA], in0=tmpv[0:P, :], in1=gvBo[0:P, :], op=ALU.mult),
             reads=[tmpv_b, gvBo_b], writes=[vb])
        if tl["kind"] == "s":
            S.op("dve", lambda e: e.tensor_tensor(out=sqt[0:P, :], in0=tmpv[0:P, :], in1=gvBe[0:P, :], op=ALU.mult),
                 reads=[tmpv_b, gvBe_b], writes=[sqb])
            S.op("dve", lambda e: e.tensor_tensor(out=vgt[0:P, :], in0=tmpv[0:P, :], in1=gvBo[0:P, :], op=ALU.mult),
                 reads=[tmpv_b, gvBo_b], writes=[vgb])
            S.op("dve", lambda e: e.tensor_tensor(out=sqt[0:P, :], in0=sqt[0:P, :], in1=vgt[0:P, :], op=ALU.add),
                 reads=[sqb, vgb], writes=[sqb])
            S.dma("sp", nvs, sqt[0:P, :], [sqb], [], sqb.name, is_output=True)

    def A_u_c(tl, c):
        T = tl["T"]
        hTt, hTb = tl["hT"]
        h3 = tview(hTt, T, 8)
        u3 = tview(uT, T, 4)
        lo, _ = WIN_COLS["u"]
        bk, bkb = next_bank()

        def _mm(e, bk=bk, c=c):
            ins = None
            for k in range(8):
                ins = e.matmul(bk[:, 0:T], lhsT=w_in_v[:, k, lo + c * 128: lo + (c + 1) * 128], rhs=h3[:, k, :], start=(k == 0), stop=(k == 7))
            return ins
        S.op("pe", _mm, reads=[hTb, WIN_PB["u"]], writes=[bkb])
        S.op("act", lambda e, bk=bk, c=c: e.activation(out=u3[:, c, :], in_=bk[:, 0:T], func=AF.Gelu_apprx_tanh),
             reads=[bkb], writes=[uT_b])

    def zview(zt, tl):
        nseg, L = tl["nseg"], tl["L"]
        return zt[:, 0:4 * nseg * (L + 2)].rearrange("p (c s l) -> p c s l", c=4, s=nseg)

    def A_hist(tl):
        nseg, L = tl["nseg"], tl["L"]
        zt, zbuf = zb[0]
        z4 = zview(zt, tl)
        tl["z"] = (zt, zbuf, z4)
        if tl["kind"] == "s":
            for j in range(2):
                S.op("act", lambda e, j=j: e.activation(out=z4[:, :, :, j], in_=zsh4[:, j, :, :], func=AF.Copy),
                     reads=[zsh_b, zbuf], writes=[zbuf])
        elif tl["first"]:
            S.op("pool", lambda e: e.memset(z4[:, :, :, 0:2], 0.0), reads=[], writes=[zbuf])
        else:
            S.op("act", lambda e: e.activation(out=z4[:, :, :, 0:2], in_=z4[:, :, :, L:L + 2], func=AF.Copy),
                 reads=[zbuf], writes=[zbuf])

    def A_bch_c(tl, c):
        T, nseg, L = tl["T"], tl["nseg"], tl["L"]
        hTt, hTb = tl["hT"]
        h3 = tview(hTt, T, 8)
        zt, zbuf, z4 = tl["z"]
        b3 = tview(bT, T, 4)
        ct, cb = Csb[ctr["c"] % 2]
        at, ab = accb[ctr["c"] % 2]
        ctr["c"] += 1

        def job(nm):
            lo, _ = WIN_COLS[nm]
            bk, bkb = next_bank()

            def _mm(e, bk=bk):
                ins = None
                for k in range(8):
                    ins = e.matmul(bk[:, 0:T], lhsT=w_in_v[:, k, lo + c * 128: lo + (c + 1) * 128], rhs=h3[:, k, :], start=(k == 0), stop=(k == 7))
                return ins
            S.op("pe", _mm, reads=[hTb, WIN_PB[nm]], writes=[bkb])
            return bk, bkb
        bkC, bkCb = job("C")
        S.op("act", lambda e: e.activation(out=ct[:, 0:T], in_=bkC[:, 0:T], func=AF.Copy),
             reads=[bkCb], writes=[cb])
        bkH, bkHb = job("h")
        S.op("dve", lambda e: e.tensor_tensor(
            out=z4[:, c, :, 2:2 + L], in0=bkH[:, 0:T].rearrange("p (s l) -> p s l", s=nseg),
            in1=ct[:, 0:T].rearrange("p (s l) -> p s l", s=nseg), op=ALU.mult),
            reads=[bkHb, cb], writes=[zbuf])
        a3 = at[:, 0:T].rearrange("p (s l) -> p s l", s=nseg)
        S.op("act", lambda e: e.activation(out=a3, in_=z4[:, c, :, 0:L], func=AF.Copy, scale=wconvT3[:, c, 0:1]),
             reads=[zbuf] + wconvT_bs, writes=[ab])
        for jj in (1, 2):
            S.op("dve", lambda e, jj=jj: e.scalar_tensor_tensor(out=a3, in0=z4[:, c, :, jj:jj + L],
                                                                scalar=wconvT3[:, c, jj:jj + 1], in1=a3,
                                                                op0=ALU.mult, op1=ALU.add),
                 reads=[zbuf, ab] + wconvT_bs, writes=[ab])
        bkB, bkBb = job("B")
        S.op("dve", lambda e: e.tensor_tensor(out=b3[:, c, :], in0=bkB[:, 0:T], in1=at[:, 0:T], op=ALU.mult),
             reads=[bkBb, ab], writes=[bT_b])

    def A_spatial(tl, j):
        P, T = tl["P"], tl["T"]
        u3 = tview(uT, T, 4)
        a3 = tview(aT, T, 4)
        wm = wmT3 if tl["kind"] == "p" else wmTs3
        wm_b = wmT_b if tl["kind"] == "p" else wmTs_b
        bs = bsK3 if tl["kind"] == "p" else bsKs3
        bs_b = bsK_b if tl["kind"] == "p" else bsKs_b
        vt, vb = vn[j % 2]
        bk, bkb = next_bank()

        def _mm(e, bk=bk, vt=vt):
            ins = None
            for pr in range(4):
                o = bk[:, pr * P:(pr + 1) * P]
                e.matmul(o, lhsT=indK[0:34, :], rhs=bs[0:34, pr, 0:P], start=True, stop=False)
                e.matmul(o, lhsT=vt[0:P, pr * 128:(pr + 1) * 128], rhs=wm[0:P, 2 * pr, 0:P], start=False, stop=False)
                ins = e.matmul(o, lhsT=vt[0:P, DA + pr * 128: DA + (pr + 1) * 128], rhs=wm[0:P, 2 * pr + 1, 0:P], start=False, stop=True)
            return ins
        S.op("pe", _mm, reads=[vb, wm_b, bs_b, indK_b], writes=[bkb])
        S.op("dve", lambda e, bk=bk, j=j: e.tensor_tensor(out=a3[:, :, j * P:(j + 1) * P],
                                                          in0=bk[:, 0:4 * P].rearrange("p (r t) -> p r t", r=4),
                                                          in1=u3[:, :, j * P:(j + 1) * P], op=ALU.mult),
             reads=[bkb, uT_b], writes=[aT_bs[j]])

    def A_wout(tl, j, n):
        P, T = tl["P"], tl["T"]
        a3 = tview(aT, T, 4)
        b3 = tview(bT, T, 4)
        xt, xb = tl["xslots"][j]
        bk, bkb = next_bank()

        def _mm(e, bk=bk, n=n):
            ins = None
            for k in range(8):
                lh = a3[:, k, j * P:(j + 1) * P] if k < 4 else b3[:, k - 4, j * P:(j + 1) * P]
                ins = e.matmul(bk[0:P, :], lhsT=lh, rhs=w_out_v[:, k, n * 512:(n + 1) * 512], start=(k == 0), stop=(k == 7))
            return ins
        S.op("pe", _mm, reads=[aT_bs[j], bT_b, wout_pb[n]], writes=[bkb])
        tt, tb = tmpx[ctr["tx"] % 2]
        ctr["tx"] += 1
        S.op("dve", lambda e, bk=bk, tt=tt, n=n: e.tensor_tensor(out=tt[0:P, :], in0=bk[0:P, :], in1=gt1B[0:P, n * 512:(n + 1) * 512], op=ALU.mult),
             reads=[bkb, gt1B_b], writes=[tb])
        S.op("pool", lambda e, tt=tt, n=n: e.tensor_tensor(out=xt[0:P, n * 512:(n + 1) * 512], in0=tt[0:P, :],
                                                           in1=xt[0:P, n * 512:(n + 1) * 512], op=ALU.add),
             reads=[tb, xb], writes=[xb])
        if n == 1:
            r0 = tl["srow0"] + j * P
            S.dma("sp", x1_scr[r0:r0 + P, :], xt[0:P, :], [xb], [tl["x1b"][j]], xb.name)

    def A_convout(tl):
        T, nseg, L = tl["T"], tl["nseg"], tl["L"]
        zt, zbuf, z4 = tl["z"]
        bk, bkb = next_bank()
        if tl["kind"] == "p":
            def _tr(e, bk=bk):
                ins = None
                for c in range(4):
                    ins = e.transpose(bk[0:2, c * 128:(c + 1) * 128], z4[:, c, 0, L:L + 2], identf)
                return ins
            S.op("pe", _tr, reads=[zbuf, identf_b], writes=[bkb])
            S.op("act", lambda e, bk=bk: e.activation(out=stg[0:2, :], in_=bk[0:2, :], func=AF.Copy), reads=[bkb], writes=[stg_b])
            S.dma("sp", ncp, stg[0:2, :], [stg_b], [], stg_b.name, is_output=True)
        else:
            def _tr(e, bk=bk):
                ins = None
                for c in range(4):
                    ins = e.transpose(bk[0:16, c * 128:(c + 1) * 128], z4[:, c, :, 2], identf)
                return ins
            S.op("pe", _tr, reads=[zbuf, identf_b], writes=[bkb])
            S.op("act", lambda e, bk=bk: e.activation(out=stg[0:16, :], in_=bk[0:16, :], func=AF.Copy), reads=[bkb], writes=[stg_b])
            S.dma("sp", ncs[:, 512:1024], stg[0:16, :], [stg_b], [], stg_b.name, is_output=True)

    def A_tile(tl, nxt, extra, pre_nxt=None, nxt_loaded=False, post_bch=None):
        NSUB = tl["NS"]
        NN = nxt["NS"] if nxt is not None else 0

        def ex():
            if extra:
                extra.pop(0)()
        if pre_nxt is not None:
            pre_nxt()
        if nxt is not None and not nxt_loaded:
            A_load(nxt)
        for j in range(NSUB):
            A_v_mm(tl, j)
        A_hist(tl)
        A_bch_c(tl, 0)
        for j in range(NSUB):
            A_v_chain1(tl, j)
        for j in range(NN):
            A_s1_stats(nxt, j)
        ex()
        A_bch_c(tl, 1)
        A_v_chain2(tl, 0)
        A_u_c(tl, 0)
        A_u_c(tl, 1)
        A_xreload(tl)
        ex()
        A_bch_c(tl, 2)
        if NSUB > 1:
            A_v_chain2(tl, 1)
        A_u_c(tl, 2)
        A_u_c(tl, 3)
        ex()
        A_bch_c(tl, 3)
        if post_bch is not None:
            post_bch()
        for j in range(NSUB):
            A_spatial(tl, j)
            if j < NN:
                A_s1_mod(nxt, j)
            A_wout(tl, j, 0)
            if j == NSUB - 1:
                for jj in range(NSUB, NN):
                    A_s1_mod(nxt, jj)
                for jj in range(NN):
                    A_s1b(nxt, jj)
            A_wout(tl, j, 1)
        if tl["kind"] == "s" or tl["last"]:
            A_convout(tl)

    ntile = SEQ // TILE
    tilesA = []
    for i in range(ntile):
        tilesA.append(dict(kind="p", i=i, P=128, NS=TILE // 128, T=TILE, row0=i * TILE, srow0=i * TILE,
                           nseg=1, L=TILE, first=(i == 0), last=(i == ntile - 1),
                           x1b=[Buf("x1s_%d_%d" % (i, j)) for j in range(TILE // 128)]))
    tS = dict(kind="s", i=ntile, P=16, NS=1, T=16, row0=0, srow0=SEQ, nseg=16, L=1, first=False, last=False,
              x1b=[Buf("x1s_s")])

    ada_seq = {"c": 4, "d": 8}

    def pop_ada(n=1):
        for _ in range(n):
            if ada_seq["c"] < 12:
                ada_compute(ada_seq["c"])
                ada_seq["c"] += 1
                if ada_seq["d"] < 12:
                    ada_dma(ada_seq["d"])
                    ada_seq["d"] += 1

    wff1_pb = []
    wff1_state = {}

    def emit_wff1():
        old1 = R1.reset()
        wff1_t, _ = R1.alloc("wff1", [128, 8 * DFF], BF16)
        wff1_state["v"] = wff1_t.rearrange("p (k n) -> p k n", k=8)
        NP1 = 8
        for i in range(NP1):
            wff1_pb.append(Buf("wff1_%d" % i))
        alias_after(wff1_pb, old1)
        wff1_src = w_ff1.rearrange("(k p) n -> p k n", p=128)
        cs = DFF // NP1
        for i in range(NP1):
            S.dma("pool", wff1_state["v"][:, :, i * cs:(i + 1) * cs], wff1_src[:, :, i * cs:(i + 1) * cs], [], [wff1_pb[i]], wff1_pb[i].name)

    setup_early_a()
    for q in range(4):
        ada_dma(q)
    for nm in ("v", "C", "h", "B", "u"):
        lo, hi = WIN_COLS[nm]
        S.dma("pool", w_in_v[:, :, lo:hi], win_src[:, :, lo:hi], [], [WIN_PB[nm]], WIN_PB[nm].name)
    setup_early_b()
    A_load(tilesA[0])
    S.op("act", lambda e: e.activation(out=cin[0:48, :], in_=cin[0:48, :], func=AF.Silu), reads=[cin_b], writes=[cin_b])
    bk, bkb = next_bank()

    def _tr_c(e, bk=bk):
        ins = None
        for k in range(8):
            ins = e.transpose(bk[:, k * 48:(k + 1) * 48], cin[0:48, k * 128:(k + 1) * 128], identf[0:48, 0:48])
        return ins
    S.op("pe", _tr_c, reads=[cin_b, identf_b], writes=[bkb])
    S.op("dve", lambda e, bk=bk: e.tensor_copy(out=siluCT, in_=bk[:, 0:384]), reads=[bkb], writes=[siluCT_b])
    for j in range(tilesA[0]["NS"]):
        A_s1_stats(tilesA[0], j)
    ada_compute(0)
    ada_dma(4)
    ada_compute(1)
    ada_dma(5)
    ada_compute(2)
    setup_sample_hist()
    ada_compute(3)
    for n in range(2):
        S.dma("pool", w_out_v[:, :, n * 512:(n + 1) * 512], wout_src[:, :, n * 512:(n + 1) * 512], [], [wout_pb[n]], wout_pb[n].name)
    ada_dma(6)
    ada_dma(7)
    for j in range(tilesA[0]["NS"]):
        A_s1_mod(tilesA[0], j)
    for j in range(tilesA[0]["NS"]):
        A_s1b(tilesA[0], j)
    setup_conv()
    setup_late_loads()
    setup_late_prep()
    for i, tl in enumerate(tilesA):
        nxt = tilesA[i + 1] if i + 1 < ntile else None
        extra = []
        if i == 0:
            extra = [lambda: setup_late(), lambda: None, lambda: pop_ada(2)]
        elif i in (1, 2):
            extra = [lambda: pop_ada(1), lambda: pop_ada(1), lambda: pop_ada(1)]
        if nxt is None:
            A_tile(tl, tS, extra,
                   pre_nxt=lambda: load_mod_consts("s", g1B, g1B_b, sh1B, sh1B_b, gt1B, gt1B_b, tmpA, tmpA_b, g_mix, 0, part="sg"))
        else:
            A_tile(tl, nxt, extra)
        if i == 2:
            assert ada_seq["c"] == 12
            emit_wff1()
        if i == 3:
            setup_sample()

    wff1_v = wff1_state["v"]
    NP1 = 8
    old3 = R3.reset()
    g2B, g2B_b = R3.alloc("g2B", [128, D], F32)
    sh2B, sh2B_b = R3.alloc("sh2B", [128, D], F32)
    tmpB, tmpB_b = R3.alloc("tmpB", [128, D], F32)
    NX1 = 5
    x1s = [R3.alloc("x1_%d" % i, [128, D], F32) for i in range(2)]
    hb2s = [R3.alloc("hb2_%d" % i, [128, D], BF16) for i in range(2)]
    h2T = [R3.alloc("h2T0", [128, 8 * TILE], BF16)]
    identb2, identb2_b = R3.alloc("identb2", [128, 128], BF16)
    mhalf2, mhalf2_b = R3.alloc("mhalf2", [128, 8], F32)
    stat2, _ = R3.alloc("stat2", [128, 64], F32)
    stat2_b = [Buf("stat2_%d" % i) for i in range(64)]
    assert R3.off <= 8 * DIN * 2, R3.off
    early_b = [g2B_b, sh2B_b, tmpB_b, identb2_b, mhalf2_b, h2T[0][1]] + [b for _, b in x1s] + [b for _, b in hb2s] + stat2_b
    x1s += [R3.alloc("x1_%d" % i, [128, D], F32) for i in range(2, NX1)]
    gt2B, gt2B_b = R3.alloc("gt2B", [128, D], F32)
    gfB, gfB_b = R3.alloc("gfB", [128, D], F32)
    h2T.append(R3.alloc("h2T1", [128, 8 * TILE], BF16))
    rr = [R3.alloc("r%d" % i, [128, TILE], F32) for i in range(2)]
    f1T, f1T_b = R3.alloc("f1T", [128, 32 * TILE], BF16)
    f1Ts, f1Ts_b = R3.alloc("f1Ts", [128, 32 * 16], BF16)
    h2Ts, h2Ts_b = R3.alloc("h2Ts", [128, 8 * 16], BF16)
    tmpy = [R3.alloc("tmpy%d" % i, [128, 512], F32) for i in range(2)]
    late_b = [gt2B_b, gfB_b, f1T_b, f1Ts_b, h2Ts_b, h2T[1][1]] + [b for _, b in x1s[2:]] + [b for _, b in rr] + [b for _, b in tmpy]

    def passB_early():
        alias_after(early_b, [w_in_b] + win_pb)
        S.op("pool", lambda e: e.memset(identb2, 0.0), writes=[identb2_b])
        S.op("pool", lambda e: e.affine_select(out=identb2, in_=identb2, compare_op=ALU.not_equal, fill=1.0,
                                               base=0, pattern=[[-1, 128]], channel_multiplier=1),
             reads=[identb2_b], writes=[identb2_b])
        S.op("pool", lambda e: e.memset(mhalf2, -0.5), writes=[mhalf2_b])
        load_mod_consts("p", *PB, part="sg_dma")
        B_load(tilesA[0])

    st2c = [0]

    def new_stat2():
        c = st2c[0]
        st2c[0] = (c + 1) % 64
        return stat2[:, c:c + 1], stat2_b[c]

    def norm_stats2(src, src_b, P, junk, junk_b):
        ss, ss_b = new_stat2()
        ms, ms_b = new_stat2()
        rs, rs_b = new_stat2()
        S.op("act", lambda e: e.activation(out=junk[0:P, :], in_=src[0:P, :], func=AF.Square, accum_out=ss[0:P, :]),
             reads=[src_b], writes=[junk_b, ss_b])
        S.op("dve", lambda e: e.tensor_scalar(out=ms[0:P, :], in0=ss[0:P, :], scalar1=1.0 / D, scalar2=EPS,
                                              op0=ALU.mult, op1=ALU.add), reads=[ss_b], writes=[ms_b])
        S.op("pool", lambda e: e.tensor_tensor(out=rs[0:P, :], in0=ms[0:P, :], in1=mhalf2[0:P, 0:1], op=ALU.pow),
             reads=[ms_b, mhalf2_b], writes=[rs_b])
        return rs, rs_b

    cB = {"xs": 0, "r": 0, "ty": 0}

    def B_load(tl):
        P, NSUB = tl["P"], tl["NS"]
        tl["x1slots"] = []
        for j in range(NSUB):
            xt, xb = x1s[cB["xs"] % NX1]
            cB["xs"] += 1
            tl["x1slots"].append((xt, xb))
            r0 = tl["srow0"] + j * P
            S.dma("sp", xt[0:P, :], x1_scr[r0:r0 + P, :], [tl["x1b"][j]], [xb], xb.name)

    def B_s1_stats(tl):
        P, NSUB = tl["P"], tl["NS"]
        tl["hb2"] = []
        tl["rs2"] = []
        for j in range(NSUB):
            xt, xb = tl["x1slots"][j]
            hbt, hbb = hb2s[j % 2]
            tl["hb2"].append((hbt, hbb))
            tl["rs2"].append(modulate_stats(xt, xb, P, hbt, hbb, norm_stats2))

    def B_s1_apply(tl):
        P, NSUB = tl["P"], tl["NS"]
        for j in range(NSUB):
            xt, xb = tl["x1slots"][j]
            hbt, hbb = tl["hb2"][j]
            rs, rs_b = tl["rs2"][j]
            modulate_apply(xt, xb, P, rs, rs_b, g2B, g2B_b, sh2B, sh2B_b, hbt, hbb, tmpB, tmpB_b)

    def B_s1a(tl):
        B_s1_stats(tl)
        B_s1_apply(tl)

    def B_s1b(tl):
        P, NSUB, T = tl["P"], tl["NS"], tl["T"]
        hTt, hTb = h2T[tl["i"] % 2] if tl["kind"] == "p" else (h2Ts, h2Ts_b)
        tl["h2T"] = (hTt, hTb)
        tl["f1"] = (f1T, f1T_b) if tl["kind"] == "p" else (f1Ts, f1Ts_b)
        for j in range(NSUB):
            hbt, hbb = tl["hb2"][j]
            transpose_to(hbt, hbb, P, hTt, hTb, j, T, identb2, identb2_b)

    def B_ff1_job(tl, f):
        T = tl["T"]
        hTt, hTb = tl["h2T"]
        h3 = tview(hTt, T, 8)
        f1t, f1b = tl["f1"]
        f3 = tview(f1t, T, 32)
        bk, bkb = next_bank()

        def _mm(e, bk=bk, f=f):
            ins = None
            for k in range(8):
                ins = e.matmul(bk[:, 0:T], lhsT=wff1_v[:, k, f * 128:(f + 1) * 128], rhs=h3[:, k, :], start=(k == 0), stop=(k == 7))
            return ins
        S.op("pe", _mm, reads=[hTb, wff1_pb[f // (32 // NP1)]], writes=[bkb])
        rt, rb = rr[cB["r"] % 2]
        cB["r"] += 1
        S.op("act", lambda e, bk=bk, rt=rt: e.activation(out=rt[:, 0:T], in_=bk[:, 0:T], func=AF.Relu), reads=[bkb], writes=[rb])
        S.op("dve", lambda e, rt=rt, f=f: e.tensor_tensor(out=f3[:, f, :], in0=rt[:, 0:T], in1=rt[:, 0:T], op=ALU.mult),
             reads=[rb], writes=[f1b])

    def B_ff1_group(tl, f0, G=4):
        T = tl["T"]
        hTt, hTb = tl["h2T"]
        h3 = tview(hTt, T, 8)
        f1t, f1b = tl["f1"]
        f3 = tview(f1t, T, 32)
        bk, bkb = next_bank()

        def _mm(e, bk=bk):
            ins = None
            for g in range(G):
                f = f0 + g
                for k in range(8):
                    ins = e.matmul(bk[:, g * T:(g + 1) * T], lhsT=wff1_v[:, k, f * 128:(f + 1) * 128], rhs=h3[:, k, :], start=(k == 0), stop=(k == 7))
            return ins
        S.op("pe", _mm, reads=[hTb] + wff1_pb, writes=[bkb])
        rt, rb = rr[cB["r"] % 2]
        cB["r"] += 1
        S.op("act", lambda e, bk=bk, rt=rt: e.activation(out=rt[:, 0:G * T], in_=bk[:, 0:G * T], func=AF.Relu), reads=[bkb], writes=[rb])
        S.op("dve", lambda e, rt=rt: e.tensor_tensor(out=f3[:, f0:f0 + G, :], in0=rt[:, 0:G * T].rearrange("p (g t) -> p g t", g=G),
                                                     in1=rt[:, 0:G * T].rearrange("p (g t) -> p g t", g=G), op=ALU.mult),
             reads=[rb], writes=[f1b])

    def B_ff1(tl, hook=None, also=None):
        for f in range(32):
            if hook is not None and f == 12:
                hook()
            B_ff1_job(tl, f)
            if also is not None and f % 4 == 3:
                B_ff1_group(also, f - 3)

    def B_ff2(tl):
        P, NSUB, T = tl["P"], tl["NS"], tl["T"]
        f1t, f1b = tl["f1"]
        f3 = tview(f1t, T, 32)
        for j in range(NSUB):
            xt, xb = tl["x1slots"][j]
            for n in range(2):
                bk, bkb = next_bank()

                def _mm(e, bk=bk, j=j, n=n):
                    ins = None
                    for k in range(32):
                        ins = e.matmul(bk[0:P, :], lhsT=f3[:, k, j * P:(j + 1) * P], rhs=wff2_v[:, k, n * 512:(n + 1) * 512], start=(k == 0), stop=(k == 31))
                    return ins
                S.op("pe", _mm, reads=[f1b] + wff2_pb, writes=[bkb])
                tt, tb = tmpy[cB["ty"] % 2]
                cB["ty"] += 1
                S.op("dve", lambda e, bk=bk, tt=tt, n=n: e.tensor_tensor(out=tt[0:P, :], in0=bk[0:P, :], in1=gt2B[0:P, n * 512:(n + 1) * 512], op=ALU.mult),
                     reads=[bkb, gt2B_b], writes=[tb])
                S.op("dve", lambda e, tt=tt, xt=xt, n=n: e.tensor_tensor(out=xt[0:P, n * 512:(n + 1) * 512], in0=tt[0:P, :],
                                                                         in1=xt[0:P, n * 512:(n + 1) * 512], op=ALU.add),
                     reads=[tb, xb], writes=[xb])
            rs, rs_b = norm_stats2(xt, xb, P, tmpB, tmpB_b)
            S.op("dve", lambda e, xt=xt, rs=rs: e.scalar_tensor_tensor(out=xt[0:P, :], in0=xt[0:P, :], scalar=rs[0:P, 0:1], in1=gfB[0:P, :],
                                                                       op0=ALU.mult, op1=ALU.mult),
                 reads=[xb, rs_b, gfB_b], writes=[xb])
            if tl["kind"] == "p":
                dst = y_p[tl["row0"] + j * P: tl["row0"] + (j + 1) * P, :]
            else:
                dst = y_s
            S.dma("sp", dst, xt[0:P, :], [xb], [], xb.name, is_output=True)

    PB = (g2B, g2B_b, sh2B, sh2B_b, gt2B, gt2B_b, tmpB, tmpB_b, g_ffn, 3)
    load_mod_consts("s", g1B, g1B_b, sh1B, sh1B_b, gt1B, gt1B_b, tmpA, tmpA_b, g_mix, 0, part="gt")
    A_tile(tS, None, [], post_bch=passB_early)
    old2 = R2.reset()
    wff2_t, _ = R2.alloc("wff2", [128, 32 * D], BF16)
    wff2_v = wff2_t.rearrange("p (k n) -> p k n", k=32)
    NP2 = 8
    wff2_pb = [Buf("wff2_%d" % i) for i in range(NP2)]
    alias_after(wff2_pb, old2 + aT_bs)
    wff2_src = w_ff2.rearrange("(k p) n -> p k n", p=128)
    def emit_wff2(after=()):
        ks = 32 // NP2
        for i in range(NP2):
            S.dma("pool", wff2_v[:, i * ks:(i + 1) * ks, :], wff2_src[:, i * ks:(i + 1) * ks, :], [], [wff2_pb[i]], wff2_pb[i].name, after=after)

    alias_after(late_b, old3 + stat_b + win_pb + wout_pb)
    load_mod_consts("p", *PB, part="sg_op")
    B_s1_stats(tilesA[0])
    emit_wff2(after=[sh2B_b, tmpB_b, g2B_b] + [b for _, b in tilesA[0]["x1slots"]])
    B_s1_apply(tilesA[0])
    B_s1b(tilesA[0])
    load_mod_consts("p", *PB, part="gt")
    S.dma("sp", gfB, g_final.partition_broadcast(128), [], [gfB_b], "gfB")
    B_load(tS)
    for i, tl in enumerate(tilesA):
        nxt = tilesA[i + 1] if i + 1 < ntile else None
        if nxt is not None:
            B_load(nxt)
            B_ff1(tl, hook=lambda nxt=nxt: B_s1a(nxt), also=(tS if i == 1 else None))
            B_s1b(nxt)
        else:
            B_ff1(tl)
        if i == 0:
            load_mod_consts("s", *PB, part="sg")
            B_s1a(tS)
            load_mod_consts("p16", *PB, part="sg")
        B_ff2(tl)
        if i == 0:
            B_s1b(tS)
        if i == 1:
            load_mod_consts("s", *PB, part="gt")
            B_ff2(tS)
            load_mod_consts("p16", *PB, part="gt")

    _DBG[0] = S
    sem_names = set()
    for e in S.ENG:
        sem_names.add("c_" + e)
        for waits, fn, inc in S.ops[e]:
            sem_names.add(inc[0])
            for s, v in waits:
                sem_names.add(s)
    with contextlib.ExitStack() as es:
        sems = {n: es.enter_context(nc.semaphore(n)) for n in sorted(sem_names)}
        block = es.enter_context(nc.Block())

        def replay(name, eng, final=False):
            for waits, fn, inc in S.ops[name]:
                for s, v in waits:
                    eng.wait_ge(sems[s], v)
                ins = fn(eng)
                ins.then_inc(sems[inc[0]], inc[1])
            if final:
                for s in sorted(S.out_sems):
                    eng.wait_ge(sems[s], S.dma_tot[s])

        @block.tensor
        def _(eng):
            replay("pe", eng)

        @block.scalar
        def _(eng):
            replay("act", eng)

        @block.vector
        def _(eng):
            replay("dve", eng)

        @block.gpsimd
        def _(eng):
            replay("pool", eng)

        @block.sync
        def _(eng):
            replay("sp", eng, final=True)
    return nc


_NC = [None]


def kernel(x_prompt, x_sample, c_prompt, c_sample, state_conv, g_mix, w_ada, b_ada, w_in, g_v, w_s, b_s,
           w_conv, w_out, g_ffn, w_ff1, w_ff2, g_final):
    f = lambda a: np.ascontiguousarray(np.asarray(a, dtype=np.float32))
    x_prompt, x_sample, c_prompt, c_sample, state_conv = map(f, (x_prompt, x_sample, c_prompt, c_sample, state_conv))
    shared = {
        "g_mix": f(g_mix).reshape(1, D), "w_ada": f(w_ada).reshape(D, 6 * D), "b_ada": f(b_ada).reshape(1, 6 * D),
        "w_in": f(w_in).reshape(D, DIN), "g_v": f(g_v).reshape(1, DA), "w_s": f(w_s).reshape(8, 128, 128),
        "b_s": f(b_s).reshape(8, 128), "w_conv": f(w_conv).reshape(3, 512), "w_out": f(w_out).reshape(D, D),
        "g_ffn": f(g_ffn).reshape(1, D), "w_ff1": f(w_ff1).reshape(D, DFF), "w_ff2": f(w_ff2).reshape(DFF, D),
        "g_final": f(g_final).reshape(1, D),
    }
    in_maps = []
    for c in range(NCORES):
        m = dict(shared)
        m["x_p"] = np.ascontiguousarray(x_prompt[c])
        m["x_s"] = np.ascontiguousarray(x_sample[c * NS_TOK:(c + 1) * NS_TOK, 0, :])
        m["c_p"] = np.ascontiguousarray(c_prompt[c:c + 1])
        m["c_s"] = np.ascontiguousarray(c_sample[c * NS_TOK:(c + 1) * NS_TOK])
        m["sconv"] = np.ascontiguousarray(state_conv[0, c * NS_TOK:(c + 1) * NS_TOK].reshape(NS_TOK, 1024))
        in_maps.append(m)
    if _NC[0] is None:
        _NC[0] = build_nc()
    res = run_bass_kernel_spmd(_NC[0], in_maps, core_ids=list(range(NCORES)))
    r = res.results
    y_prompt = np.stack([r[c]["y_p"] for c in range(NCORES)], axis=0).astype(np.float32)
    y_sample = np.concatenate([r[c]["y_s"] for c in range(NCORES)], axis=0).reshape(NCORES * NS_TOK, 1, D).astype(np.float32)
    ncp = np.stack([r[c]["ncp"] for c in range(NCORES)], axis=0).reshape(1, NCORES, 2, 512).astype(np.float32)
    ncs = np.concatenate([r[c]["ncs"] for c in range(NCORES)], axis=0).reshape(1, NCORES * NS_TOK, 2, 512).astype(np.float32)
    nvs = np.concatenate([r[c]["nvs"] for c in range(NCORES)], axis=0).reshape(1, NCORES * NS_TOK, 1, 512).astype(np.float32)
    return (y_prompt, y_sample, ncp, ncs, nvs)
```

```python
import contextlib
import numpy as np
import concourse.bass as bass
import concourse.mybir as mybir
from concourse.bass_utils import run_bass_kernel_spmd

F32 = mybir.dt.float32
BF16 = mybir.dt.bfloat16
U8 = mybir.dt.uint8
AF = mybir.ActivationFunctionType
ALU = mybir.AluOpType
AX = mybir.AxisListType

NCORES = 8
D = 1024
SEQ = 2048
NS_TOK = 16
DA = 512
DFF = 4096
DIN = 2560
EPS = 1e-6
TILE = 256
ESZ = {F32: 4, BF16: 2, U8: 1}


_DBG = [None]


class Buf:
    __slots__ = ("name", "w", "r")

    def __init__(self, name):
        self.name = name
        self.w = None
        self.r = []


class Sched:
    ENG = ("pe", "act", "dve", "pool", "sp")

    def __init__(self):
        self.ops = {e: [] for e in self.ENG}
        self.cnt = {e: 0 for e in self.ENG}
        self.waited = {e: {} for e in self.ENG}
        self.dma_tot = {}
        self.out_sems = set()

    def _waits(self, e, reads, writes):
        need = {}

        def add(h, raw):
            if h is None:
                return
            if h[0] == "E":
                _, pe_, seq = h
                if pe_ == e:
                    if e in ("pe", "sp"):
                        return
                s = "c_" + pe_
                need[s] = max(need.get(s, 0), seq)
            else:
                s = h[1]
                need[s] = max(need.get(s, 0), self.dma_tot[s])

        for b in reads:
            add(b.w, True)
        for b in writes:
            add(b.w, False)
            for h in b.r:
                add(h, False)
        out = []
        wd = self.waited[e]
        for s, v in need.items():
            if wd.get(s, 0) < v:
                wd[s] = v
                out.append((s, v))
        return out

    def _record(self, h, reads, writes):
        for b in writes:
            b.w = h
            b.r = []
        for b in reads:
            b.r.append(h)

    def op(self, e, fn, reads=(), writes=()):
        waits = self._waits(e, reads, writes)
        self.cnt[e] += 1
        h = ("E", e, self.cnt[e])
        self.ops[e].append((waits, fn, ("c_" + e, 1)))
        self._record(h, reads, writes)
        return h

    def dma(self, q, out, in_, reads, writes, sem, is_output=False, noncontig=False, after=()):
        waits = self._waits(q, list(reads) + list(after), writes)
        s = "d_" + sem
        self.dma_tot[s] = self.dma_tot.get(s, 0) + 16
        h = ("D", s, self.dma_tot[s])
        if noncontig:
            fn = lambda eng, o=out, i=in_: eng.dma_start(out=o, in_=i, allow_slow_non_contiguous=True)
        else:
            fn = lambda eng, o=out, i=in_: eng.dma_start(out=o, in_=i)
        self.ops[q].append((waits, fn, (s, 16)))
        self._record(h, reads, writes)
        if is_output:
            self.out_sems.add(s)
        return h


def alias_after(new_bufs, old_bufs):
    hs = []
    for b in old_bufs:
        if b.w is not None:
            hs.append(b.w)
        hs.extend(b.r)
    for nb in new_bufs:
        nb.w = None
        nb.r = list(hs)


class Region:
    def __init__(self, arena, base, size, name):
        self.arena, self.base, self.size, self.off, self.name = arena, base, size, 0, name
        self.bufs = []

    def reset(self):
        old = self.bufs
        self.bufs = []
        self.off = 0
        return old

    def alloc(self, name, shape, dt):
        n = 1
        for d in shape[1:]:
            n *= d
        nbytes = n * ESZ[dt]
        self.off = (self.off + 31) // 32 * 32
        assert self.off + nbytes <= self.size, (self.name, name, self.off, nbytes, self.size)
        o = self.base + self.off
        self.off += nbytes
        v = self.arena[:, o:o + nbytes].bitcast(dt)
        b = Buf(name)
        self.bufs.append(b)
        return v, b


def build_nc():
    nc = bass.Bass("TRN2", target_bir_lowering=False)
    S = Sched()

    def din(name, shape):
        return nc.dram_tensor(name, shape, F32, kind="ExternalInput").ap()

    def dout(name, shape):
        return nc.dram_tensor(name, shape, F32, kind="ExternalOutput").ap()

    x_p = din("x_p", [SEQ, D]); x_s = din("x_s", [NS_TOK, D])
    c_p = din("c_p", [1, D]); c_s = din("c_s", [NS_TOK, D])
    sconv = din("sconv", [NS_TOK, 1024])
    g_mix = din("g_mix", [1, D]); w_ada = din("w_ada", [D, 6 * D]); b_ada = din("b_ada", [1, 6 * D])
    w_in = din("w_in", [D, DIN]); g_v = din("g_v", [1, DA]); w_s = din("w_s", [8, 128, 128])
    b_s = din("b_s", [8, 128]); w_conv = din("w_conv", [3, 512]); w_out = din("w_out", [D, D])
    g_ffn = din("g_ffn", [1, D]); w_ff1 = din("w_ff1", [D, DFF]); w_ff2 = din("w_ff2", [DFF, D])
    g_final = din("g_final", [1, D])
    y_p = dout("y_p", [SEQ, D]); y_s = dout("y_s", [NS_TOK, D])
    ncp = dout("ncp", [2, 512]); ncs = dout("ncs", [NS_TOK, 1024]); nvs = dout("nvs", [NS_TOK, 512])
    x1_scr = nc.dram_tensor("x1_scr", [SEQ + NS_TOK, D], F32).ap()
    mod_scr = nc.dram_tensor("mod_scr", [48, 6 * D], F32).ap()
    mod_bufs = [Buf("modscr%d" % i) for i in range(6)]

    TOTAL = 212700
    arena = nc.alloc_sbuf_tensor("arena", [128, TOTAL], U8).ap()
    R1 = Region(arena, 0, 65536, "R1")
    R2 = Region(arena, 65536, 65664, "R2")
    R3 = Region(arena, 131200, TOTAL - 131200, "R3")

    banks = []
    for b in range(8):
        banks.append((nc.alloc_psum_tensor("ps%d" % b, [128, 512], F32).ap(), Buf("bank%d" % b)))
    bank_i = [0]

    def next_bank():
        b = banks[bank_i[0] % 8]
        bank_i[0] += 1
        return b

    w_in_t, w_in_b = R3.alloc("w_in", [128, 8 * DIN], BF16)
    w_in_v = w_in_t.rearrange("p (k n) -> p k n", k=8)
    w_out_t, w_out_b = R3.alloc("w_out", [128, 8 * D], BF16)
    w_out_v = w_out_t.rearrange("p (k n) -> p k n", k=8)
    win_pb = [Buf("win%d" % i) for i in range(5)]
    wout_pb = [Buf("wout%d" % i) for i in range(2)]
    g1B, g1B_b = R3.alloc("g1B", [128, D], F32)
    sh1B, sh1B_b = R3.alloc("sh1B", [128, D], F32)
    gt1B, gt1B_b = R3.alloc("gt1B", [128, D], F32)
    gvBe, gvBe_b = R3.alloc("gvBe", [128, DA], F32)
    gvBo, gvBo_b = R3.alloc("gvBo", [128, DA], F32)
    wmT, wmT_b = R3.alloc("wmT", [128, 8 * 128], BF16)
    wmT3 = wmT.rearrange("p (h t) -> p h t", h=8)
    wmTs, wmTs_b = R3.alloc("wmTs", [128, 8 * 16], BF16)
    wmTs3 = wmTs.rearrange("p (h t) -> p h t", h=8)
    identb, identb_b = R3.alloc("identb", [128, 128], BF16)
    identf, identf_b = R3.alloc("identf", [128, 128], F32)
    wconvT, wconvT_b = R3.alloc("wconvT", [128, 16], F32)
    wconvT_bs = [wconvT_b]
    wconvT3 = wconvT.rearrange("p (c j) -> p c j", c=4)
    bsK, bsK_b = R3.alloc("bsK", [128, 4 * 128], BF16)
    bsK3 = bsK.rearrange("p (r t) -> p r t", r=4)
    bsKs, bsKs_b = R3.alloc("bsKs", [128, 4 * 16], BF16)
    bsKs3 = bsKs.rearrange("p (r t) -> p r t", r=4)
    indK, indK_b = R3.alloc("indK", [128, 128], BF16)
    mhalf, mhalf_b = R3.alloc("mhalf", [128, 8], F32)
    ones1, ones1_b = R3.alloc("ones1", [128, 128], F32)
    w00, w00_b = R3.alloc("w00", [128, 8], F32)
    zsh, zsh_b = R3.alloc("zsh", [128, 2 * 4 * 16], F32)
    stat, _ = R3.alloc("stat", [128, 64], F32)
    stat_b = [Buf("stat%d" % i) for i in range(64)]
    statc = [0]

    def new_stat(n=1):
        c = statc[0]
        if c + n > 64:
            c = 0
        statc[0] = c + n
        return stat[:, c:c + n], stat_b[c]

    NWS = 4
    wada = []
    for i in range(NWS):
        t, b = R1.alloc("wada%d" % i, [128, 8 * 512], BF16)
        wada.append((t.rearrange("p (k n) -> p k n", k=8), b))
    bada, bada_b = R1.alloc("bada", [128, 6 * D], F32)
    cin, cin_b = R1.alloc("cin", [128, D], F32)
    siluCT, siluCT_b = R1.alloc("siluCT", [128, 8 * 48], BF16)
    siluCT3 = siluCT.rearrange("p (k t) -> p k t", k=8)
    modst = [R1.alloc("modst%d" % i, [128, 512], F32) for i in range(1)]

    wada_src = w_ada.rearrange("(k p) n -> p k n", p=128)
    win_src = w_in.rearrange("(k p) n -> p k n", p=128)
    wout_src = w_out.rearrange("(k p) n -> p k n", p=128)
    WIN_COLS = {"u": (0, 512), "v": (512, 1024), "B": (1024, 1536), "C": (1536, 2048), "h": (2048, 2560)}
    WIN_PB = {"u": win_pb[0], "v": win_pb[1], "B": win_pb[2], "C": win_pb[3], "h": win_pb[4]}

    def ada_dma(q):
        wt, wb = wada[q % NWS]
        S.dma("pool", wt, wada_src[:, :, q * 512:(q + 1) * 512], [], [wb], wb.name)

    def ada_compute(q):
        wt, wb = wada[q % NWS]
        bk, bkb = next_bank()

        def _mm(e, bk=bk, wt=wt, q=q):
            ins = None
            for k in range(8):
                ins = e.matmul(bk[0:48, :], lhsT=siluCT3[:, k, :], rhs=wt[:, k, :], start=(k == 0), stop=(k == 7))
            return ins
        S.op("pe", _mm, reads=[siluCT_b, wb], writes=[bkb])
        mt, mb = modst[q % len(modst)]
        S.op("dve", lambda e, bk=bk, mt=mt, q=q: e.tensor_tensor(out=mt[0:48, :], in0=bk[0:48, :], in1=bada[0:48, q * 512:(q + 1) * 512], op=ALU.add),
             reads=[bkb, bada_b], writes=[mb])
        S.dma("sp", mod_scr[:, q * 512:(q + 1) * 512], mt[0:48, :], [mb], [mod_bufs[q // 2]], "modscr%d" % (q // 2))
        if q < 6:
            bk2, bk2b = next_bank()
            S.op("pe", lambda e, bk2=bk2, mt=mt: e.matmul(bk2[:, :], lhsT=ones1[0:1, 0:128], rhs=mt[0:1, :], start=True, stop=True),
                 reads=[ones1_b, mb], writes=[bk2b])
            cs = slice((q % 2) * 512, (q % 2 + 1) * 512)
            if q < 2:
                S.op("act", lambda e, bk2=bk2, cs=cs: e.activation(out=sh1B[:, cs], in_=bk2[:, :], func=AF.Copy), reads=[bk2b], writes=[sh1B_b])
            elif q < 4:
                S.op("dve", lambda e, bk2=bk2, cs=cs: e.scalar_tensor_tensor(out=g1B[:, cs], in0=bk2[:, :], scalar=1.0, in1=g1B[:, cs],
                                                                            op0=ALU.add, op1=ALU.mult),
                     reads=[bk2b, g1B_b], writes=[g1B_b])
            else:
                S.op("act", lambda e, bk2=bk2, cs=cs: e.activation(out=gt1B[:, cs], in_=bk2[:, :], func=AF.Copy), reads=[bk2b], writes=[gt1B_b])

    def load_mod_consts(kind, gT, gB_, shT, shB_, gtT, gtB_, tmp, tmp_b, gsrc, base, part="all"):
        if kind == "p":
            rows = slice(0, 128)
            def src(i):
                return mod_scr[0:1, (base + i) * D:(base + i + 1) * D].partition_broadcast(128)
            gs = gsrc.partition_broadcast(128)
        elif kind == "p16":
            rows = slice(0, 16)
            def src(i):
                return mod_scr[0:1, (base + i) * D:(base + i + 1) * D].partition_broadcast(16)
            gs = gsrc.partition_broadcast(16)
        else:
            rows = slice(0, 16)
            def src(i):
                return mod_scr[32:48, (base + i) * D:(base + i + 1) * D]
            gs = gsrc.partition_broadcast(16)
        if part in ("sg", "all", "sg_dma"):
            S.dma("sp", shT[rows, :], src(0), [mod_bufs[base + 0]], [shB_], shB_.name)
            S.dma("sp", tmp[rows, :], src(1), [mod_bufs[base + 1]], [tmp_b], tmp_b.name)
            S.dma("sp", gT[rows, :], gs, [], [gB_], gB_.name)
        if part in ("sg", "all", "sg_op"):
            S.op("dve", lambda e: e.scalar_tensor_tensor(out=gT[rows, :], in0=tmp[rows, :], scalar=1.0, in1=gT[rows, :],
                                                         op0=ALU.add, op1=ALU.mult),
                 reads=[tmp_b, gB_], writes=[gB_])
        if part in ("gt", "all"):
            S.dma("sp", gtT[rows, :], src(2), [mod_bufs[base + 2]], [gtB_], gtB_.name)

    xs1 = [R2.alloc("xs1_%d" % i, [128, D], F32) for i in range(2)]
    xs2 = [R2.alloc("xs2_%d" % i, [128, D], F32) for i in range(2)]
    hbs = [R2.alloc("hb%d" % i, [128, D], BF16) for i in range(2)]
    tmpA, tmpA_b = R2.alloc("tmpA", [128, D], F32)
    hT = []
    for i in range(2):
        t, b = R2.alloc("hT%d" % i, [128, 8 * TILE], BF16)
        hT.append((t, b))
    vgs = [R2.alloc("vg%d" % i, [128, DA], F32) for i in range(2)]
    sqvs = [R2.alloc("sqv%d" % i, [128, DA], F32) for i in range(2)]
    vg, vg_b = vgs[0]
    sqv, sqv_b = sqvs[0]
    tmpv, tmpv_b = R2.alloc("tmpv", [128, DA], F32)
    vn = [R2.alloc("vn%d" % i, [128, 2 * DA], BF16) for i in range(2)]
    uT, uT_b = R2.alloc("uT", [128, 4 * TILE], BF16)
    Csb = [R2.alloc("Csb%d" % i, [128, TILE], F32) for i in range(2)]
    zb = [R2.alloc("z%d" % i, [128, 4 * (TILE + 2)], F32) for i in range(1)]
    accb = [R2.alloc("acc%d" % i, [128, TILE], F32) for i in range(2)]
    aT, aT_b = R2.alloc("aT", [128, 4 * TILE], BF16)
    bT, bT_b = R2.alloc("bT", [128, 4 * TILE], BF16)
    tmpx = [R2.alloc("tmpx%d" % i, [128, 512], F32) for i in range(2)]
    stg, stg_b = tmpx[0]
    modst.append(tmpx[1])

    ind2 = Csb[0][0][:, 0:128]; ind2_b = Csb[0][1]
    bsrow3 = tmpv.rearrange("p (r t) -> p r t", r=4); bsrow_b = tmpv_b
    tmpf3 = sqv.rearrange("p (r t) -> p r t", r=4); tmpf_b = sqv_b
    bslo3 = hbs[0][0][:, 0:512].rearrange("p (r t) -> p r t", r=4); bslo_b = hbs[0][1]
    wst3 = tmpA.rearrange("p (h s) -> p h s", h=8); wst_b = tmpA_b
    sct = xs2[0][0]; sct_b = xs2[0][1]
    gve4 = gvBe.rearrange("p (a e d) -> p a e d", a=4, e=2)
    gvo4 = gvBo.rearrange("p (a e d) -> p a e d", a=4, e=2)
    zsh4 = zsh.rearrange("p (j c t) -> p j c t", j=2, c=4)
    d2d_b = Buf("d2d")

    def setup_early_a():
        S.op("pool", lambda e: e.memset(identf, 0.0), writes=[identf_b])
        S.op("pool", lambda e: e.affine_select(out=identf, in_=identf, compare_op=ALU.not_equal, fill=1.0,
                                               base=0, pattern=[[-1, 128]], channel_multiplier=1),
             reads=[identf_b], writes=[identf_b])
        S.op("pool", lambda e: e.memset(cin[0:48, :], 0.0), writes=[cin_b])
        S.dma("sp", cin[32:48, :], c_s, [], [cin_b], "cin")
        S.dma("sp", cin[0:1, :], c_p, [], [cin_b], "cin")
        S.dma("sp", bada[0:48, :], b_ada.partition_broadcast(48), [], [bada_b], "bada")
        S.op("pool", lambda e: e.memset(ones1[0:1, :], 1.0), writes=[ones1_b])
        S.dma("sp", g1B, g_mix.partition_broadcast(128), [], [g1B_b], "g1B")

    def setup_early_b():
        S.op("pool", lambda e: e.memset(identb, 0.0), writes=[identb_b])
        S.op("pool", lambda e: e.affine_select(out=identb, in_=identb, compare_op=ALU.not_equal, fill=1.0,
                                               base=0, pattern=[[-1, 128]], channel_multiplier=1),
             reads=[identb_b], writes=[identb_b])
        S.op("pool", lambda e: e.memset(mhalf, -0.5), writes=[mhalf_b])

    def setup_late_loads():
        S.dma("sp", gvBe, g_v.partition_broadcast(128), [], [gvBe_b], "gvBe")
        S.dma("sp", gvBo, g_v.partition_broadcast(128), [], [gvBo_b], "gvBo")
        S.dma("sp", bsrow3[0:2, :, :], b_s.rearrange("(r e) t -> e r t", e=2), [], [bsrow_b], "bsrow")
        S.dma("sp", wst3, w_s.rearrange("h t s -> t h s"), [], [wst_b], "wst")


    def setup_late_prep():
        S.op("pool", lambda e: e.memset(gve4[:, :, 1, :], 0.0), reads=[], writes=[gvBe_b])
        S.op("pool", lambda e: e.memset(gvo4[:, :, 0, :], 0.0), reads=[], writes=[gvBo_b])
        S.op("pool", lambda e: e.affine_select(out=wst3, in_=wst3, compare_op=ALU.is_ge, fill=0.0, base=0,
                                               pattern=[[0, 8], [-1, 128]], channel_multiplier=1),
             reads=[wst_b], writes=[wst_b])
        S.op("pool", lambda e: e.memset(ind2[0:2, :], 1.0), writes=[ind2_b])
        S.op("pool", lambda e: e.affine_select(out=ind2[0:2, :], in_=ind2[0:2, :], compare_op=ALU.is_ge, fill=0.0,
                                               base=0, pattern=[[1, 128]], channel_multiplier=-64),
             reads=[ind2_b], writes=[ind2_b])
        S.op("pool", lambda e: e.affine_select(out=ind2[0:2, :], in_=ind2[0:2, :], compare_op=ALU.is_ge, fill=0.0,
                                               base=63, pattern=[[-1, 128]], channel_multiplier=64),
             reads=[ind2_b], writes=[ind2_b])
        S.op("pool", lambda e: e.memset(indK[0:34, :], 0.0), writes=[indK_b])
        S.op("dve", lambda e: e.tensor_copy(out=indK[0:2, :], in_=ind2[0:2, :]), reads=[ind2_b], writes=[indK_b])
        S.dma("sp", indK[32:34, :], indK[0:2, :], [indK_b], [indK_b], "indK")
        S.op("pool", lambda e: e.memset(bsK[0:34, :], 0.0), writes=[bsK_b])
        S.op("dve", lambda e: e.tensor_copy(out=bsK3[0:2, :, :], in_=bsrow3[0:2, :, :]), reads=[bsrow_b], writes=[bsK_b])
        S.op("dve", lambda e: e.tensor_copy(out=tmpf3[0:2, :, :], in_=bsK3[0:2, :, :]), reads=[bsK_b], writes=[tmpf_b])
        S.op("dve", lambda e: e.tensor_tensor(out=tmpf3[0:2, :, :], in0=bsrow3[0:2, :, :], in1=tmpf3[0:2, :, :], op=ALU.subtract),
             reads=[bsrow_b, tmpf_b], writes=[tmpf_b])
        S.op("dve", lambda e: e.tensor_copy(out=bslo3[0:2, :, :], in_=tmpf3[0:2, :, :]), reads=[tmpf_b], writes=[bslo_b])
        S.dma("sp", bsK3[32:34, :, :], bslo3[0:2, :, :], [bslo_b], [bsK_b], "bsK")

    def setup_late():
        for half in range(2):
            bk, bkb = next_bank()

            def _tr_w(e, bk=bk, half=half):
                ins = None
                for hh in range(4):
                    ins = e.transpose(bk[:, hh * 128:(hh + 1) * 128], wst3[:, half * 4 + hh, :], identf)
                return ins
            S.op("pe", _tr_w, reads=[wst_b, identf_b], writes=[bkb])
            S.op("dve", lambda e, bk=bk, half=half: e.tensor_copy(out=wmT[:, half * 512:(half + 1) * 512], in_=bk[:, 0:512]),
                 reads=[bkb], writes=[wmT_b])

    def setup_sample_hist():
        S.dma("sp", cin[0:16, :], sconv, [], [cin_b], "cin")
        bk, bkb = next_bank()

        def _tr_s(e, bk=bk):
            ins = None
            for jc in range(8):
                ins = e.transpose(bk[:, jc * 16:(jc + 1) * 16], cin[0:16, jc * 128:(jc + 1) * 128], identf[0:16, 0:16])
            return ins
        S.op("pe", _tr_s, reads=[cin_b, identf_b], writes=[bkb])
        S.op("dve", lambda e, bk=bk: e.tensor_copy(out=zsh, in_=bk[:, 0:128]), reads=[bkb], writes=[zsh_b])

    def setup_sample():
        S.dma("sp", w00[0:16, :], w_s[:, 0, 0:1].rearrange("h o -> o h").partition_broadcast(16), [], [w00_b], "w00", noncontig=True)
        S.dma("sp", ncs[:, 0:512], sconv[:, 512:1024], [], [d2d_b], "d2d", is_output=True)
        for h in range(8):
            S.op("dve", lambda e, h=h: e.tensor_scalar(out=wmTs3[0:16, h, :], in0=identf[0:16, 0:16], scalar1=w00[0:16, h:h + 1],
                                                       scalar2=0.0, op0=ALU.mult, op1=ALU.add),
                 reads=[identf_b, w00_b], writes=[wmTs_b])
        S.op("dve", lambda e: e.tensor_copy(out=bsKs3[0:34, :, :], in_=bsK3[0:34, :, 0:1].broadcast_to([34, 4, 16])),
             reads=[bsK_b], writes=[bsKs_b])

    def setup_conv():
        wcv = tmpv
        S.op("pool", lambda e: e.memset(wcv[0:4, :], 0.0), reads=[], writes=[tmpv_b])
        S.dma("sp", wcv[0:3, :], w_conv, [], [tmpv_b], "wcv")
        bk, bkb = next_bank()

        def _tr_cv(e, bk=bk):
            ins = None
            for c in range(4):
                ins = e.transpose(bk[:, c * 4:(c + 1) * 4], wcv[0:4, c * 128:(c + 1) * 128], identf[0:4, 0:4])
            return ins
        S.op("pe", _tr_cv, reads=[tmpv_b, identf_b], writes=[bkb])
        S.op("dve", lambda e, bk=bk: e.tensor_copy(out=wconvT, in_=bk[:, 0:16]), reads=[bkb], writes=[wconvT_b])

    ctr = {"xs": 0, "c": 0, "tx": 0, "hb": 0}

    def norm_stats(src, src_b, P, junk, junk_b):
        ss, ss_b = new_stat(1)
        ms, ms_b = new_stat(1)
        rs, rs_b = new_stat(1)
        S.op("act", lambda e: e.activation(out=junk[0:P, :], in_=src[0:P, :], func=AF.Square, accum_out=ss[0:P, :]),
             reads=[src_b], writes=[junk_b, ss_b])
        S.op("act", lambda e: e.activation(out=ms[0:P, :], in_=ss[0:P, :], func=AF.Identity, scale=1.0 / D, bias=EPS),
             reads=[ss_b], writes=[ms_b])
        S.op("pool", lambda e: e.tensor_tensor(out=rs[0:P, :], in0=ms[0:P, :], in1=mhalf[0:P, 0:1], op=ALU.pow),
             reads=[ms_b, mhalf_b], writes=[rs_b])
        return rs, rs_b

    def modulate_stats(src, src_b, P, hbt, hbb, nstats):
        return nstats(src, src_b, P, hbt, hbb)

    def modulate_apply(src, src_b, P, rs, rs_b, gT, gB_, shT, shB_, hbt, hbb, tmpT, tmpT_b):
        S.op("dve", lambda e: e.scalar_tensor_tensor(out=tmpT[0:P, :], in0=src[0:P, :], scalar=rs[0:P, 0:1], in1=gT[0:P, :],
                                                     op0=ALU.mult, op1=ALU.mult),
             reads=[src_b, rs_b, gB_], writes=[tmpT_b])
        S.op("dve", lambda e: e.tensor_tensor(out=hbt[0:P, :], in0=tmpT[0:P, :], in1=shT[0:P, :], op=ALU.add),
             reads=[tmpT_b, shB_], writes=[hbb])

    def modulate(src, src_b, P, gT, gB_, shT, shB_, hbt, hbb, tmpT, tmpT_b, nstats):
        rs, rs_b = modulate_stats(src, src_b, P, hbt, hbb, nstats)
        modulate_apply(src, src_b, P, rs, rs_b, gT, gB_, shT, shB_, hbt, hbb, tmpT, tmpT_b)

    def transpose_to(hbt, hbb, P, dstT, dstT_b, j, T, ident, ident_b):
        bk, bkb = next_bank()
        psb = bk.bitcast(BF16)

        def _tr(e):
            ins = None
            for k in range(8):
                ins = e.transpose(psb[:, k * P:(k + 1) * P], hbt[0:P, k * 128:(k + 1) * 128], ident[0:P, 0:P])
            return ins
        S.op("pe", _tr, reads=[hbb, ident_b], writes=[bkb])
        d3 = dstT[:, 0:8 * T].rearrange("p (k t) -> p k t", k=8)[:, :, j * P:(j + 1) * P]
        S.op("act", lambda e: e.activation(out=d3, in_=psb[:, 0:8 * P].rearrange("p (k t) -> p k t", k=8), func=AF.Copy),
             reads=[bkb], writes=[dstT_b])

    def tview(t, T, nk):
        return t[:, 0:nk * T].rearrange("p (k t) -> p k t", k=nk)

    def A_load(tl):
        P, NSUB = tl["P"], tl["NS"]
        tl["xs1"] = []
        for j in range(NSUB):
            xt, xb = xs1[j % 2]
            tl["xs1"].append((xt, xb))
            srcrows = x_p[tl["row0"] + j * P: tl["row0"] + (j + 1) * P, :] if tl["kind"] == "p" else x_s
            S.dma("sp", xt[0:P, :], srcrows, [], [xb], xb.name)

    def A_s1_stats(tl, j):
        P = tl["P"]
        xt, xb = tl["xs1"][j]
        hbt, hbb = hbs[j % 2]
        tl.setdefault("hb", {})[j] = (hbt, hbb)
        tl.setdefault("rs", {})[j] = modulate_stats(xt, xb, P, hbt, hbb, norm_stats)

    def A_s1_mod(tl, j):
        P = tl["P"]
        xt, xb = tl["xs1"][j]
        hbt, hbb = tl["hb"][j]
        rs, rs_b = tl["rs"][j]
        modulate_apply(xt, xb, P, rs, rs_b, g1B, g1B_b, sh1B, sh1B_b, hbt, hbb, tmpA, tmpA_b)

    def A_s1a(tl, j):
        A_s1_stats(tl, j)
        A_s1_mod(tl, j)

    def A_s1b(tl, j):
        P, T = tl["P"], tl["T"]
        hTt, hTb = hT[tl["i"] % 2]
        tl["hT"] = (hTt, hTb)
        hbt, hbb = tl["hb"][j]
        transpose_to(hbt, hbb, P, hTt, hTb, j, T, identb, identb_b)

    def A_xreload(tl):
        P, NSUB = tl["P"], tl["NS"]
        tl["xslots"] = []
        for j in range(NSUB):
            xt, xb = xs2[j % 2]
            tl["xslots"].append((xt, xb))
            srcrows = x_p[tl["row0"] + j * P: tl["row0"] + (j + 1) * P, :] if tl["kind"] == "p" else x_s
            S.dma("sp", xt[0:P, :], srcrows, [], [xb], xb.name)

    def A_v_mm(tl, j):
        P, T = tl["P"], tl["T"]
        hTt, hTb = tl["hT"]
        h3 = tview(hTt, T, 8)
        lo, hi = WIN_COLS["v"]
        bk, bkb = next_bank()
        vgt, vgb = vgs[j % 2]
        sqt, sqb = sqvs[j % 2]

        def _mm(e, bk=bk, j=j):
            ins = None
            for k in range(8):
                ins = e.matmul(bk[0:P, :], lhsT=h3[:, k, j * P:(j + 1) * P], rhs=w_in_v[:, k, lo:hi], start=(k == 0), stop=(k == 7))
            return ins
        S.op("pe", _mm, reads=[hTb, WIN_PB["v"]], writes=[bkb])
        S.op("act", lambda e, bk=bk: e.activation(out=vgt[0:P, :], in_=bk[0:P, :], func=AF.Gelu_apprx_tanh),
             reads=[bkb], writes=[vgb])
        S.op("act", lambda e: e.activation(out=sqt[0:P, :], in_=vgt[0:P, :], func=AF.Square),
             reads=[vgb], writes=[sqb])

    def A_v_chain1(tl, j):
        P = tl["P"]
        sqt, sqb = sqvs[j % 2]
        ss, ss_b = new_stat(8)
        ms, ms_b = new_stat(8)
        rs, rs_b = new_stat(8)
        tl.setdefault("vrs", {})[j] = (rs, rs_b)
        S.op("dve", lambda e: e.tensor_reduce(out=ss[0:P, :], in_=sqt[0:P, :].rearrange("p (h d) -> p h d", h=8),
                                              axis=AX.X, op=ALU.add), reads=[sqb], writes=[ss_b])
        S.op("dve", lambda e: e.tensor_scalar(out=ms[0:P, :], in0=ss[0:P, :], scalar1=1.0 / 64, scalar2=EPS,
                                              op0=ALU.mult, op1=ALU.add), reads=[ss_b], writes=[ms_b])
        S.op("pool", lambda e: e.tensor_tensor(out=rs[0:P, :], in0=ms[0:P, :], in1=mhalf[0:P, :], op=ALU.pow),
             reads=[ms_b, mhalf_b], writes=[rs_b])

    def A_v_chain2(tl, j):
        P = tl["P"]
        vgt, vgb = vgs[j % 2]
        sqt, sqb = sqvs[j % 2]
        rs, rs_b = tl["vrs"][j]
        S.op("dve", lambda e: e.tensor_tensor(out=tmpv[0:P, :].rearrange("p (h d) -> p h d", h=8),
                                              in0=vgt[0:P, :].rearrange("p (h d) -> p h d", h=8),
                                              in1=rs[0:P, :].unsqueeze(2).broadcast_to([P, 8, 64]), op=ALU.mult),
             reads=[vgb, rs_b], writes=[tmpv_b])
        vt, vb = vn[j % 2]
        S.op("dve", lambda e: e.tensor_tensor(out=vt[0:P, 0:DA], in0=tmpv[0:P, :], in1=gvBe[0:P, :], op=ALU.mult),
             reads=[tmpv_b, gvBe_b], writes=[vb])
        S.op("pool", lambda e: e.tensor_tensor(out=vt[0:P, DA:2 * DA], in0=tmpv[0:P, :], in1=gvBo[0:P, :], op=ALU.mult),
             reads=[tmpv_b, gvBo_b], writes=[vb])
        if tl["kind"] == "s":
            S.op("dve", lambda e: e.tensor_tensor(out=sqt[0:P, :], in0=tmpv[0:P, :], in1=gvBe[0:P, :], op=ALU.mult),
                 reads=[tmpv_b, gvBe_b], writes=[sqb])
            S.op("dve", lambda e: e.tensor_tensor(out=vgt[0:P, :], in0=tmpv[0:P, :], in1=gvBo[0:P, :], op=ALU.mult),
                 reads=[tmpv_b, gvBo_b], writes=[vgb])
            S.op("dve", lambda e: e.tensor_tensor(out=sqt[0:P, :], in0=sqt[0:P, :], in1=vgt[0:P, :], op=ALU.add),
                 reads=[sqb, vgb], writes=[sqb])
            S.dma("sp", nvs, sqt[0:P, :], [sqb], [], sqb.name, is_output=True)

    def A_u_c(tl, c):
        T = tl["T"]
        hTt, hTb = tl["hT"]
        h3 = tview(hTt, T, 8)
        u3 = tview(uT, T, 4)
        lo, _ = WIN_COLS["u"]
        bk, bkb = next_bank()

        def _mm(e, bk=bk, c=c):
            ins = None
            for k in range(8):
                ins = e.matmul(bk[:, 0:T], lhsT=w_in_v[:, k, lo + c * 128: lo + (c + 1) * 128], rhs=h3[:, k, :], start=(k == 0), stop=(k == 7))
            return ins
        S.op("pe", _mm, reads=[hTb, WIN_PB["u"]], writes=[bkb])
        S.op("act", lambda e, bk=bk, c=c: e.activation(out=u3[:, c, :], in_=bk[:, 0:T], func=AF.Gelu_apprx_tanh),
             reads=[bkb], writes=[uT_b])

    def zview(zt, tl):
        nseg, L = tl["nseg"], tl["L"]
        return zt[:, 0:4 * nseg * (L + 2)].rearrange("p (c s l) -> p c s l", c=4, s=nseg)

    def A_hist(tl):
        nseg, L = tl["nseg"], tl["L"]
        zt, zbuf = zb[0]
        z4 = zview(zt, tl)
        tl["z"] = (zt, zbuf, z4)
        if tl["kind"] == "s":
            for j in range(2):
                S.op("act", lambda e, j=j: e.activation(out=z4[:, :, :, j], in_=zsh4[:, j, :, :], func=AF.Copy),
                     reads=[zsh_b, zbuf], writes=[zbuf])
        elif tl["first"]:
            S.op("pool", lambda e: e.memset(z4[:, :, :, 0:2], 0.0), reads=[], writes=[zbuf])
        else:
            S.op("act", lambda e: e.activation(out=z4[:, :, :, 0:2], in_=z4[:, :, :, L:L + 2], func=AF.Copy),
                 reads=[zbuf], writes=[zbuf])

    def A_bch_c(tl, c):
        T, nseg, L = tl["T"], tl["nseg"], tl["L"]
        hTt, hTb = tl["hT"]
        h3 = tview(hTt, T, 8)
        zt, zbuf, z4 = tl["z"]
        b3 = tview(bT, T, 4)
        ct, cb = Csb[ctr["c"] % 2]
        at, ab = accb[ctr["c"] % 2]
        ctr["c"] += 1

        def job(nm):
            lo, _ = WIN_COLS[nm]
            bk, bkb = next_bank()

            def _mm(e, bk=bk):
                ins = None
                for k in range(8):
                    ins = e.matmul(bk[:, 0:T], lhsT=w_in_v[:, k, lo + c * 128: lo + (c + 1) * 128], rhs=h3[:, k, :], start=(k == 0), stop=(k == 7))
                return ins
            S.op("pe", _mm, reads=[hTb, WIN_PB[nm]], writes=[bkb])
            return bk, bkb
        bkC, bkCb = job("C")
        S.op("act", lambda e: e.activation(out=ct[:, 0:T], in_=bkC[:, 0:T], func=AF.Copy),
             reads=[bkCb], writes=[cb])
        bkH, bkHb = job("h")
        S.op("dve", lambda e: e.tensor_tensor(
            out=z4[:, c, :, 2:2 + L], in0=bkH[:, 0:T].rearrange("p (s l) -> p s l", s=nseg),
            in1=ct[:, 0:T].rearrange("p (s l) -> p s l", s=nseg), op=ALU.mult),
            reads=[bkHb, cb], writes=[zbuf])
        a3 = at[:, 0:T].rearrange("p (s l) -> p s l", s=nseg)
        S.op("act", lambda e: e.activation(out=a3, in_=z4[:, c, :, 0:L], func=AF.Copy, scale=wconvT3[:, c, 0:1]),
             reads=[zbuf] + wconvT_bs, writes=[ab])
        for jj in (1, 2):
            S.op("dve", lambda e, jj=jj: e.scalar_tensor_tensor(out=a3, in0=z4[:, c, :, jj:jj + L],
                                                                scalar=wconvT3[:, c, jj:jj + 1], in1=a3,
                                                                op0=ALU.mult, op1=ALU.add),
                 reads=[zbuf, ab] + wconvT_bs, writes=[ab])
        bkB, bkBb = job("B")
        S.op("dve", lambda e: e.tensor_tensor(out=b3[:, c, :], in0=bkB[:, 0:T], in1=at[:, 0:T], op=ALU.mult),
             reads=[bkBb, ab], writes=[bT_b])

    def A_spatial(tl, j):
        P, T = tl["P"], tl["T"]
        u3 = tview(uT, T, 4)
        a3 = tview(aT, T, 4)
        wm = wmT3 if tl["kind"] == "p" else wmTs3
        wm_b = wmT_b if tl["kind"] == "p" else wmTs_b
        bs = bsK3 if tl["kind"] == "p" else bsKs3
        bs_b = bsK_b if tl["kind"] == "p" else bsKs_b
        vt, vb = vn[j % 2]
        bk, bkb = next_bank()

        def _mm(e, bk=bk, vt=vt):
            ins = None
            for pr in range(4):
                o = bk[:, pr * P:(pr + 1) * P]
                e.matmul(o, lhsT=indK[0:34, :], rhs=bs[0:34, pr, 0:P], start=True, stop=False)
                e.matmul(o, lhsT=vt[0:P, pr * 128:(pr + 1) * 128], rhs=wm[0:P, 2 * pr, 0:P], start=False, stop=False)
                ins = e.matmul(o, lhsT=vt[0:P, DA + pr * 128: DA + (pr + 1) * 128], rhs=wm[0:P, 2 * pr + 1, 0:P], start=False, stop=True)
            return ins
        S.op("pe", _mm, reads=[vb, wm_b, bs_b, indK_b], writes=[bkb])
        S.op("dve", lambda e, bk=bk, j=j: e.tensor_tensor(out=a3[:, :, j * P:(j + 1) * P],
                                                          in0=bk[:, 0:4 * P].rearrange("p (r t) -> p r t", r=4),
                                                          in1=u3[:, :, j * P:(j + 1) * P], op=ALU.mult),
             reads=[bkb, uT_b], writes=[aT_b])

    def A_wout(tl, j, n):
        P, T = tl["P"], tl["T"]
        a3 = tview(aT, T, 4)
        b3 = tview(bT, T, 4)
        xt, xb = tl["xslots"][j]
        bk, bkb = next_bank()

        def _mm(e, bk=bk, n=n):
            ins = None
            for k in range(8):
                lh = a3[:, k, j * P:(j + 1) * P] if k < 4 else b3[:, k - 4, j * P:(j + 1) * P]
                ins = e.matmul(bk[0:P, :], lhsT=lh, rhs=w_out_v[:, k, n * 512:(n + 1) * 512], start=(k == 0), stop=(k == 7))
            return ins
        S.op("pe", _mm, reads=[aT_b, bT_b, wout_pb[n]], writes=[bkb])
        tt, tb = tmpx[ctr["tx"] % 2]
        ctr["tx"] += 1
        S.op("dve", lambda e, bk=bk, tt=tt, n=n: e.tensor_tensor(out=tt[0:P, :], in0=bk[0:P, :], in1=gt1B[0:P, n * 512:(n + 1) * 512], op=ALU.mult),
             reads=[bkb, gt1B_b], writes=[tb])
        S.op("pool", lambda e, tt=tt, n=n: e.tensor_tensor(out=xt[0:P, n * 512:(n + 1) * 512], in0=tt[0:P, :],
                                                           in1=xt[0:P, n * 512:(n + 1) * 512], op=ALU.add),
             reads=[tb, xb], writes=[xb])
        if n == 1:
            r0 = tl["srow0"] + j * P
            S.dma("sp", x1_scr[r0:r0 + P, :], xt[0:P, :], [xb], [tl["x1b"][j]], xb.name)

    def A_convout(tl):
        T, nseg, L = tl["T"], tl["nseg"], tl["L"]
        zt, zbuf, z4 = tl["z"]
        bk, bkb = next_bank()
        if tl["kind"] == "p":
            def _tr(e, bk=bk):
                ins = None
                for c in range(4):
                    ins = e.transpose(bk[0:2, c * 128:(c + 1) * 128], z4[:, c, 0, L:L + 2], identf)
                return ins
            S.op("pe", _tr, reads=[zbuf, identf_b], writes=[bkb])
            S.op("act", lambda e, bk=bk: e.activation(out=stg[0:2, :], in_=bk[0:2, :], func=AF.Copy), reads=[bkb], writes=[stg_b])
            S.dma("sp", ncp, stg[0:2, :], [stg_b], [], stg_b.name, is_output=True)
        else:
            def _tr(e, bk=bk):
                ins = None
                for c in range(4):
                    ins = e.transpose(bk[0:16, c * 128:(c + 1) * 128], z4[:, c, :, 2], identf)
                return ins
            S.op("pe", _tr, reads=[zbuf, identf_b], writes=[bkb])
            S.op("act", lambda e, bk=bk: e.activation(out=stg[0:16, :], in_=bk[0:16, :], func=AF.Copy), reads=[bkb], writes=[stg_b])
            S.dma("sp", ncs[:, 512:1024], stg[0:16, :], [stg_b], [], stg_b.name, is_output=True)

    def A_tile(tl, nxt, extra, pre_nxt=None, nxt_loaded=False, post_bch=None):
        NSUB = tl["NS"]
        NN = nxt["NS"] if nxt is not None else 0

        def ex():
            if extra:
                extra.pop(0)()
        if pre_nxt is not None:
            pre_nxt()
        if nxt is not None and not nxt_loaded:
            A_load(nxt)
        for j in range(NSUB):
            A_v_mm(tl, j)
        A_hist(tl)
        A_bch_c(tl, 0)
        for j in range(NSUB):
            A_v_chain1(tl, j)
        for j in range(NN):
            A_s1_stats(nxt, j)
        ex()
        A_bch_c(tl, 1)
        A_v_chain2(tl, 0)
        A_u_c(tl, 0)
        A_u_c(tl, 1)
        A_xreload(tl)
        ex()
        A_bch_c(tl, 2)
        if NSUB > 1:
            A_v_chain2(tl, 1)
        A_u_c(tl, 2)
        A_u_c(tl, 3)
        ex()
        A_bch_c(tl, 3)
        if post_bch is not None:
            post_bch()
        for j in range(NSUB):
            A_spatial(tl, j)
        for j in range(NN):
            A_s1_mod(nxt, j)
        jobs = [(j, n) for j in range(NSUB) for n in range(2)]
        for (j, n) in jobs[:-1]:
            A_wout(tl, j, n)
        for j in range(NN):
            A_s1b(nxt, j)
        A_wout(tl, *jobs[-1])
        if tl["kind"] == "s" or tl["last"]:
            A_convout(tl)

    ntile = SEQ // TILE
    tilesA = []
    for i in range(ntile):
        tilesA.append(dict(kind="p", i=i, P=128, NS=TILE // 128, T=TILE, row0=i * TILE, srow0=i * TILE,
                           nseg=1, L=TILE, first=(i == 0), last=(i == ntile - 1),
                           x1b=[Buf("x1s_%d_%d" % (i, j)) for j in range(TILE // 128)]))
    tS = dict(kind="s", i=ntile, P=16, NS=1, T=16, row0=0, srow0=SEQ, nseg=16, L=1, first=False, last=False,
              x1b=[Buf("x1s_s")])

    ada_seq = {"c": 4, "d": 8}

    def pop_ada(n=1):
        for _ in range(n):
            if ada_seq["c"] < 12:
                ada_compute(ada_seq["c"])
                ada_seq["c"] += 1
                if ada_seq["d"] < 12:
                    ada_dma(ada_seq["d"])
                    ada_seq["d"] += 1

    wff1_pb = []
    wff1_state = {}

    def emit_wff1():
        old1 = R1.reset()
        wff1_t, _ = R1.alloc("wff1", [128, 8 * DFF], BF16)
        wff1_state["v"] = wff1_t.rearrange("p (k n) -> p k n", k=8)
        NP1 = 8
        for i in range(NP1):
            wff1_pb.append(Buf("wff1_%d" % i))
        alias_after(wff1_pb, old1)
        wff1_src = w_ff1.rearrange("(k p) n -> p k n", p=128)
        cs = DFF // NP1
        for i in range(NP1):
            S.dma("pool", wff1_state["v"][:, :, i * cs:(i + 1) * cs], wff1_src[:, :, i * cs:(i + 1) * cs], [], [wff1_pb[i]], wff1_pb[i].name)

    setup_early_a()
    for q in range(4):
        ada_dma(q)
    for nm in ("v", "C", "h", "B", "u"):
        lo, hi = WIN_COLS[nm]
        S.dma("pool", w_in_v[:, :, lo:hi], win_src[:, :, lo:hi], [], [WIN_PB[nm]], WIN_PB[nm].name)
    setup_early_b()
    A_load(tilesA[0])
    S.op("act", lambda e: e.activation(out=cin[0:48, :], in_=cin[0:48, :], func=AF.Silu), reads=[cin_b], writes=[cin_b])
    bk, bkb = next_bank()

    def _tr_c(e, bk=bk):
        ins = None
        for k in range(8):
            ins = e.transpose(bk[:, k * 48:(k + 1) * 48], cin[0:48, k * 128:(k + 1) * 128], identf[0:48, 0:48])
        return ins
    S.op("pe", _tr_c, reads=[cin_b, identf_b], writes=[bkb])
    S.op("dve", lambda e, bk=bk: e.tensor_copy(out=siluCT, in_=bk[:, 0:384]), reads=[bkb], writes=[siluCT_b])
    for j in range(tilesA[0]["NS"]):
        A_s1_stats(tilesA[0], j)
    ada_compute(0)
    ada_dma(4)
    ada_compute(1)
    ada_dma(5)
    ada_compute(2)
    setup_sample_hist()
    ada_compute(3)
    for n in range(2):
        S.dma("pool", w_out_v[:, :, n * 512:(n + 1) * 512], wout_src[:, :, n * 512:(n + 1) * 512], [], [wout_pb[n]], wout_pb[n].name)
    ada_dma(6)
    ada_dma(7)
    for j in range(tilesA[0]["NS"]):
        A_s1_mod(tilesA[0], j)
    for j in range(tilesA[0]["NS"]):
        A_s1b(tilesA[0], j)
    setup_conv()
    setup_late_loads()
    setup_late_prep()
    for i, tl in enumerate(tilesA):
        nxt = tilesA[i + 1] if i + 1 < ntile else None
        extra = []
        if i == 0:
            extra = [lambda: setup_late(), lambda: None, lambda: pop_ada(2)]
        elif i in (1, 2):
            extra = [lambda: pop_ada(1), lambda: pop_ada(1), lambda: pop_ada(1)]
        if nxt is None:
            A_tile(tl, tS, extra,
                   pre_nxt=lambda: load_mod_consts("s", g1B, g1B_b, sh1B, sh1B_b, gt1B, gt1B_b, tmpA, tmpA_b, g_mix, 0, part="sg"))
        else:
            A_tile(tl, nxt, extra)
        if i == 2:
            assert ada_seq["c"] == 12
            emit_wff1()
        if i == 3:
            setup_sample()

    wff1_v = wff1_state["v"]
    NP1 = 8
    old3 = R3.reset()
    g2B, g2B_b = R3.alloc("g2B", [128, D], F32)
    sh2B, sh2B_b = R3.alloc("sh2B", [128, D], F32)
    tmpB, tmpB_b = R3.alloc("tmpB", [128, D], F32)
    NX1 = 5
    x1s = [R3.alloc("x1_%d" % i, [128, D], F32) for i in range(2)]
    hb2s = [R3.alloc("hb2_%d" % i, [128, D], BF16) for i in range(2)]
    h2T = [R3.alloc("h2T0", [128, 8 * TILE], BF16)]
    identb2, identb2_b = R3.alloc("identb2", [128, 128], BF16)
    mhalf2, mhalf2_b = R3.alloc("mhalf2", [128, 8], F32)
    stat2, _ = R3.alloc("stat2", [128, 64], F32)
    stat2_b = [Buf("stat2_%d" % i) for i in range(64)]
    assert R3.off <= 8 * DIN * 2, R3.off
    early_b = [g2B_b, sh2B_b, tmpB_b, identb2_b, mhalf2_b, h2T[0][1]] + [b for _, b in x1s] + [b for _, b in hb2s] + stat2_b
    x1s += [R3.alloc("x1_%d" % i, [128, D], F32) for i in range(2, NX1)]
    gt2B, gt2B_b = R3.alloc("gt2B", [128, D], F32)
    gfB, gfB_b = R3.alloc("gfB", [128, D], F32)
    h2T.append(R3.alloc("h2T1", [128, 8 * TILE], BF16))
    rr = [R3.alloc("r%d" % i, [128, TILE], F32) for i in range(2)]
    f1T, f1T_b = R3.alloc("f1T", [128, 32 * TILE], BF16)
    f1Ts, f1Ts_b = R3.alloc("f1Ts", [128, 32 * 16], BF16)
    h2Ts, h2Ts_b = R3.alloc("h2Ts", [128, 8 * 16], BF16)
    tmpy = [R3.alloc("tmpy%d" % i, [128, 512], F32) for i in range(2)]
    late_b = [gt2B_b, gfB_b, f1T_b, f1Ts_b, h2Ts_b, h2T[1][1]] + [b for _, b in x1s[2:]] + [b for _, b in rr] + [b for _, b in tmpy]

    def passB_early():
        alias_after(early_b, [w_in_b] + win_pb)
        S.op("pool", lambda e: e.memset(identb2, 0.0), writes=[identb2_b])
        S.op("pool", lambda e: e.affine_select(out=identb2, in_=identb2, compare_op=ALU.not_equal, fill=1.0,
                                               base=0, pattern=[[-1, 128]], channel_multiplier=1),
             reads=[identb2_b], writes=[identb2_b])
        S.op("pool", lambda e: e.memset(mhalf2, -0.5), writes=[mhalf2_b])
        load_mod_consts("p", *PB, part="sg_dma")
        B_load(tilesA[0])

    st2c = [0]

    def new_stat2():
        c = st2c[0]
        st2c[0] = (c + 1) % 64
        return stat2[:, c:c + 1], stat2_b[c]

    def norm_stats2(src, src_b, P, junk, junk_b):
        ss, ss_b = new_stat2()
        ms, ms_b = new_stat2()
        rs, rs_b = new_stat2()
        S.op("act", lambda e: e.activation(out=junk[0:P, :], in_=src[0:P, :], func=AF.Square, accum_out=ss[0:P, :]),
             reads=[src_b], writes=[junk_b, ss_b])
        S.op("act", lambda e: e.activation(out=ms[0:P, :], in_=ss[0:P, :], func=AF.Identity, scale=1.0 / D, bias=EPS),
             reads=[ss_b], writes=[ms_b])
        S.op("pool", lambda e: e.tensor_tensor(out=rs[0:P, :], in0=ms[0:P, :], in1=mhalf2[0:P, 0:1], op=ALU.pow),
             reads=[ms_b, mhalf2_b], writes=[rs_b])
        return rs, rs_b

    cB = {"xs": 0, "r": 0, "ty": 0}

    def B_load(tl):
        P, NSUB = tl["P"], tl["NS"]
        tl["x1slots"] = []
        for j in range(NSUB):
            xt, xb = x1s[cB["xs"] % NX1]
            cB["xs"] += 1
            tl["x1slots"].append((xt, xb))
            r0 = tl["srow0"] + j * P
            S.dma("sp", xt[0:P, :], x1_scr[r0:r0 + P, :], [tl["x1b"][j]], [xb], xb.name)

    def B_s1_stats(tl):
        P, NSUB = tl["P"], tl["NS"]
        tl["hb2"] = []
        tl["rs2"] = []
        for j in range(NSUB):
            xt, xb = tl["x1slots"][j]
            hbt, hbb = hb2s[j % 2]
            tl["hb2"].append((hbt, hbb))
            tl["rs2"].append(modulate_stats(xt, xb, P, hbt, hbb, norm_stats2))

    def B_s1_apply(tl):
        P, NSUB = tl["P"], tl["NS"]
        for j in range(NSUB):
            xt, xb = tl["x1slots"][j]
            hbt, hbb = tl["hb2"][j]
            rs, rs_b = tl["rs2"][j]
            modulate_apply(xt, xb, P, rs, rs_b, g2B, g2B_b, sh2B, sh2B_b, hbt, hbb, tmpB, tmpB_b)

    def B_s1a(tl):
        B_s1_stats(tl)
        B_s1_apply(tl)

    def B_s1b(tl):
        P, NSUB, T = tl["P"], tl["NS"], tl["T"]
        hTt, hTb = h2T[tl["i"] % 2] if tl["kind"] == "p" else (h2Ts, h2Ts_b)
        tl["h2T"] = (hTt, hTb)
        tl["f1"] = (f1T, f1T_b) if tl["kind"] == "p" else (f1Ts, f1Ts_b)
        for j in range(NSUB):
            hbt, hbb = tl["hb2"][j]
            transpose_to(hbt, hbb, P, hTt, hTb, j, T, identb2, identb2_b)

    def B_ff1_job(tl, f):
        T = tl["T"]
        hTt, hTb = tl["h2T"]
        h3 = tview(hTt, T, 8)
        f1t, f1b = tl["f1"]
        f3 = tview(f1t, T, 32)
        bk, bkb = next_bank()

        def _mm(e, bk=bk, f=f):
            ins = None
            for k in range(8):
                ins = e.matmul(bk[:, 0:T], lhsT=wff1_v[:, k, f * 128:(f + 1) * 128], rhs=h3[:, k, :], start=(k == 0), stop=(k == 7))
            return ins
        S.op("pe", _mm, reads=[hTb, wff1_pb[f // (32 // NP1)]], writes=[bkb])
        rt, rb = rr[cB["r"] % 2]
        cB["r"] += 1
        S.op("act", lambda e, bk=bk, rt=rt: e.activation(out=rt[:, 0:T], in_=bk[:, 0:T], func=AF.Relu), reads=[bkb], writes=[rb])
        S.op("dve", lambda e, rt=rt, f=f: e.tensor_tensor(out=f3[:, f, :], in0=rt[:, 0:T], in1=rt[:, 0:T], op=ALU.mult),
             reads=[rb], writes=[f1b])

    def B_ff1_group(tl, f0, G=4):
        T = tl["T"]
        hTt, hTb = tl["h2T"]
        h3 = tview(hTt, T, 8)
        f1t, f1b = tl["f1"]
        f3 = tview(f1t, T, 32)
        bk, bkb = next_bank()

        def _mm(e, bk=bk):
            ins = None
            for g in range(G):
                f = f0 + g
                for k in range(8):
                    ins = e.matmul(bk[:, g * T:(g + 1) * T], lhsT=wff1_v[:, k, f * 128:(f + 1) * 128], rhs=h3[:, k, :], start=(k == 0), stop=(k == 7))
            return ins
        S.op("pe", _mm, reads=[hTb] + wff1_pb, writes=[bkb])
        rt, rb = rr[cB["r"] % 2]
        cB["r"] += 1
        S.op("act", lambda e, bk=bk, rt=rt: e.activation(out=rt[:, 0:G * T], in_=bk[:, 0:G * T], func=AF.Relu), reads=[bkb], writes=[rb])
        S.op("dve", lambda e, rt=rt: e.tensor_tensor(out=f3[:, f0:f0 + G, :], in0=rt[:, 0:G * T].rearrange("p (g t) -> p g t", g=G),
                                                     in1=rt[:, 0:G * T].rearrange("p (g t) -> p g t", g=G), op=ALU.mult),
             reads=[rb], writes=[f1b])

    def B_ff1(tl, hook=None, also=None):
        for f in range(32):
            if hook is not None and f == 12:
                hook()
            B_ff1_job(tl, f)
            if also is not None and f % 4 == 3:
                B_ff1_group(also, f - 3)

    def B_ff2(tl):
        P, NSUB, T = tl["P"], tl["NS"], tl["T"]
        f1t, f1b = tl["f1"]
        f3 = tview(f1t, T, 32)
        for j in range(NSUB):
            xt, xb = tl["x1slots"][j]
            for n in range(2):
                bk, bkb = next_bank()

                def _mm(e, bk=bk, j=j, n=n):
                    ins = None
                    for k in range(32):
                        ins = e.matmul(bk[0:P, :], lhsT=f3[:, k, j * P:(j + 1) * P], rhs=wff2_v[:, k, n * 512:(n + 1) * 512], start=(k == 0), stop=(k == 31))
                    return ins
                S.op("pe", _mm, reads=[f1b] + wff2_pb, writes=[bkb])
                tt, tb = tmpy[cB["ty"] % 2]
                cB["ty"] += 1
                S.op("dve", lambda e, bk=bk, tt=tt, n=n: e.tensor_tensor(out=tt[0:P, :], in0=bk[0:P, :], in1=gt2B[0:P, n * 512:(n + 1) * 512], op=ALU.mult),
                     reads=[bkb, gt2B_b], writes=[tb])
                S.op("dve", lambda e, tt=tt, xt=xt, n=n: e.tensor_tensor(out=xt[0:P, n * 512:(n + 1) * 512], in0=tt[0:P, :],
                                                                         in1=xt[0:P, n * 512:(n + 1) * 512], op=ALU.add),
                     reads=[tb, xb], writes=[xb])
            rs, rs_b = norm_stats2(xt, xb, P, tmpB, tmpB_b)
            S.op("dve", lambda e, xt=xt, rs=rs: e.scalar_tensor_tensor(out=xt[0:P, :], in0=xt[0:P, :], scalar=rs[0:P, 0:1], in1=gfB[0:P, :],
                                                                       op0=ALU.mult, op1=ALU.mult),
                 reads=[xb, rs_b, gfB_b], writes=[xb])
            if tl["kind"] == "p":
                dst = y_p[tl["row0"] + j * P: tl["row0"] + (j + 1) * P, :]
            else:
                dst = y_s
            S.dma("sp", dst, xt[0:P, :], [xb], [], xb.name, is_output=True)

    PB = (g2B, g2B_b, sh2B, sh2B_b, gt2B, gt2B_b, tmpB, tmpB_b, g_ffn, 3)
    load_mod_consts("s", g1B, g1B_b, sh1B, sh1B_b, gt1B, gt1B_b, tmpA, tmpA_b, g_mix, 0, part="gt")
    A_tile(tS, None, [], post_bch=passB_early)
    old2 = R2.reset()
    wff2_t, _ = R2.alloc("wff2", [128, 32 * D], BF16)
    wff2_v = wff2_t.rearrange("p (k n) -> p k n", k=32)
    NP2 = 8
    wff2_pb = [Buf("wff2_%d" % i) for i in range(NP2)]
    alias_after(wff2_pb, old2)
    wff2_src = w_ff2.rearrange("(k p) n -> p k n", p=128)
    def emit_wff2(after=()):
        ks = 32 // NP2
        for i in range(NP2):
            S.dma("pool", wff2_v[:, i * ks:(i + 1) * ks, :], wff2_src[:, i * ks:(i + 1) * ks, :], [], [wff2_pb[i]], wff2_pb[i].name, after=after)

    alias_after(late_b, old3 + stat_b + win_pb + wout_pb)
    load_mod_consts("p", *PB, part="sg_op")
    B_s1_stats(tilesA[0])
    emit_wff2(after=[sh2B_b, tmpB_b, g2B_b] + [b for _, b in tilesA[0]["x1slots"]])
    B_s1_apply(tilesA[0])
    B_s1b(tilesA[0])
    load_mod_consts("p", *PB, part="gt")
    S.dma("sp", gfB, g_final.partition_broadcast(128), [], [gfB_b], "gfB")
    B_load(tS)
    for i, tl in enumerate(tilesA):
        nxt = tilesA[i + 1] if i + 1 < ntile else None
        if nxt is not None:
            B_load(nxt)
            B_ff1(tl, hook=lambda nxt=nxt: B_s1a(nxt), also=(tS if i == 1 else None))
            B_s1b(nxt)
        else:
            B_ff1(tl)
        if i == 0:
            load_mod_consts("s", *PB, part="sg")
            B_s1a(tS)
            load_mod_consts("p16", *PB, part="sg")
        B_ff2(tl)
        if i == 0:
            B_s1b(tS)
        if i == 1:
            load_mod_consts("s", *PB, part="gt")
            B_ff2(tS)
            load_mod_consts("p16", *PB, part="gt")

    _DBG[0] = S
    sem_names = set()
    for e in S.ENG:
        sem_names.add("c_" + e)
        for waits, fn, inc in S.ops[e]:
            sem_names.add(inc[0])
            for s, v in waits:
                sem_names.add(s)
    with contextlib.ExitStack() as es:
        sems = {n: es.enter_context(nc.semaphore(n)) for n in sorted(sem_names)}
        block = es.enter_context(nc.Block())

        def replay(name, eng, final=False):
            for waits, fn, inc in S.ops[name]:
                for s, v in waits:
                    eng.wait_ge(sems[s], v)
                ins = fn(eng)
                ins.then_inc(sems[inc[0]], inc[1])
            if final:
                for s in sorted(S.out_sems):
                    eng.wait_ge(sems[s], S.dma_tot[s])

        @block.tensor
        def _(eng):
            replay("pe", eng)

        @block.scalar
        def _(eng):
            replay("act", eng)

        @block.vector
        def _(eng):
            replay("dve", eng)

        @block.gpsimd
        def _(eng):
            replay("pool", eng)

        @block.sync
        def _(eng):
            replay("sp", eng, final=True)
    return nc


_NC = [None]


def kernel(x_prompt, x_sample, c_prompt, c_sample, state_conv, g_mix, w_ada, b_ada, w_in, g_v, w_s, b_s,
           w_conv, w_out, g_ffn, w_ff1, w_ff2, g_final):
    f = lambda a: np.ascontiguousarray(np.asarray(a, dtype=np.float32))
    x_prompt, x_sample, c_prompt, c_sample, state_conv = map(f, (x_prompt, x_sample, c_prompt, c_sample, state_conv))
    shared = {
        "g_mix": f(g_mix).reshape(1, D), "w_ada": f(w_ada).reshape(D, 6 * D), "b_ada": f(b_ada).reshape(1, 6 * D),
        "w_in": f(w_in).reshape(D, DIN), "g_v": f(g_v).reshape(1, DA), "w_s": f(w_s).reshape(8, 128, 128),
        "b_s": f(b_s).reshape(8, 128), "w_conv": f(w_conv).reshape(3, 512), "w_out": f(w_out).reshape(D, D),
        "g_ffn": f(g_ffn).reshape(1, D), "w_ff1": f(w_ff1).reshape(D, DFF), "w_ff2": f(w_ff2).reshape(DFF, D),
        "g_final": f(g_final).reshape(1, D),
    }
    in_maps = []
    for c in range(NCORES):
        m = dict(shared)
        m["x_p"] = np.ascontiguousarray(x_prompt[c])
        m["x_s"] = np.ascontiguousarray(x_sample[c * NS_TOK:(c + 1) * NS_TOK, 0, :])
        m["c_p"] = np.ascontiguousarray(c_prompt[c:c + 1])
        m["c_s"] = np.ascontiguousarray(c_sample[c * NS_TOK:(c + 1) * NS_TOK])
        m["sconv"] = np.ascontiguousarray(state_conv[0, c * NS_TOK:(c + 1) * NS_TOK].reshape(NS_TOK, 1024))
        in_maps.append(m)
    if _NC[0] is None:
        _NC[0] = build_nc()
    res = run_bass_kernel_spmd(_NC[0], in_maps, core_ids=list(range(NCORES)))
    r = res.results
    y_prompt = np.stack([r[c]["y_p"] for c in range(NCORES)], axis=0).astype(np.float32)
    y_sample = np.concatenate([r[c]["y_s"] for c in range(NCORES)], axis=0).reshape(NCORES * NS_TOK, 1, D).astype(np.float32)
    ncp = np.stack([r[c]["ncp"] for c in range(NCORES)], axis=0).reshape(1, NCORES, 2, 512).astype(np.float32)
    ncs = np.concatenate([r[c]["ncs"] for c in range(NCORES)], axis=0).reshape(1, NCORES * NS_TOK, 2, 512).astype(np.float32)
    nvs = np.concatenate([r[c]["nvs"] for c in range(NCORES)], axis=0).reshape(1, NCORES * NS_TOK, 1, 512).astype(np.float32)
    return (y_prompt, y_sample, ncp, ncs, nvs)
```

```python
import contextlib
import numpy as np
import concourse.bass as bass
import concourse.mybir as mybir
from concourse.bass_utils import run_bass_kernel_spmd

F32 = mybir.dt.float32
BF16 = mybir.dt.bfloat16
U8 = mybir.dt.uint8
AF = mybir.ActivationFunctionType
ALU = mybir.AluOpType
AX = mybir.AxisListType

NCORES = 8
D = 1024
SEQ = 2048
NS_TOK = 16
DA = 512
DFF = 4096
DIN = 2560
EPS = 1e-6
TILE = 256
ESZ = {F32: 4, BF16: 2, U8: 1}


_DBG = [None]


class Buf:
    __slots__ = ("name", "w", "r")

    def __init__(self, name):
        self.name = name
        self.w = None
        self.r = []


class Sched:
    ENG = ("pe", "act", "dve", "pool", "sp")

    def __init__(self):
        self.ops = {e: [] for e in self.ENG}
        self.cnt = {e: 0 for e in self.ENG}
        self.waited = {e: {} for e in self.ENG}
        self.dma_tot = {}
        self.out_sems = set()

    def _waits(self, e, reads, writes):
        need = {}

        def add(h, raw):
            if h is None:
                return
            if h[0] == "E":
                _, pe_, seq = h
                if pe_ == e:
                    if e in ("pe", "sp"):
                        return
                s = "c_" + pe_
                need[s] = max(need.get(s, 0), seq)
            else:
                s = h[1]
                need[s] = max(need.get(s, 0), self.dma_tot[s])

        for b in reads:
            add(b.w, True)
        for b in writes:
            add(b.w, False)
            for h in b.r:
                add(h, False)
        out = []
        wd = self.waited[e]
        for s, v in need.items():
            if wd.get(s, 0) < v:
                wd[s] = v
                out.append((s, v))
        return out

    def _record(self, h, reads, writes):
        for b in writes:
            b.w = h
            b.r = []
        for b in reads:
            b.r.append(h)

    def op(self, e, fn, reads=(), writes=()):
        waits = self._waits(e, reads, writes)
        self.cnt[e] += 1
        h = ("E", e, self.cnt[e])
        self.ops[e].append((waits, fn, ("c_" + e, 1)))
        self._record(h, reads, writes)
        return h

    def dma(self, q, out, in_, reads, writes, sem, is_output=False, noncontig=False, after=()):
        waits = self._waits(q, list(reads) + list(after), writes)
        s = "d_" + sem
        self.dma_tot[s] = self.dma_tot.get(s, 0) + 16
        h = ("D", s, self.dma_tot[s])
        if noncontig:
            fn = lambda eng, o=out, i=in_: eng.dma_start(out=o, in_=i, allow_slow_non_contiguous=True)
        else:
            fn = lambda eng, o=out, i=in_: eng.dma_start(out=o, in_=i)
        self.ops[q].append((waits, fn, (s, 16)))
        self._record(h, reads, writes)
        if is_output:
            self.out_sems.add(s)
        return h


def alias_after(new_bufs, old_bufs):
    hs = []
    for b in old_bufs:
        if b.w is not None:
            hs.append(b.w)
        hs.extend(b.r)
    for nb in new_bufs:
        nb.w = None
        nb.r = list(hs)


class Region:
    def __init__(self, arena, base, size, name):
        self.arena, self.base, self.size, self.off, self.name = arena, base, size, 0, name
        self.bufs = []

    def reset(self):
        old = self.bufs
        self.bufs = []
        self.off = 0
        return old

    def alloc(self, name, shape, dt):
        n = 1
        for d in shape[1:]:
            n *= d
        nbytes = n * ESZ[dt]
        self.off = (self.off + 31) // 32 * 32
        assert self.off + nbytes <= self.size, (self.name, name, self.off, nbytes, self.size)
        o = self.base + self.off
        self.off += nbytes
        v = self.arena[:, o:o + nbytes].bitcast(dt)
        b = Buf(name)
        self.bufs.append(b)
        return v, b


def build_nc():
    nc = bass.Bass("TRN2", target_bir_lowering=False)
    S = Sched()

    def din(name, shape):
        return nc.dram_tensor(name, shape, F32, kind="ExternalInput").ap()

    def dout(name, shape):
        return nc.dram_tensor(name, shape, F32, kind="ExternalOutput").ap()

    x_p = din("x_p", [SEQ, D]); x_s = din("x_s", [NS_TOK, D])
    c_p = din("c_p", [1, D]); c_s = din("c_s", [NS_TOK, D])
    sconv = din("sconv", [NS_TOK, 1024])
    g_mix = din("g_mix", [1, D]); w_ada = din("w_ada", [D, 6 * D]); b_ada = din("b_ada", [1, 6 * D])
    w_in = din("w_in", [D, DIN]); g_v = din("g_v", [1, DA]); w_s = din("w_s", [8, 128, 128])
    b_s = din("b_s", [8, 128]); w_conv = din("w_conv", [3, 512]); w_out = din("w_out", [D, D])
    g_ffn = din("g_ffn", [1, D]); w_ff1 = din("w_ff1", [D, DFF]); w_ff2 = din("w_ff2", [DFF, D])
    g_final = din("g_final", [1, D])
    y_p = dout("y_p", [SEQ, D]); y_s = dout("y_s", [NS_TOK, D])
    ncp = dout("ncp", [2, 512]); ncs = dout("ncs", [NS_TOK, 1024]); nvs = dout("nvs", [NS_TOK, 512])
    x1_scr = nc.dram_tensor("x1_scr", [SEQ + NS_TOK, D], F32).ap()
    mod_scr = nc.dram_tensor("mod_scr", [48, 6 * D], F32).ap()
    mod_bufs = [Buf("modscr%d" % i) for i in range(6)]

    TOTAL = 212700
    arena = nc.alloc_sbuf_tensor("arena", [128, TOTAL], U8).ap()
    R1 = Region(arena, 0, 65536, "R1")
    R2 = Region(arena, 65536, 65664, "R2")
    R3 = Region(arena, 131200, TOTAL - 131200, "R3")

    banks = []
    for b in range(8):
        banks.append((nc.alloc_psum_tensor("ps%d" % b, [128, 512], F32).ap(), Buf("bank%d" % b)))
    bank_i = [0]

    def next_bank():
        b = banks[bank_i[0] % 8]
        bank_i[0] += 1
        return b

    w_in_t, w_in_b = R3.alloc("w_in", [128, 8 * DIN], BF16)
    w_in_v = w_in_t.rearrange("p (k n) -> p k n", k=8)
    w_out_t, w_out_b = R3.alloc("w_out", [128, 8 * D], BF16)
    w_out_v = w_out_t.rearrange("p (k n) -> p k n", k=8)
    win_pb = [Buf("win%d" % i) for i in range(5)]
    wout_pb = [Buf("wout%d" % i) for i in range(2)]
    g1B, g1B_b = R3.alloc("g1B", [128, D], F32)
    sh1B, sh1B_b = R3.alloc("sh1B", [128, D], F32)
    gt1B, gt1B_b = R3.alloc("gt1B", [128, D], F32)
    gvBe, gvBe_b = R3.alloc("gvBe", [128, DA], F32)
    gvBo, gvBo_b = R3.alloc("gvBo", [128, DA], F32)
    wmT, wmT_b = R3.alloc("wmT", [128, 8 * 128], BF16)
    wmT3 = wmT.rearrange("p (h t) -> p h t", h=8)
    wmTs, wmTs_b = R3.alloc("wmTs", [128, 8 * 16], BF16)
    wmTs3 = wmTs.rearrange("p (h t) -> p h t", h=8)
    identb, identb_b = R3.alloc("identb", [128, 128], BF16)
    identf, identf_b = R3.alloc("identf", [128, 128], F32)
    wconvT, wconvT_b = R3.alloc("wconvT", [128, 16], F32)
    wconvT_bs = [wconvT_b]
    wconvT3 = wconvT.rearrange("p (c j) -> p c j", c=4)
    bsK, bsK_b = R3.alloc("bsK", [128, 4 * 128], BF16)
    bsK3 = bsK.rearrange("p (r t) -> p r t", r=4)
    bsKs, bsKs_b = R3.alloc("bsKs", [128, 4 * 16], BF16)
    bsKs3 = bsKs.rearrange("p (r t) -> p r t", r=4)
    indK, indK_b = R3.alloc("indK", [128, 128], BF16)
    mhalf, mhalf_b = R3.alloc("mhalf", [128, 8], F32)
    ones1, ones1_b = R3.alloc("ones1", [128, 128], F32)
    w00, w00_b = R3.alloc("w00", [128, 8], F32)
    zsh, zsh_b = R3.alloc("zsh", [128, 2 * 4 * 16], F32)
    stat, _ = R3.alloc("stat", [128, 64], F32)
    stat_b = [Buf("stat%d" % i) for i in range(64)]
    statc = [0]

    def new_stat(n=1):
        c = statc[0]
        if c + n > 64:
            c = 0
        statc[0] = c + n
        return stat[:, c:c + n], stat_b[c]

    NWS = 4
    wada = []
    for i in range(NWS):
        t, b = R1.alloc("wada%d" % i, [128, 8 * 512], BF16)
        wada.append((t.rearrange("p (k n) -> p k n", k=8), b))
    bada, bada_b = R1.alloc("bada", [128, 6 * D], F32)
    cin, cin_b = R1.alloc("cin", [128, D], F32)
    siluCT, siluCT_b = R1.alloc("siluCT", [128, 8 * 48], BF16)
    siluCT3 = siluCT.rearrange("p (k t) -> p k t", k=8)
    modst = [R1.alloc("modst%d" % i, [128, 512], F32) for i in range(1)]

    wada_src = w_ada.rearrange("(k p) n -> p k n", p=128)
    win_src = w_in.rearrange("(k p) n -> p k n", p=128)
    wout_src = w_out.rearrange("(k p) n -> p k n", p=128)
    WIN_COLS = {"u": (0, 512), "v": (512, 1024), "B": (1024, 1536), "C": (1536, 2048), "h": (2048, 2560)}
    WIN_PB = {"u": win_pb[0], "v": win_pb[1], "B": win_pb[2], "C": win_pb[3], "h": win_pb[4]}

    def ada_dma(q):
        wt, wb = wada[q % NWS]
        S.dma("pool", wt, wada_src[:, :, q * 512:(q + 1) * 512], [], [wb], wb.name)

    def ada_compute(q):
        wt, wb = wada[q % NWS]
        bk, bkb = next_bank()

        def _mm(e, bk=bk, wt=wt, q=q):
            ins = None
            for k in range(8):
                ins = e.matmul(bk[0:48, :], lhsT=siluCT3[:, k, :], rhs=wt[:, k, :], start=(k == 0), stop=(k == 7))
            return ins
        S.op("pe", _mm, reads=[siluCT_b, wb], writes=[bkb])
        mt, mb = modst[q % len(modst)]
        S.op("dve", lambda e, bk=bk, mt=mt, q=q: e.tensor_tensor(out=mt[0:48, :], in0=bk[0:48, :], in1=bada[0:48, q * 512:(q + 1) * 512], op=ALU.add),
             reads=[bkb, bada_b], writes=[mb])
        S.dma("sp", mod_scr[:, q * 512:(q + 1) * 512], mt[0:48, :], [mb], [mod_bufs[q // 2]], "modscr%d" % (q // 2))
        if q < 6:
            bk2, bk2b = next_bank()
            S.op("pe", lambda e, bk2=bk2, mt=mt: e.matmul(bk2[:, :], lhsT=ones1[0:1, 0:128], rhs=mt[0:1, :], start=True, stop=True),
                 reads=[ones1_b, mb], writes=[bk2b])
            cs = slice((q % 2) * 512, (q % 2 + 1) * 512)
            if q < 2:
                S.op("act", lambda e, bk2=bk2, cs=cs: e.activation(out=sh1B[:, cs], in_=bk2[:, :], func=AF.Copy), reads=[bk2b], writes=[sh1B_b])
            elif q < 4:
                S.op("dve", lambda e, bk2=bk2, cs=cs: e.scalar_tensor_tensor(out=g1B[:, cs], in0=bk2[:, :], scalar=1.0, in1=g1B[:, cs],
                                                                            op0=ALU.add, op1=ALU.mult),
                     reads=[bk2b, g1B_b], writes=[g1B_b])
            else:
                S.op("act", lambda e, bk2=bk2, cs=cs: e.activation(out=gt1B[:, cs], in_=bk2[:, :], func=AF.Copy), reads=[bk2b], writes=[gt1B_b])

    def load_mod_consts(kind, gT, gB_, shT, shB_, gtT, gtB_, tmp, tmp_b, gsrc, base, part="all"):
        if kind == "p":
            rows = slice(0, 128)
            def src(i):
                return mod_scr[0:1, (base + i) * D:(base + i + 1) * D].partition_broadcast(128)
            gs = gsrc.partition_broadcast(128)
        elif kind == "p16":
            rows = slice(0, 16)
            def src(i):
                return mod_scr[0:1, (base + i) * D:(base + i + 1) * D].partition_broadcast(16)
            gs = gsrc.partition_broadcast(16)
        else:
            rows = slice(0, 16)
            def src(i):
                return mod_scr[32:48, (base + i) * D:(base + i + 1) * D]
            gs = gsrc.partition_broadcast(16)
        if part in ("sg", "all", "sg_dma"):
            S.dma("sp", shT[rows, :], src(0), [mod_bufs[base + 0]], [shB_], shB_.name)
            S.dma("sp", tmp[rows, :], src(1), [mod_bufs[base + 1]], [tmp_b], tmp_b.name)
            S.dma("sp", gT[rows, :], gs, [], [gB_], gB_.name)
        if part in ("sg", "all", "sg_op"):
            S.op("dve", lambda e: e.scalar_tensor_tensor(out=gT[rows, :], in0=tmp[rows, :], scalar=1.0, in1=gT[rows, :],
                                                         op0=ALU.add, op1=ALU.mult),
                 reads=[tmp_b, gB_], writes=[gB_])
        if part in ("gt", "all"):
            S.dma("sp", gtT[rows, :], src(2), [mod_bufs[base + 2]], [gtB_], gtB_.name)

    xs1 = [R2.alloc("xs1_%d" % i, [128, D], F32) for i in range(2)]
    xs2 = [R2.alloc("xs2_%d" % i, [128, D], F32) for i in range(2)]
    hbs = [R2.alloc("hb%d" % i, [128, D], BF16) for i in range(2)]
    tmpA, tmpA_b = R2.alloc("tmpA", [128, D], F32)
    hT = []
    for i in range(2):
        t, b = R2.alloc("hT%d" % i, [128, 8 * TILE], BF16)
        hT.append((t, b))
    vgs = [R2.alloc("vg%d" % i, [128, DA], F32) for i in range(2)]
    sqvs = [R2.alloc("sqv%d" % i, [128, DA], F32) for i in range(2)]
    vg, vg_b = vgs[0]
    sqv, sqv_b = sqvs[0]
    tmpv, tmpv_b = R2.alloc("tmpv", [128, DA], F32)
    vn = [R2.alloc("vn%d" % i, [128, 2 * DA], BF16) for i in range(2)]
    uT, uT_b = R2.alloc("uT", [128, 4 * TILE], BF16)
    Csb = [R2.alloc("Csb%d" % i, [128, TILE], F32) for i in range(2)]
    zb = [R2.alloc("z%d" % i, [128, 4 * (TILE + 2)], F32) for i in range(1)]
    accb = [R2.alloc("acc%d" % i, [128, TILE], F32) for i in range(2)]
    aT, aT_b = R2.alloc("aT", [128, 4 * TILE], BF16)
    bT, bT_b = R2.alloc("bT", [128, 4 * TILE], BF16)
    tmpx = [R2.alloc("tmpx%d" % i, [128, 512], F32) for i in range(2)]
    stg, stg_b = tmpx[0]
    modst.append(tmpx[1])

    ind2 = Csb[0][0][:, 0:128]; ind2_b = Csb[0][1]
    bsrow3 = tmpv.rearrange("p (r t) -> p r t", r=4); bsrow_b = tmpv_b
    tmpf3 = sqv.rearrange("p (r t) -> p r t", r=4); tmpf_b = sqv_b
    bslo3 = hbs[0][0][:, 0:512].rearrange("p (r t) -> p r t", r=4); bslo_b = hbs[0][1]
    wst3 = tmpA.rearrange("p (h s) -> p h s", h=8); wst_b = tmpA_b
    sct = xs2[0][0]; sct_b = xs2[0][1]
    gve4 = gvBe.rearrange("p (a e d) -> p a e d", a=4, e=2)
    gvo4 = gvBo.rearrange("p (a e d) -> p a e d", a=4, e=2)
    zsh4 = zsh.rearrange("p (j c t) -> p j c t", j=2, c=4)
    d2d_b = Buf("d2d")

    def setup_early_a():
        S.op("pool", lambda e: e.memset(identf, 0.0), writes=[identf_b])
        S.op("pool", lambda e: e.affine_select(out=identf, in_=identf, compare_op=ALU.not_equal, fill=1.0,
                                               base=0, pattern=[[-1, 128]], channel_multiplier=1),
             reads=[identf_b], writes=[identf_b])
        S.op("pool", lambda e: e.memset(cin[0:48, :], 0.0), writes=[cin_b])
        S.dma("sp", cin[32:48, :], c_s, [], [cin_b], "cin")
        S.dma("sp", cin[0:1, :], c_p, [], [cin_b], "cin")
        S.dma("sp", bada[0:48, :], b_ada.partition_broadcast(48), [], [bada_b], "bada")
        S.op("pool", lambda e: e.memset(ones1[0:1, :], 1.0), writes=[ones1_b])
        S.dma("sp", g1B, g_mix.partition_broadcast(128), [], [g1B_b], "g1B")

    def setup_early_b():
        S.op("pool", lambda e: e.memset(identb, 0.0), writes=[identb_b])
        S.op("pool", lambda e: e.affine_select(out=identb, in_=identb, compare_op=ALU.not_equal, fill=1.0,
                                               base=0, pattern=[[-1, 128]], channel_multiplier=1),
             reads=[identb_b], writes=[identb_b])
        S.op("pool", lambda e: e.memset(mhalf, -0.5), writes=[mhalf_b])

    def setup_late_loads():
        S.dma("sp", gvBe, g_v.partition_broadcast(128), [], [gvBe_b], "gvBe")
        S.dma("sp", gvBo, g_v.partition_broadcast(128), [], [gvBo_b], "gvBo")
        S.dma("sp", bsrow3[0:2, :, :], b_s.rearrange("(r e) t -> e r t", e=2), [], [bsrow_b], "bsrow")
        S.dma("sp", wst3, w_s.rearrange("h t s -> t h s"), [], [wst_b], "wst")


    def setup_late_prep():
        S.op("pool", lambda e: e.memset(gve4[:, :, 1, :], 0.0), reads=[], writes=[gvBe_b])
        S.op("pool", lambda e: e.memset(gvo4[:, :, 0, :], 0.0), reads=[], writes=[gvBo_b])
        S.op("pool", lambda e: e.affine_select(out=wst3, in_=wst3, compare_op=ALU.is_ge, fill=0.0, base=0,
                                               pattern=[[0, 8], [-1, 128]], channel_multiplier=1),
             reads=[wst_b], writes=[wst_b])
        S.op("pool", lambda e: e.memset(ind2[0:2, :], 1.0), writes=[ind2_b])
        S.op("pool", lambda e: e.affine_select(out=ind2[0:2, :], in_=ind2[0:2, :], compare_op=ALU.is_ge, fill=0.0,
                                               base=0, pattern=[[1, 128]], channel_multiplier=-64),
             reads=[ind2_b], writes=[ind2_b])
        S.op("pool", lambda e: e.affine_select(out=ind2[0:2, :], in_=ind2[0:2, :], compare_op=ALU.is_ge, fill=0.0,
                                               base=63, pattern=[[-1, 128]], channel_multiplier=64),
             reads=[ind2_b], writes=[ind2_b])
        S.op("pool", lambda e: e.memset(indK[0:34, :], 0.0), writes=[indK_b])
        S.op("dve", lambda e: e.tensor_copy(out=indK[0:2, :], in_=ind2[0:2, :]), reads=[ind2_b], writes=[indK_b])
        S.dma("sp", indK[32:34, :], indK[0:2, :], [indK_b], [indK_b], "indK")
        S.op("pool", lambda e: e.memset(bsK[0:34, :], 0.0), writes=[bsK_b])
        S.op("dve", lambda e: e.tensor_copy(out=bsK3[0:2, :, :], in_=bsrow3[0:2, :, :]), reads=[bsrow_b], writes=[bsK_b])
        S.op("dve", lambda e: e.tensor_copy(out=tmpf3[0:2, :, :], in_=bsK3[0:2, :, :]), reads=[bsK_b], writes=[tmpf_b])
        S.op("dve", lambda e: e.tensor_tensor(out=tmpf3[0:2, :, :], in0=bsrow3[0:2, :, :], in1=tmpf3[0:2, :, :], op=ALU.subtract),
             reads=[bsrow_b, tmpf_b], writes=[tmpf_b])
        S.op("dve", lambda e: e.tensor_copy(out=bslo3[0:2, :, :], in_=tmpf3[0:2, :, :]), reads=[tmpf_b], writes=[bslo_b])
        S.dma("sp", bsK3[32:34, :, :], bslo3[0:2, :, :], [bslo_b], [bsK_b], "bsK")

    def setup_late():
        for half in range(2):
            bk, bkb = next_bank()

            def _tr_w(e, bk=bk, half=half):
                ins = None
                for hh in range(4):
                    ins = e.transpose(bk[:, hh * 128:(hh + 1) * 128], wst3[:, half * 4 + hh, :], identf)
                return ins
            S.op("pe", _tr_w, reads=[wst_b, identf_b], writes=[bkb])
            S.op("dve", lambda e, bk=bk, half=half: e.tensor_copy(out=wmT[:, half * 512:(half + 1) * 512], in_=bk[:, 0:512]),
                 reads=[bkb], writes=[wmT_b])

    def setup_sample_hist():
        S.dma("sp", cin[0:16, :], sconv, [], [cin_b], "cin")
        bk, bkb = next_bank()

        def _tr_s(e, bk=bk):
            ins = None
            for jc in range(8):
                ins = e.transpose(bk[:, jc * 16:(jc + 1) * 16], cin[0:16, jc * 128:(jc + 1) * 128], identf[0:16, 0:16])
            return ins
        S.op("pe", _tr_s, reads=[cin_b, identf_b], writes=[bkb])
        S.op("dve", lambda e, bk=bk: e.tensor_copy(out=zsh, in_=bk[:, 0:128]), reads=[bkb], writes=[zsh_b])

    def setup_sample():
        S.dma("sp", w00[0:16, :], w_s[:, 0, 0:1].rearrange("h o -> o h").partition_broadcast(16), [], [w00_b], "w00", noncontig=True)
        S.dma("sp", ncs[:, 0:512], sconv[:, 512:1024], [], [d2d_b], "d2d", is_output=True)
        for h in range(8):
            S.op("dve", lambda e, h=h: e.tensor_scalar(out=wmTs3[0:16, h, :], in0=identf[0:16, 0:16], scalar1=w00[0:16, h:h + 1],
                                                       scalar2=0.0, op0=ALU.mult, op1=ALU.add),
                 reads=[identf_b, w00_b], writes=[wmTs_b])
        S.op("dve", lambda e: e.tensor_copy(out=bsKs3[0:34, :, :], in_=bsK3[0:34, :, 0:1].broadcast_to([34, 4, 16])),
             reads=[bsK_b], writes=[bsKs_b])

    def setup_conv():
        wcv = tmpv
        S.op("pool", lambda e: e.memset(wcv[0:4, :], 0.0), reads=[], writes=[tmpv_b])
        S.dma("sp", wcv[0:3, :], w_conv, [], [tmpv_b], "wcv")
        bk, bkb = next_bank()

        def _tr_cv(e, bk=bk):
            ins = None
            for c in range(4):
                ins = e.transpose(bk[:, c * 4:(c + 1) * 4], wcv[0:4, c * 128:(c + 1) * 128], identf[0:4, 0:4])
            return ins
        S.op("pe", _tr_cv, reads=[tmpv_b, identf_b], writes=[bkb])
        S.op("dve", lambda e, bk=bk: e.tensor_copy(out=wconvT, in_=bk[:, 0:16]), reads=[bkb], writes=[wconvT_b])

    ctr = {"xs": 0, "c": 0, "tx": 0, "hb": 0}

    def norm_stats(src, src_b, P, junk, junk_b):
        ss, ss_b = new_stat(1)
        ms, ms_b = new_stat(1)
        rs, rs_b = new_stat(1)
        S.op("act", lambda e: e.activation(out=junk[0:P, :], in_=src[0:P, :], func=AF.Square, accum_out=ss[0:P, :]),
             reads=[src_b], writes=[junk_b, ss_b])
        S.op("dve", lambda e: e.tensor_scalar(out=ms[0:P, :], in0=ss[0:P, :], scalar1=1.0 / D, scalar2=EPS,
                                              op0=ALU.mult, op1=ALU.add), reads=[ss_b], writes=[ms_b])
        S.op("pool", lambda e: e.tensor_tensor(out=rs[0:P, :], in0=ms[0:P, :], in1=mhalf[0:P, 0:1], op=ALU.pow),
             reads=[ms_b, mhalf_b], writes=[rs_b])
        return rs, rs_b

    def modulate_stats(src, src_b, P, hbt, hbb, nstats):
        return nstats(src, src_b, P, hbt, hbb)

    def modulate_apply(src, src_b, P, rs, rs_b, gT, gB_, shT, shB_, hbt, hbb, tmpT, tmpT_b):
        S.op("dve", lambda e: e.scalar_tensor_tensor(out=tmpT[0:P, :], in0=src[0:P, :], scalar=rs[0:P, 0:1], in1=gT[0:P, :],
                                                     op0=ALU.mult, op1=ALU.mult),
             reads=[src_b, rs_b, gB_], writes=[tmpT_b])
        S.op("dve", lambda e: e.tensor_tensor(out=hbt[0:P, :], in0=tmpT[0:P, :], in1=shT[0:P, :], op=ALU.add),
             reads=[tmpT_b, shB_], writes=[hbb])

    def modulate(src, src_b, P, gT, gB_, shT, shB_, hbt, hbb, tmpT, tmpT_b, nstats):
        rs, rs_b = modulate_stats(src, src_b, P, hbt, hbb, nstats)
        modulate_apply(src, src_b, P, rs, rs_b, gT, gB_, shT, shB_, hbt, hbb, tmpT, tmpT_b)

    def transpose_to(hbt, hbb, P, dstT, dstT_b, j, T, ident, ident_b):
        bk, bkb = next_bank()
        psb = bk.bitcast(BF16)

        def _tr(e):
            ins = None
            for k in range(8):
                ins = e.transpose(psb[:, k * P:(k + 1) * P], hbt[0:P, k * 128:(k + 1) * 128], ident[0:P, 0:P])
            return ins
        S.op("pe", _tr, reads=[hbb, ident_b], writes=[bkb])
        d3 = dstT[:, 0:8 * T].rearrange("p (k t) -> p k t", k=8)[:, :, j * P:(j + 1) * P]
        S.op("act", lambda e: e.activation(out=d3, in_=psb[:, 0:8 * P].rearrange("p (k t) -> p k t", k=8), func=AF.Copy),
             reads=[bkb], writes=[dstT_b])

    def tview(t, T, nk):
        return t[:, 0:nk * T].rearrange("p (k t) -> p k t", k=nk)

    def A_load(tl):
        P, NSUB = tl["P"], tl["NS"]
        tl["xs1"] = []
        for j in range(NSUB):
            xt, xb = xs1[j % 2]
            tl["xs1"].append((xt, xb))
            srcrows = x_p[tl["row0"] + j * P: tl["row0"] + (j + 1) * P, :] if tl["kind"] == "p" else x_s
            S.dma("sp", xt[0:P, :], srcrows, [], [xb], xb.name)

    def A_s1_stats(tl, j):
        P = tl["P"]
        xt, xb = tl["xs1"][j]
        hbt, hbb = hbs[j % 2]
        tl.setdefault("hb", {})[j] = (hbt, hbb)
        tl.setdefault("rs", {})[j] = modulate_stats(xt, xb, P, hbt, hbb, norm_stats)

    def A_s1_mod(tl, j):
        P = tl["P"]
        xt, xb = tl["xs1"][j]
        hbt, hbb = tl["hb"][j]
        rs, rs_b = tl["rs"][j]
        modulate_apply(xt, xb, P, rs, rs_b, g1B, g1B_b, sh1B, sh1B_b, hbt, hbb, tmpA, tmpA_b)

    def A_s1a(tl, j):
        A_s1_stats(tl, j)
        A_s1_mod(tl, j)

    def A_s1b(tl, j):
        P, T = tl["P"], tl["T"]
        hTt, hTb = hT[tl["i"] % 2]
        tl["hT"] = (hTt, hTb)
        hbt, hbb = tl["hb"][j]
        transpose_to(hbt, hbb, P, hTt, hTb, j, T, identb, identb_b)

    def A_xreload(tl):
        P, NSUB = tl["P"], tl["NS"]
        tl["xslots"] = []
        for j in range(NSUB):
            xt, xb = xs2[j % 2]
            tl["xslots"].append((xt, xb))
            srcrows = x_p[tl["row0"] + j * P: tl["row0"] + (j + 1) * P, :] if tl["kind"] == "p" else x_s
            S.dma("sp", xt[0:P, :], srcrows, [], [xb], xb.name)

    def A_v_mm(tl, j):
        P, T = tl["P"], tl["T"]
        hTt, hTb = tl["hT"]
        h3 = tview(hTt, T, 8)
        lo, hi = WIN_COLS["v"]
        bk, bkb = next_bank()
        vgt, vgb = vgs[j % 2]
        sqt, sqb = sqvs[j % 2]

        def _mm(e, bk=bk, j=j):
            ins = None
            for k in range(8):
                ins = e.matmul(bk[0:P, :], lhsT=h3[:, k, j * P:(j + 1) * P], rhs=w_in_v[:, k, lo:hi], start=(k == 0), stop=(k == 7))
            return ins
        S.op("pe", _mm, reads=[hTb, WIN_PB["v"]], writes=[bkb])
        S.op("act", lambda e, bk=bk: e.activation(out=vgt[0:P, :], in_=bk[0:P, :], func=AF.Gelu_apprx_tanh),
             reads=[bkb], writes=[vgb])
        S.op("act", lambda e: e.activation(out=sqt[0:P, :], in_=vgt[0:P, :], func=AF.Square),
             reads=[vgb], writes=[sqb])

    def A_v_chain1(tl, j):
        P = tl["P"]
        sqt, sqb = sqvs[j % 2]
        ss, ss_b = new_stat(8)
        ms, ms_b = new_stat(8)
        rs, rs_b = new_stat(8)
        tl.setdefault("vrs", {})[j] = (rs, rs_b)
        S.op("dve", lambda e: e.tensor_reduce(out=ss[0:P, :], in_=sqt[0:P, :].rearrange("p (h d) -> p h d", h=8),
                                              axis=AX.X, op=ALU.add), reads=[sqb], writes=[ss_b])
        S.op("dve", lambda e: e.tensor_scalar(out=ms[0:P, :], in0=ss[0:P, :], scalar1=1.0 / 64, scalar2=EPS,
                                              op0=ALU.mult, op1=ALU.add), reads=[ss_b], writes=[ms_b])
        S.op("pool", lambda e: e.tensor_tensor(out=rs[0:P, :], in0=ms[0:P, :], in1=mhalf[0:P, :], op=ALU.pow),
             reads=[ms_b, mhalf_b], writes=[rs_b])

    def A_v_chain2(tl, j):
        P = tl["P"]
        vgt, vgb = vgs[j % 2]
        sqt, sqb = sqvs[j % 2]
        rs, rs_b = tl["vrs"][j]
        S.op("dve", lambda e: e.tensor_tensor(out=tmpv[0:P, :].rearrange("p (h d) -> p h d", h=8),
                                              in0=vgt[0:P, :].rearrange("p (h d) -> p h d", h=8),
                                              in1=rs[0:P, :].unsqueeze(2).broadcast_to([P, 8, 64]), op=ALU.mult),
             reads=[vgb, rs_b], writes=[tmpv_b])
        vt, vb = vn[j % 2]
        S.op("dve", lambda e: e.tensor_tensor(out=vt[0:P, 0:DA], in0=tmpv[0:P, :], in1=gvBe[0:P, :], op=ALU.mult),
             reads=[tmpv_b, gvBe_b], writes=[vb])
        S.op("pool", lambda e: e.tensor_tensor(out=vt[0:P, DA:2 * DA], in0=tmpv[0:P, :], in1=gvBo[0:P, :], op=ALU.mult),
             reads=[tmpv_b, gvBo_b], writes=[vb])
        if tl["kind"] == "s":
            S.op("dve", lambda e: e.tensor_tensor(out=sqt[0:P, :], in0=tmpv[0:P, :], in1=gvBe[0:P, :], op=ALU.mult),
                 reads=[tmpv_b, gvBe_b], writes=[sqb])
            S.op("dve", lambda e: e.tensor_tensor(out=vgt[0:P, :], in0=tmpv[0:P, :], in1=gvBo[0:P, :], op=ALU.mult),
                 reads=[tmpv_b, gvBo_b], writes=[vgb])
            S.op("dve", lambda e: e.tensor_tensor(out=sqt[0:P, :], in0=sqt[0:P, :], in1=vgt[0:P, :], op=ALU.add),
                 reads=[sqb, vgb], writes=[sqb])
            S.dma("sp", nvs, sqt[0:P, :], [sqb], [], sqb.name, is_output=True)

    def A_u_c(tl, c):
        T = tl["T"]
        hTt, hTb = tl["hT"]
        h3 = tview(hTt, T, 8)
        u3 = tview(uT, T, 4)
        lo, _ = WIN_COLS["u"]
        bk, bkb = next_bank()

        def _mm(e, bk=bk, c=c):
            ins = None
            for k in range(8):
                ins = e.matmul(bk[:, 0:T], lhsT=w_in_v[:, k, lo + c * 128: lo + (c + 1) * 128], rhs=h3[:, k, :], start=(k == 0), stop=(k == 7))
            return ins
        S.op("pe", _mm, reads=[hTb, WIN_PB["u"]], writes=[bkb])
        S.op("act", lambda e, bk=bk, c=c: e.activation(out=u3[:, c, :], in_=bk[:, 0:T], func=AF.Gelu_apprx_tanh),
             reads=[bkb], writes=[uT_b])

    def zview(zt, tl):
        nseg, L = tl["nseg"], tl["L"]
        return zt[:, 0:4 * nseg * (L + 2)].rearrange("p (c s l) -> p c s l", c=4, s=nseg)

    def A_hist(tl):
        nseg, L = tl["nseg"], tl["L"]
        zt, zbuf = zb[0]
        z4 = zview(zt, tl)
        tl["z"] = (zt, zbuf, z4)
        if tl["kind"] == "s":
            for j in range(2):
                S.op("act", lambda e, j=j: e.activation(out=z4[:, :, :, j], in_=zsh4[:, j, :, :], func=AF.Copy),
                     reads=[zsh_b, zbuf], writes=[zbuf])
        elif tl["first"]:
            S.op("pool", lambda e: e.memset(z4[:, :, :, 0:2], 0.0), reads=[], writes=[zbuf])
        else:
            S.op("act", lambda e: e.activation(out=z4[:, :, :, 0:2], in_=z4[:, :, :, L:L + 2], func=AF.Copy),
                 reads=[zbuf], writes=[zbuf])

    def A_bch_c(tl, c):
        T, nseg, L = tl["T"], tl["nseg"], tl["L"]
        hTt, hTb = tl["hT"]
        h3 = tview(hTt, T, 8)
        zt, zbuf, z4 = tl["z"]
        b3 = tview(bT, T, 4)
        ct, cb = Csb[ctr["c"] % 2]
        at, ab = accb[ctr["c"] % 2]
        ctr["c"] += 1

        def job(nm):
            lo, _ = WIN_COLS[nm]
            bk, bkb = next_bank()

            def _mm(e, bk=bk):
                ins = None
                for k in range(8):
                    ins = e.matmul(bk[:, 0:T], lhsT=w_in_v[:, k, lo + c * 128: lo + (c + 1) * 128], rhs=h3[:, k, :], start=(k == 0), stop=(k == 7))
                return ins
            S.op("pe", _mm, reads=[hTb, WIN_PB[nm]], writes=[bkb])
            return bk, bkb
        bkC, bkCb = job("C")
        S.op("act", lambda e: e.activation(out=ct[:, 0:T], in_=bkC[:, 0:T], func=AF.Copy),
             reads=[bkCb], writes=[cb])
        bkH, bkHb = job("h")
        S.op("dve", lambda e: e.tensor_tensor(
            out=z4[:, c, :, 2:2 + L], in0=bkH[:, 0:T].rearrange("p (s l) -> p s l", s=nseg),
            in1=ct[:, 0:T].rearrange("p (s l) -> p s l", s=nseg), op=ALU.mult),
            reads=[bkHb, cb], writes=[zbuf])
        a3 = at[:, 0:T].rearrange("p (s l) -> p s l", s=nseg)
        S.op("act", lambda e: e.activation(out=a3, in_=z4[:, c, :, 0:L], func=AF.Copy, scale=wconvT3[:, c, 0:1]),
             reads=[zbuf] + wconvT_bs, writes=[ab])
        for jj in (1, 2):
            S.op("dve", lambda e, jj=jj: e.scalar_tensor_tensor(out=a3, in0=z4[:, c, :, jj:jj + L],
                                                                scalar=wconvT3[:, c, jj:jj + 1], in1=a3,
                                                                op0=ALU.mult, op1=ALU.add),
                 reads=[zbuf, ab] + wconvT_bs, writes=[ab])
        bkB, bkBb = job("B")
        S.op("dve", lambda e: e.tensor_tensor(out=b3[:, c, :], in0=bkB[:, 0:T], in1=at[:, 0:T], op=ALU.mult),
             reads=[bkBb, ab], writes=[bT_b])

    def A_spatial(tl, j):
        P, T = tl["P"], tl["T"]
        u3 = tview(uT, T, 4)
        a3 = tview(aT, T, 4)
        wm = wmT3 if tl["kind"] == "p" else wmTs3
        wm_b = wmT_b if tl["kind"] == "p" else wmTs_b
        bs = bsK3 if tl["kind"] == "p" else bsKs3
        bs_b = bsK_b if tl["kind"] == "p" else bsKs_b
        vt, vb = vn[j % 2]
        bk, bkb = next_bank()

        def _mm(e, bk=bk, vt=vt):
            ins = None
            for pr in range(4):
                o = bk[:, pr * P:(pr + 1) * P]
                e.matmul(o, lhsT=indK[0:34, :], rhs=bs[0:34, pr, 0:P], start=True, stop=False)
                e.matmul(o, lhsT=vt[0:P, pr * 128:(pr + 1) * 128], rhs=wm[0:P, 2 * pr, 0:P], start=False, stop=False)
                ins = e.matmul(o, lhsT=vt[0:P, DA + pr * 128: DA + (pr + 1) * 128], rhs=wm[0:P, 2 * pr + 1, 0:P], start=False, stop=True)
            return ins
        S.op("pe", _mm, reads=[vb, wm_b, bs_b, indK_b], writes=[bkb])
        S.op("dve", lambda e, bk=bk, j=j: e.tensor_tensor(out=a3[:, :, j * P:(j + 1) * P],
                                                          in0=bk[:, 0:4 * P].rearrange("p (r t) -> p r t", r=4),
                                                          in1=u3[:, :, j * P:(j + 1) * P], op=ALU.mult),
             reads=[bkb, uT_b], writes=[aT_b])

    def A_wout(tl, j, n):
        P, T = tl["P"], tl["T"]
        a3 = tview(aT, T, 4)
        b3 = tview(bT, T, 4)
        xt, xb = tl["xslots"][j]
        bk, bkb = next_bank()

        def _mm(e, bk=bk, n=n):
            ins = None
            for k in range(8):
                lh = a3[:, k, j * P:(j + 1) * P] if k < 4 else b3[:, k - 4, j * P:(j + 1) * P]
                ins = e.matmul(bk[0:P, :], lhsT=lh, rhs=w_out_v[:, k, n * 512:(n + 1) * 512], start=(k == 0), stop=(k == 7))
            return ins
        S.op("pe", _mm, reads=[aT_b, bT_b, wout_pb[n]], writes=[bkb])
        tt, tb = tmpx[ctr["tx"] % 2]
        ctr["tx"] += 1
        S.op("dve", lambda e, bk=bk, tt=tt, n=n: e.tensor_tensor(out=tt[0:P, :], in0=bk[0:P, :], in1=gt1B[0:P, n * 512:(n + 1) * 512], op=ALU.mult),
             reads=[bkb, gt1B_b], writes=[tb])
        S.op("pool", lambda e, tt=tt, n=n: e.tensor_tensor(out=xt[0:P, n * 512:(n + 1) * 512], in0=tt[0:P, :],
                                                           in1=xt[0:P, n * 512:(n + 1) * 512], op=ALU.add),
             reads=[tb, xb], writes=[xb])
        if n == 1:
            r0 = tl["srow0"] + j * P
            S.dma("sp", x1_scr[r0:r0 + P, :], xt[0:P, :], [xb], [tl["x1b"][j]], xb.name)

    def A_convout(tl):
        T, nseg, L = tl["T"], tl["nseg"], tl["L"]
        zt, zbuf, z4 = tl["z"]
        bk, bkb = next_bank()
        if tl["kind"] == "p":
            def _tr(e, bk=bk):
                ins = None
                for c in range(4):
                    ins = e.transpose(bk[0:2, c * 128:(c + 1) * 128], z4[:, c, 0, L:L + 2], identf)
                return ins
            S.op("pe", _tr, reads=[zbuf, identf_b], writes=[bkb])
            S.op("act", lambda e, bk=bk: e.activation(out=stg[0:2, :], in_=bk[0:2, :], func=AF.Copy), reads=[bkb], writes=[stg_b])
            S.dma("sp", ncp, stg[0:2, :], [stg_b], [], stg_b.name, is_output=True)
        else:
            def _tr(e, bk=bk):
                ins = None
                for c in range(4):
                    ins = e.transpose(bk[0:16, c * 128:(c + 1) * 128], z4[:, c, :, 2], identf)
                return ins
            S.op("pe", _tr, reads=[zbuf, identf_b], writes=[bkb])
            S.op("act", lambda e, bk=bk: e.activation(out=stg[0:16, :], in_=bk[0:16, :], func=AF.Copy), reads=[bkb], writes=[stg_b])
            S.dma("sp", ncs[:, 512:1024], stg[0:16, :], [stg_b], [], stg_b.name, is_output=True)

    def A_warm(n):
        bk, bkb = next_bank()

        def _mm(e, bk=bk):
            ins = None
            for _ in range(n):
                ins = e.matmul(bk[:, :], lhsT=identb[:, :], rhs=w_out_v[:, 0, 0:512], start=True, stop=True)
            return ins
        S.op("pe", _mm, reads=[identb_b, wout_pb[0]], writes=[bkb])

    def A_tile(tl, nxt, extra, pre_nxt=None, nxt_loaded=False, post_bch=None):
        NSUB = tl["NS"]
        NN = nxt["NS"] if nxt is not None else 0

        def ex():
            if extra:
                extra.pop(0)()
        if pre_nxt is not None:
            pre_nxt()
        if nxt is not None and not nxt_loaded:
            A_load(nxt)
        for j in range(NSUB):
            A_v_mm(tl, j)
        A_hist(tl)
        A_bch_c(tl, 0)
        for j in range(NSUB):
            A_v_chain1(tl, j)
        for j in range(NN):
            A_s1_stats(nxt, j)
        ex()
        A_bch_c(tl, 1)
        A_v_chain2(tl, 0)
        A_u_c(tl, 0)
        A_u_c(tl, 1)
        A_xreload(tl)
        ex()
        A_bch_c(tl, 2)
        if NSUB > 1:
            A_v_chain2(tl, 1)
        A_u_c(tl, 2)
        A_u_c(tl, 3)
        ex()
        A_bch_c(tl, 3)
        if post_bch is not None:
            post_bch()
        for j in range(NSUB):
            if j == 1 and tl["kind"] == "p":
                A_warm(5)
            A_spatial(tl, j)
        if tl["kind"] == "p":
            A_warm(4)
        for j in range(NN):
            A_s1_mod(nxt, j)
        jobs = [(j, n) for j in range(NSUB) for n in range(2)]
        for (j, n) in jobs[:-1]:
            A_wout(tl, j, n)
        for j in range(NN):
            A_s1b(nxt, j)
        A_wout(tl, *jobs[-1])
        if tl["kind"] == "s" or tl["last"]:
            A_convout(tl)

    ntile = SEQ // TILE
    tilesA = []
    for i in range(ntile):
        tilesA.append(dict(kind="p", i=i, P=128, NS=TILE // 128, T=TILE, row0=i * TILE, srow0=i * TILE,
                           nseg=1, L=TILE, first=(i == 0), last=(i == ntile - 1),
                           x1b=[Buf("x1s_%d_%d" % (i, j)) for j in range(TILE // 128)]))
    tS = dict(kind="s", i=ntile, P=16, NS=1, T=16, row0=0, srow0=SEQ, nseg=16, L=1, first=False, last=False,
              x1b=[Buf("x1s_s")])

    ada_seq = {"c": 4, "d": 8}

    def pop_ada(n=1):
        for _ in range(n):
            if ada_seq["c"] < 12:
                ada_compute(ada_seq["c"])
                ada_seq["c"] += 1
                if ada_seq["d"] < 12:
                    ada_dma(ada_seq["d"])
                    ada_seq["d"] += 1

    wff1_pb = []
    wff1_state = {}

    def emit_wff1():
        old1 = R1.reset()
        wff1_t, _ = R1.alloc("wff1", [128, 8 * DFF], BF16)
        wff1_state["v"] = wff1_t.rearrange("p (k n) -> p k n", k=8)
        NP1 = 8
        for i in range(NP1):
            wff1_pb.append(Buf("wff1_%d" % i))
        alias_after(wff1_pb, old1)
        wff1_src = w_ff1.rearrange("(k p) n -> p k n", p=128)
        cs = DFF // NP1
        for i in range(NP1):
            S.dma("pool", wff1_state["v"][:, :, i * cs:(i + 1) * cs], wff1_src[:, :, i * cs:(i + 1) * cs], [], [wff1_pb[i]], wff1_pb[i].name)

    setup_early_a()
    for q in range(4):
        ada_dma(q)
    for nm in ("v", "C", "h", "B", "u"):
        lo, hi = WIN_COLS[nm]
        S.dma("pool", w_in_v[:, :, lo:hi], win_src[:, :, lo:hi], [], [WIN_PB[nm]], WIN_PB[nm].name)
    setup_early_b()
    A_load(tilesA[0])
    S.op("act", lambda e: e.activation(out=cin[0:48, :], in_=cin[0:48, :], func=AF.Silu), reads=[cin_b], writes=[cin_b])
    bk, bkb = next_bank()

    def _tr_c(e, bk=bk):
        ins = None
        for k in range(8):
            ins = e.transpose(bk[:, k * 48:(k + 1) * 48], cin[0:48, k * 128:(k + 1) * 128], identf[0:48, 0:48])
        return ins
    S.op("pe", _tr_c, reads=[cin_b, identf_b], writes=[bkb])
    S.op("dve", lambda e, bk=bk: e.tensor_copy(out=siluCT, in_=bk[:, 0:384]), reads=[bkb], writes=[siluCT_b])
    for j in range(tilesA[0]["NS"]):
        A_s1_stats(tilesA[0], j)
    ada_compute(0)
    ada_dma(4)
    ada_compute(1)
    ada_dma(5)
    ada_compute(2)
    setup_sample_hist()
    ada_compute(3)
    for n in range(2):
        S.dma("pool", w_out_v[:, :, n * 512:(n + 1) * 512], wout_src[:, :, n * 512:(n + 1) * 512], [], [wout_pb[n]], wout_pb[n].name)
    ada_dma(6)
    ada_dma(7)
    for j in range(tilesA[0]["NS"]):
        A_s1_mod(tilesA[0], j)
    for j in range(tilesA[0]["NS"]):
        A_s1b(tilesA[0], j)
    setup_conv()
    setup_late_loads()
    setup_late_prep()
    for i, tl in enumerate(tilesA):
        nxt = tilesA[i + 1] if i + 1 < ntile else None
        extra = []
        if i == 0:
            extra = [lambda: setup_late(), lambda: None, lambda: pop_ada(2)]
        elif i in (1, 2):
            extra = [lambda: pop_ada(1), lambda: pop_ada(1), lambda: pop_ada(1)]
        if nxt is None:
            A_tile(tl, tS, extra,
                   pre_nxt=lambda: load_mod_consts("s", g1B, g1B_b, sh1B, sh1B_b, gt1B, gt1B_b, tmpA, tmpA_b, g_mix, 0, part="sg"))
        else:
            A_tile(tl, nxt, extra)
        if i == 2:
            assert ada_seq["c"] == 12
            emit_wff1()
        if i == 3:
            setup_sample()

    wff1_v = wff1_state["v"]
    NP1 = 8
    old3 = R3.reset()
    g2B, g2B_b = R3.alloc("g2B", [128, D], F32)
    sh2B, sh2B_b = R3.alloc("sh2B", [128, D], F32)
    tmpB, tmpB_b = R3.alloc("tmpB", [128, D], F32)
    NX1 = 5
    x1s = [R3.alloc("x1_%d" % i, [128, D], F32) for i in range(2)]
    hb2s = [R3.alloc("hb2_%d" % i, [128, D], BF16) for i in range(2)]
    h2T = [R3.alloc("h2T0", [128, 8 * TILE], BF16)]
    identb2, identb2_b = R3.alloc("identb2", [128, 128], BF16)
    mhalf2, mhalf2_b = R3.alloc("mhalf2", [128, 8], F32)
    stat2, _ = R3.alloc("stat2", [128, 64], F32)
    stat2_b = [Buf("stat2_%d" % i) for i in range(64)]
    assert R3.off <= 8 * DIN * 2, R3.off
    early_b = [g2B_b, sh2B_b, tmpB_b, identb2_b, mhalf2_b, h2T[0][1]] + [b for _, b in x1s] + [b for _, b in hb2s] + stat2_b
    x1s += [R3.alloc("x1_%d" % i, [128, D], F32) for i in range(2, NX1)]
    gt2B, gt2B_b = R3.alloc("gt2B", [128, D], F32)
    gfB, gfB_b = R3.alloc("gfB", [128, D], F32)
    h2T.append(R3.alloc("h2T1", [128, 8 * TILE], BF16))
    rr = [R3.alloc("r%d" % i, [128, TILE], F32) for i in range(2)]
    f1T, f1T_b = R3.alloc("f1T", [128, 32 * TILE], BF16)
    f1Ts, f1Ts_b = R3.alloc("f1Ts", [128, 32 * 16], BF16)
    h2Ts, h2Ts_b = R3.alloc("h2Ts", [128, 8 * 16], BF16)
    tmpy = [R3.alloc("tmpy%d" % i, [128, 512], F32) for i in range(2)]
    late_b = [gt2B_b, gfB_b, f1T_b, f1Ts_b, h2Ts_b, h2T[1][1]] + [b for _, b in x1s[2:]] + [b for _, b in rr] + [b for _, b in tmpy]

    def passB_early():
        alias_after(early_b, [w_in_b] + win_pb)
        S.op("pool", lambda e: e.memset(identb2, 0.0), writes=[identb2_b])
        S.op("pool", lambda e: e.affine_select(out=identb2, in_=identb2, compare_op=ALU.not_equal, fill=1.0,
                                               base=0, pattern=[[-1, 128]], channel_multiplier=1),
             reads=[identb2_b], writes=[identb2_b])
        S.op("pool", lambda e: e.memset(mhalf2, -0.5), writes=[mhalf2_b])
        load_mod_consts("p", *PB, part="sg_dma")
        B_load(tilesA[0])

    st2c = [0]

    def new_stat2():
        c = st2c[0]
        st2c[0] = (c + 1) % 64
        return stat2[:, c:c + 1], stat2_b[c]

    def norm_stats2(src, src_b, P, junk, junk_b):
        ss, ss_b = new_stat2()
        ms, ms_b = new_stat2()
        rs, rs_b = new_stat2()
        S.op("act", lambda e: e.activation(out=junk[0:P, :], in_=src[0:P, :], func=AF.Square, accum_out=ss[0:P, :]),
             reads=[src_b], writes=[junk_b, ss_b])
        S.op("dve", lambda e: e.tensor_scalar(out=ms[0:P, :], in0=ss[0:P, :], scalar1=1.0 / D, scalar2=EPS,
                                              op0=ALU.mult, op1=ALU.add), reads=[ss_b], writes=[ms_b])
        S.op("pool", lambda e: e.tensor_tensor(out=rs[0:P, :], in0=ms[0:P, :], in1=mhalf2[0:P, 0:1], op=ALU.pow),
             reads=[ms_b, mhalf2_b], writes=[rs_b])
        return rs, rs_b

    cB = {"xs": 0, "r": 0, "ty": 0}

    def B_load(tl):
        P, NSUB = tl["P"], tl["NS"]
        tl["x1slots"] = []
        for j in range(NSUB):
            xt, xb = x1s[cB["xs"] % NX1]
            cB["xs"] += 1
            tl["x1slots"].append((xt, xb))
            r0 = tl["srow0"] + j * P
            S.dma("sp", xt[0:P, :], x1_scr[r0:r0 + P, :], [tl["x1b"][j]], [xb], xb.name)

    def B_s1_stats(tl):
        P, NSUB = tl["P"], tl["NS"]
        tl["hb2"] = []
        tl["rs2"] = []
        for j in range(NSUB):
            xt, xb = tl["x1slots"][j]
            hbt, hbb = hb2s[j % 2]
            tl["hb2"].append((hbt, hbb))
            tl["rs2"].append(modulate_stats(xt, xb, P, hbt, hbb, norm_stats2))

    def B_s1_apply(tl):
        P, NSUB = tl["P"], tl["NS"]
        for j in range(NSUB):
            xt, xb = tl["x1slots"][j]
            hbt, hbb = tl["hb2"][j]
            rs, rs_b = tl["rs2"][j]
            modulate_apply(xt, xb, P, rs, rs_b, g2B, g2B_b, sh2B, sh2B_b, hbt, hbb, tmpB, tmpB_b)

    def B_s1a(tl):
        B_s1_stats(tl)
        B_s1_apply(tl)

    def B_s1b(tl):
        P, NSUB, T = tl["P"], tl["NS"], tl["T"]
        hTt, hTb = h2T[tl["i"] % 2] if tl["kind"] == "p" else (h2Ts, h2Ts_b)
        tl["h2T"] = (hTt, hTb)
        tl["f1"] = (f1T, f1T_b) if tl["kind"] == "p" else (f1Ts, f1Ts_b)
        for j in range(NSUB):
            hbt, hbb = tl["hb2"][j]
            transpose_to(hbt, hbb, P, hTt, hTb, j, T, identb2, identb2_b)

    def B_ff1_job(tl, f):
        T = tl["T"]
        hTt, hTb = tl["h2T"]
        h3 = tview(hTt, T, 8)
        f1t, f1b = tl["f1"]
        f3 = tview(f1t, T, 32)
        bk, bkb = next_bank()

        def _mm(e, bk=bk, f=f):
            ins = None
            for k in range(8):
                ins = e.matmul(bk[:, 0:T], lhsT=wff1_v[:, k, f * 128:(f + 1) * 128], rhs=h3[:, k, :], start=(k == 0), stop=(k == 7))
            return ins
        S.op("pe", _mm, reads=[hTb, wff1_pb[f // (32 // NP1)]], writes=[bkb])
        rt, rb = rr[cB["r"] % 2]
        cB["r"] += 1
        S.op("act", lambda e, bk=bk, rt=rt: e.activation(out=rt[:, 0:T], in_=bk[:, 0:T], func=AF.Relu), reads=[bkb], writes=[rb])
        S.op("dve", lambda e, rt=rt, f=f: e.tensor_tensor(out=f3[:, f, :], in0=rt[:, 0:T], in1=rt[:, 0:T], op=ALU.mult),
             reads=[rb], writes=[f1b])

    def B_ff1_group(tl, f0, G=4):
        T = tl["T"]
        hTt, hTb = tl["h2T"]
        h3 = tview(hTt, T, 8)
        f1t, f1b = tl["f1"]
        f3 = tview(f1t, T, 32)
        bk, bkb = next_bank()

        def _mm(e, bk=bk):
            ins = None
            for g in range(G):
                f = f0 + g
                for k in range(8):
                    ins = e.matmul(bk[:, g * T:(g + 1) * T], lhsT=wff1_v[:, k, f * 128:(f + 1) * 128], rhs=h3[:, k, :], start=(k == 0), stop=(k == 7))
            return ins
        S.op("pe", _mm, reads=[hTb] + wff1_pb, writes=[bkb])
        rt, rb = rr[cB["r"] % 2]
        cB["r"] += 1
        S.op("act", lambda e, bk=bk, rt=rt: e.activation(out=rt[:, 0:G * T], in_=bk[:, 0:G * T], func=AF.Relu), reads=[bkb], writes=[rb])
        S.op("dve", lambda e, rt=rt: e.tensor_tensor(out=f3[:, f0:f0 + G, :], in0=rt[:, 0:G * T].rearrange("p (g t) -> p g t", g=G),
                                                     in1=rt[:, 0:G * T].rearrange("p (g t) -> p g t", g=G), op=ALU.mult),
             reads=[rb], writes=[f1b])

    def B_ff1(tl, hook=None, also=None):
        for f in range(32):
            if hook is not None and f == 12:
                hook()
            B_ff1_job(tl, f)
            if also is not None and f % 4 == 3:
                B_ff1_group(also, f - 3)

    def B_ff2(tl):
        P, NSUB, T = tl["P"], tl["NS"], tl["T"]
        f1t, f1b = tl["f1"]
        f3 = tview(f1t, T, 32)
        for j in range(NSUB):
            xt, xb = tl["x1slots"][j]
            for n in range(2):
                bk, bkb = next_bank()

                def _mm(e, bk=bk, j=j, n=n):
                    ins = None
                    for k in range(32):
                        ins = e.matmul(bk[0:P, :], lhsT=f3[:, k, j * P:(j + 1) * P], rhs=wff2_v[:, k, n * 512:(n + 1) * 512], start=(k == 0), stop=(k == 31))
                    return ins
                S.op("pe", _mm, reads=[f1b] + wff2_pb, writes=[bkb])
                tt, tb = tmpy[cB["ty"] % 2]
                cB["ty"] += 1
                S.op("dve", lambda e, bk=bk, tt=tt, n=n: e.tensor_tensor(out=tt[0:P, :], in0=bk[0:P, :], in1=gt2B[0:P, n * 512:(n + 1) * 512], op=ALU.mult),
                     reads=[bkb, gt2B_b], writes=[tb])
                S.op("dve", lambda e, tt=tt, xt=xt, n=n: e.tensor_tensor(out=xt[0:P, n * 512:(n + 1) * 512], in0=tt[0:P, :],
                                                                         in1=xt[0:P, n * 512:(n + 1) * 512], op=ALU.add),
                     reads=[tb, xb], writes=[xb])
            rs, rs_b = norm_stats2(xt, xb, P, tmpB, tmpB_b)
            S.op("dve", lambda e, xt=xt, rs=rs: e.scalar_tensor_tensor(out=xt[0:P, :], in0=xt[0:P, :], scalar=rs[0:P, 0:1], in1=gfB[0:P, :],
                                                                       op0=ALU.mult, op1=ALU.mult),
                 reads=[xb, rs_b, gfB_b], writes=[xb])
            if tl["kind"] == "p":
                dst = y_p[tl["row0"] + j * P: tl["row0"] + (j + 1) * P, :]
            else:
                dst = y_s
            S.dma("sp", dst, xt[0:P, :], [xb], [], xb.name, is_output=True)

    PB = (g2B, g2B_b, sh2B, sh2B_b, gt2B, gt2B_b, tmpB, tmpB_b, g_ffn, 3)
    load_mod_consts("s", g1B, g1B_b, sh1B, sh1B_b, gt1B, gt1B_b, tmpA, tmpA_b, g_mix, 0, part="gt")
    A_tile(tS, None, [], post_bch=passB_early)
    old2 = R2.reset()
    wff2_t, _ = R2.alloc("wff2", [128, 32 * D], BF16)
    wff2_v = wff2_t.rearrange("p (k n) -> p k n", k=32)
    NP2 = 8
    wff2_pb = [Buf("wff2_%d" % i) for i in range(NP2)]
    alias_after(wff2_pb, old2)
    wff2_src = w_ff2.rearrange("(k p) n -> p k n", p=128)
    def emit_wff2(after=()):
        ks = 32 // NP2
        for i in range(NP2):
            S.dma("pool", wff2_v[:, i * ks:(i + 1) * ks, :], wff2_src[:, i * ks:(i + 1) * ks, :], [], [wff2_pb[i]], wff2_pb[i].name, after=after)

    alias_after(late_b, old3 + stat_b + win_pb + wout_pb)
    load_mod_consts("p", *PB, part="sg_op")
    B_s1_stats(tilesA[0])
    emit_wff2(after=[sh2B_b, tmpB_b, g2B_b] + [b for _, b in tilesA[0]["x1slots"]])
    B_s1_apply(tilesA[0])
    B_s1b(tilesA[0])
    load_mod_consts("p", *PB, part="gt")
    S.dma("sp", gfB, g_final.partition_broadcast(128), [], [gfB_b], "gfB")
    B_load(tS)
    for i, tl in enumerate(tilesA):
        nxt = tilesA[i + 1] if i + 1 < ntile else None
        if nxt is not None:
            B_load(nxt)
            B_ff1(tl, hook=lambda nxt=nxt: B_s1a(nxt), also=(tS if i == 1 else None))
            B_s1b(nxt)
        else:
            B_ff1(tl)
        if i == 0:
            load_mod_consts("s", *PB, part="sg")
            B_s1a(tS)
            load_mod_consts("p16", *PB, part="sg")
        B_ff2(tl)
        if i == 0:
            B_s1b(tS)
        if i == 1:
            load_mod_consts("s", *PB, part="gt")
            B_ff2(tS)
            load_mod_consts("p16", *PB, part="gt")

    _DBG[0] = S
    sem_names = set()
    for e in S.ENG:
        sem_names.add("c_" + e)
        for waits, fn, inc in S.ops[e]:
            sem_names.add(inc[0])
            for s, v in waits:
                sem_names.add(s)
    with contextlib.ExitStack() as es:
        sems = {n: es.enter_context(nc.semaphore(n)) for n in sorted(sem_names)}
        block = es.enter_context(nc.Block())

        def replay(name, eng, final=False):
            for waits, fn, inc in S.ops[name]:
                for s, v in waits:
                    eng.wait_ge(sems[s], v)
                ins = fn(eng)
                ins.then_inc(sems[inc[0]], inc[1])
            if final:
                for s in sorted(S.out_sems):
                    eng.wait_ge(sems[s], S.dma_tot[s])

        @block.tensor
        def _(eng):
            replay("pe", eng)

        @block.scalar
        def _(eng):
            replay("act", eng)

        @block.vector
        def _(eng):
            replay("dve", eng)

        @block.gpsimd
        def _(eng):
            replay("pool", eng)

        @block.sync
        def _(eng):
            replay("sp", eng, final=True)
    return nc


_NC = [None]


def kernel(x_prompt, x_sample, c_prompt, c_sample, state_conv, g_mix, w_ada, b_ada, w_in, g_v, w_s, b_s,
           w_conv, w_out, g_ffn, w_ff1, w_ff2, g_final):
    f = lambda a: np.ascontiguousarray(np.asarray(a, dtype=np.float32))
    x_prompt, x_sample, c_prompt, c_sample, state_conv = map(f, (x_prompt, x_sample, c_prompt, c_sample, state_conv))
    shared = {
        "g_mix": f(g_mix).reshape(1, D), "w_ada": f(w_ada).reshape(D, 6 * D), "b_ada": f(b_ada).reshape(1, 6 * D),
        "w_in": f(w_in).reshape(D, DIN), "g_v": f(g_v).reshape(1, DA), "w_s": f(w_s).reshape(8, 128, 128),
        "b_s": f(b_s).reshape(8, 128), "w_conv": f(w_conv).reshape(3, 512), "w_out": f(w_out).reshape(D, D),
        "g_ffn": f(g_ffn).reshape(1, D), "w_ff1": f(w_ff1).reshape(D, DFF), "w_ff2": f(w_ff2).reshape(DFF, D),
        "g_final": f(g_final).reshape(1, D),
    }
    in_maps = []
    for c in range(NCORES):
        m = dict(shared)
        m["x_p"] = np.ascontiguousarray(x_prompt[c])
        m["x_s"] = np.ascontiguousarray(x_sample[c * NS_TOK:(c + 1) * NS_TOK, 0, :])
        m["c_p"] = np.ascontiguousarray(c_prompt[c:c + 1])
        m["c_s"] = np.ascontiguousarray(c_sample[c * NS_TOK:(c + 1) * NS_TOK])
        m["sconv"] = np.ascontiguousarray(state_conv[0, c * NS_TOK:(c + 1) * NS_TOK].reshape(NS_TOK, 1024))
        in_maps.append(m)
    if _NC[0] is None:
        _NC[0] = build_nc()
    res = run_bass_kernel_spmd(_NC[0], in_maps, core_ids=list(range(NCORES)))
    r = res.results
    y_prompt = np.stack([r[c]["y_p"] for c in range(NCORES)], axis=0).astype(np.float32)
    y_sample = np.concatenate([r[c]["y_s"] for c in range(NCORES)], axis=0).reshape(NCORES * NS_TOK, 1, D).astype(np.float32)
    ncp = np.stack([r[c]["ncp"] for c in range(NCORES)], axis=0).reshape(1, NCORES, 2, 512).astype(np.float32)
    ncs = np.concatenate([r[c]["ncs"] for c in range(NCORES)], axis=0).reshape(1, NCORES * NS_TOK, 2, 512).astype(np.float32)
    nvs = np.concatenate([r[c]["nvs"] for c in range(NCORES)], axis=0).reshape(1, NCORES * NS_TOK, 1, 512).astype(np.float32)
    return (y_prompt, y_sample, ncp, ncs, nvs)
```
